# Optimizing a Trainium2 kernel written in Bass

```python
import jax
import jax.numpy as jnp
from jax import lax
import numpy as np

D_MODEL = 2048
BATCH = 4
SEQ = 8192
DEPTH = 1

CHUNK = 64
N_HEADS_M = 4
HEAD_DIM_M = 256
D_M = N_HEADS_M * HEAD_DIM_M
N_HEADS_SB = 8
HEAD_DIM_SB = 128
D_SB = N_HEADS_SB * HEAD_DIM_SB
CONV_W = 4
D_FF = 5632
D_PLE = 256
Q_BLOCK = 128
ALPHA = (2.0 * DEPTH) ** 0.25
BETA = (8.0 * DEPTH) ** -0.25
LN_EPS = 1e-5
NEG_BIG = -1e30
F_BIAS_LO = 3.0
F_BIAS_HI = 6.0
IN_COLS = 4 * D_M + 2 * N_HEADS_M + 3 * D_SB + 2 * D_MODEL

kernel_name = "hybrid_mlstm_stickbreaking_macaron_deepnorm"


def _layer_norm(x, g, b):
    xf = x.astype(jnp.float32)
    mu = jnp.mean(xf, axis=-1, keepdims=True)
    var = jnp.mean(jnp.square(xf - mu), axis=-1, keepdims=True)
    return ((xf - mu) * lax.rsqrt(var + LN_EPS) * g + b).astype(x.dtype)


def _head_norm(h, g):
    mu = jnp.mean(h, axis=-1, keepdims=True)
    var = jnp.mean(jnp.square(h - mu), axis=-1, keepdims=True)
    return (h - mu) * lax.rsqrt(var + LN_EPS) * g


def _swiglu(x, w1, w3, w2):
    return (jax.nn.silu(x @ w1) * (x @ w3)) @ w2


def _causal_conv(x, w):
    s = x.shape[1]
    xp = jnp.pad(x, ((0, 0), (CONV_W - 1, 0), (0, 0)))
    out = xp[:, 0:s] * w[0]
    for j in range(1, CONV_W):
        out = out + xp[:, j:j + s] * w[j]
    return out


def _split_cols(proj):
    sizes = [D_M, D_M, D_M, D_M, N_HEADS_M, N_HEADS_M, D_SB, D_SB, D_SB, D_MODEL, D_MODEL]
    offs = np.cumsum(sizes)[:-1].tolist()
    return jnp.split(proj, offs, axis=-1)


def _heads(t, nh):
    b, s, _ = t.shape
    return t.reshape(b, s, nh, -1).transpose(0, 2, 1, 3).astype(jnp.float32)


def _mlstm_chunkwise(q, k, v, log_i, log_f):
    bsz, nh, s, dh = q.shape
    nc = s // CHUNK

    def to_chunks(t):
        return jnp.moveaxis(t.reshape(bsz, nh, nc, CHUNK, *t.shape[3:]), 2, 0)

    xs = (to_chunks(q), to_chunks(k), to_chunks(v), to_chunks(log_i), to_chunks(log_f))
    causal = jnp.tril(jnp.ones((CHUNK, CHUNK), dtype=bool))

    def step(carry, inp):
        c_prev, n_prev, m_prev = carry
        qc, kc, vc, ic, fc = inp
        b = jnp.cumsum(fc, axis=-1)
        d_log = jnp.where(causal, b[..., :, None] - b[..., None, :] + ic[..., None, :], NEG_BIG)
        inter_log = b + m_prev[..., None]
        m_t = jnp.maximum(jnp.max(d_log, axis=-1), inter_log)
        scores = jnp.einsum('bhtk,bhsk->bhts', qc, kc) * jnp.exp(d_log - m_t[..., None])
        inter_scale = jnp.exp(inter_log - m_t)
        num = (jnp.einsum('bhts,bhsv->bhtv', scores, vc)
               + inter_scale[..., None] * jnp.einsum('bhtk,bhvk->bhtv', qc, c_prev))
        den = jnp.sum(scores, axis=-1) + inter_scale * jnp.einsum('bhtk,bhk->bht', qc, n_prev)
        h = num / jnp.maximum(jnp.abs(den), jnp.exp(-m_t))[..., None]
        g = b[..., -1]
        w_log = g[..., None] - b + ic
        m_new = jnp.maximum(g + m_prev, jnp.max(w_log, axis=-1))
        w = jnp.exp(w_log - m_new[..., None])
        decay = jnp.exp(g + m_prev - m_new)
        c_new = decay[..., None, None] * c_prev + jnp.einsum('bhs,bhsv,bhsk->bhvk', w, vc, kc)
        n_new = decay[..., None] * n_prev + jnp.einsum('bhs,bhsk->bhk', w, kc)
        return (c_new, n_new, m_new), h

    init = (jnp.zeros((bsz, nh, dh, dh), jnp.float32),
            jnp.zeros((bsz, nh, dh), jnp.float32),
            jnp.full((bsz, nh), NEG_BIG, jnp.float32))
    _, h = lax.scan(step, init, xs)
    return jnp.moveaxis(h, 0, 2).reshape(bsz, nh, s, dh)


def _stick_breaking(q, k, v):
    s, dh = q.shape[2], q.shape[3]
    scale = dh ** -0.5
    outs = []
    for t0 in range(0, s, Q_BLOCK):
        end = t0 + Q_BLOCK
        z = jnp.einsum('bhtd,bhsd->bhts', q[:, :, t0:end], k[:, :, :end]) * scale
        tpos = t0 + jnp.arange(Q_BLOCK)
        spos = jnp.arange(end)
        strict = spos[None, :] < tpos[:, None]
        log_one_minus = jnp.where(strict, jax.nn.log_sigmoid(-z), 0.0)
        cum = jnp.cumsum(log_one_minus, axis=-1)
        rem = cum[..., -1:] - cum
        att = jnp.where(strict, jnp.exp(jax.nn.log_sigmoid(z) + rem), 0.0)
        outs.append(jnp.einsum('bhts,bhsd->bhtd', att, v[:, :, :end]))
    return jnp.concatenate(outs, axis=2)


def _hybrid_mixer(x, w_in, b_gates, conv_w, norm_g, w_up_m, w_up_sb, w_out):
    bsz, s, _ = x.shape
    proj = x @ w_in
    mq, mk, mv, mo, mi, mf, sq, sk, sv, ga, gb = _split_cols(proj)
    qk = jax.nn.silu(_causal_conv(jnp.concatenate([mq, mk], axis=-1), conv_w))
    q_m, k_m = jnp.split(qk, 2, axis=-1)
    log_i = (mi + b_gates[:N_HEADS_M]).astype(jnp.float32).transpose(0, 2, 1)
    log_f = jax.nn.log_sigmoid((mf + b_gates[N_HEADS_M:]).astype(jnp.float32)).transpose(0, 2, 1)
    h = _mlstm_chunkwise(_heads(q_m, N_HEADS_M),
                         _heads(k_m, N_HEADS_M) * (HEAD_DIM_M ** -0.5),
                         _heads(mv, N_HEADS_M), log_i, log_f)
    h = _head_norm(h.transpose(0, 2, 1, 3), norm_g.astype(jnp.float32).reshape(N_HEADS_M, HEAD_DIM_M))
    y_m = (h.reshape(bsz, s, D_M) * jax.nn.sigmoid(mo.astype(jnp.float32))).astype(x.dtype)
    y_sb = _stick_breaking(_heads(sq, N_HEADS_SB), _heads(sk, N_HEADS_SB), _heads(sv, N_HEADS_SB))
    y_sb = y_sb.transpose(0, 2, 1, 3).reshape(bsz, s, D_SB).astype(x.dtype)
    merged = jax.nn.sigmoid(ga) * (y_m @ w_up_m) + jax.nn.sigmoid(gb) * (y_sb @ w_up_sb)
    return merged @ w_out


def setup_inputs(seed: int = 0) -> dict:
    key = jax.random.key(seed)
    ks = jax.random.split(key, 24)
    f32 = jnp.float32

    def nrm(k, shape, scale):
        return jax.random.normal(k, shape, f32) * scale

    col_scale = jnp.concatenate([
        jnp.ones((2 * D_M,), f32), jnp.full((D_M,), BETA, f32),
        jnp.ones((D_M + 2 * N_HEADS_M + 2 * D_SB,), f32), jnp.full((D_SB,), BETA, f32),
        jnp.ones((2 * D_MODEL,), f32)])
    f_bias = jnp.linspace(F_BIAS_LO, F_BIAS_HI, N_HEADS_M, dtype=f32)
    b_gates_m = jnp.concatenate([nrm(ks[5], (DEPTH, N_HEADS_M), 0.1),
                                 f_bias[None, :] + nrm(ks[6], (DEPTH, N_HEADS_M), 0.1)], axis=-1)
    return {
        "x": nrm(ks[0], (BATCH, SEQ, D_MODEL), 1.0),
        "p": nrm(ks[1], (DEPTH, BATCH, SEQ, D_PLE), 1.0),
        "ffn1_w1": nrm(ks[2], (DEPTH, D_MODEL, D_FF), D_MODEL ** -0.5),
        "ffn1_w3": nrm(ks[3], (DEPTH, D_MODEL, D_FF), D_MODEL ** -0.5),
        "ffn1_w2": nrm(ks[4], (DEPTH, D_FF, D_MODEL), BETA * D_FF ** -0.5),
        "ln1_g": 1.0 + nrm(ks[7], (DEPTH, D_MODEL), 0.02),
        "ln1_b": nrm(ks[8], (DEPTH, D_MODEL), 0.02),
        "w_in": nrm(ks[9], (DEPTH, D_MODEL, IN_COLS), D_MODEL ** -0.5) * col_scale,
        "b_gates_m": b_gates_m,
        "conv_m": nrm(ks[10], (DEPTH, CONV_W, 2 * D_M), CONV_W ** -0.5),
        "norm_m": 1.0 + nrm(ks[11], (DEPTH, D_M), 0.02),
        "w_up_m": nrm(ks[12], (DEPTH, D_M, D_MODEL), BETA * D_M ** -0.5),
        "w_up_sb": nrm(ks[13], (DEPTH, D_SB, D_MODEL), BETA * D_SB ** -0.5),
        "w_out": nrm(ks[14], (DEPTH, D_MODEL, D_MODEL), BETA * D_MODEL ** -0.5),
        "ln2_g": 1.0 + nrm(ks[15], (DEPTH, D_MODEL), 0.02),
        "ln2_b": nrm(ks[16], (DEPTH, D_MODEL), 0.02),
        "ffn2_w1": nrm(ks[17], (DEPTH, D_MODEL, D_FF), D_MODEL ** -0.5),
        "ffn2_w3": nrm(ks[18], (DEPTH, D_MODEL, D_FF), D_MODEL ** -0.5),
        "ffn2_w2": nrm(ks[19], (DEPTH, D_FF, D_MODEL), BETA * D_FF ** -0.5),
        "ln3_g": 1.0 + nrm(ks[20], (DEPTH, D_MODEL), 0.02),
        "ln3_b": nrm(ks[21], (DEPTH, D_MODEL), 0.02),
        "w_ple_gate": nrm(ks[22], (DEPTH, D_MODEL, D_MODEL), D_MODEL ** -0.5),
        "w_ple_proj": nrm(ks[23], (DEPTH, D_PLE, D_MODEL), D_PLE ** -0.5),
    }


def reference(x, p, ffn1_w1, ffn1_w3, ffn1_w2, ln1_g, ln1_b, w_in, b_gates_m, conv_m, norm_m,
              w_up_m, w_up_sb, w_out, ln2_g, ln2_b, ffn2_w1, ffn2_w3, ffn2_w2, ln3_g, ln3_b,
              w_ple_gate, w_ple_proj):
    for i in range(DEPTH):
        x = _layer_norm(ALPHA * x + 0.5 * _swiglu(x, ffn1_w1[i], ffn1_w3[i], ffn1_w2[i]), ln1_g[i], ln1_b[i])
        mix = _hybrid_mixer(x, w_in[i], b_gates_m[i], conv_m[i], norm_m[i], w_up_m[i], w_up_sb[i], w_out[i])
        x = _layer_norm(ALPHA * x + mix, ln2_g[i], ln2_b[i])
        x = _layer_norm(ALPHA * x + 0.5 * _swiglu(x, ffn2_w1[i], ffn2_w3[i], ffn2_w2[i]), ln3_g[i], ln3_b[i])
        x = x + jax.nn.sigmoid(x @ w_ple_gate[i]) * (p[i] @ w_ple_proj[i])
    return x
```

```python
import numpy as np
from contextlib import ExitStack
import concourse.bass as bass
import concourse.mybir as mybir
from concourse.bass_utils import run_bass_kernel_spmd

F32 = mybir.dt.float32
BF16 = mybir.dt.bfloat16
ALU = mybir.AluOpType
AF = mybir.ActivationFunctionType
AX = mybir.AxisListType

D = 2048
DFF = 5632
NJ = DFF // 128
KC = D // 128
T = 512
NG = T // 128
DPLE = 256
NHM, DHM = 4, 256
NHS, DHS = 8, 128
INCOLS = 11272
C_MQ, C_MK, C_MV, C_MO, C_GT, C_SQ, C_SK, C_SV, C_GA, C_GB = 0, 1024, 2048, 3072, 4096, 4104, 5128, 6152, 7176, 9224
ALPHA = 2.0 ** 0.25
CRES = 0.5 / ALPHA
INVA = 1.0 / ALPHA
EPS_LN = 1e-5 / (ALPHA * ALPHA)
EPS_H = 1e-5
NEG = -1e30
SB_SCALE = DHS ** -0.5
K_SCALE = DHM ** -0.5
SAME_ENGINE_SYNC = True


class Op:
    __slots__ = ("eng", "fn", "args", "kw", "dma", "deps", "signal", "sem", "val")

    def __init__(self, eng, fn, args, kw, dma):
        self.eng, self.fn, self.args, self.kw, self.dma = eng, fn, args, kw, dma
        self.deps = []
        self.signal = False
        self.sem = None
        self.val = 0


class Sched:
    COMPUTE = ("pe", "act", "dve", "pool")

    def __init__(self, nc, stack):
        self.nc = nc
        self.h = {"pe": nc.tensor, "act": nc.scalar, "dve": nc.vector, "pool": nc.gpsimd, "sp": nc.sync}
        self.esem = {e: stack.enter_context(nc.semaphore("s_" + e)) for e in self.COMPUTE}
        self.ecount = {e: 0 for e in self.COMPUTE}
        self.dsems = {}
        for q, n in (("sp", 20), ("pool", 12), ("act", 4)):
            self.dsems[q] = [stack.enter_context(nc.semaphore("d_%s_%d" % (q, i))) for i in range(n)]
        self.dlast = {q: [0] * len(v) for q, v in self.dsems.items()}
        self.dnext = {q: 0 for q in self.dsems}
        self.waited = {e: {} for e in self.h}
        self.ops = []
        self.tok = {}
        self.n_inst = 0

    def add(self, eng, fn, *args, reads=(), writes=(), dma=False, **kw):
        op = Op(eng, fn, args, kw, dma)
        tok = self.tok
        deps = []
        for t in reads:
            st = tok.get(t)
            if st is None:
                st = tok[t] = [None, {}, []]
            if st[0] is not None:
                deps.append(st[0])
        for t in writes:
            st = tok.get(t)
            if st is None:
                st = tok[t] = [None, {}, []]
            if st[0] is not None:
                deps.append(st[0])
            deps.extend(st[1].values())
            deps.extend(st[2])
        for t in reads:
            st = tok[t]
            if dma:
                st[2].append(op)
            else:
                st[1][eng] = op
        for t in writes:
            st = tok[t]
            st[0] = op
            st[1] = {}
            st[2] = []
        seen = set()
        for d in deps:
            if d is op or id(d) in seen:
                continue
            seen.add(id(d))
            if (not d.dma) and (not dma) and d.eng == eng and (eng == "pe" or not SAME_ENGINE_SYNC):
                continue
            d.signal = True
            op.deps.append(d)
        self.ops.append(op)
        return op

    def _wait(self, eng, sem, val):
        w = self.waited[eng]
        if w.get(id(sem), 0) < val:
            self.h[eng].wait_ge(sem, val)
            w[id(sem)] = val
            self.n_inst += 1

    def barrier(self):
        last = {}
        for op in self.ops:
            if not op.dma:
                last[op.eng] = op
        for op in last.values():
            op.signal = True
        for op in self.ops:
            for d in op.deps:
                self._wait(op.eng, d.sem, d.val)
            if op.dma:
                q = op.eng
                k = self.dnext[q]
                self.dnext[q] = (k + 1) % len(self.dsems[q])
                sem = self.dsems[q][k]
                prev = self.dlast[q][k]
                if prev > 0:
                    self._wait(q, sem, prev)
                ins = op.fn(*op.args, **op.kw)
                ins.then_inc(sem, 16)
                self.dlast[q][k] = prev + 16
                op.sem, op.val = sem, prev + 16
            else:
                ins = op.fn(*op.args, **op.kw)
                if op.signal:
                    self.ecount[op.eng] += 1
                    ins.then_inc(self.esem[op.eng], 1)
                    op.sem, op.val = self.esem[op.eng], self.ecount[op.eng]
            self.n_inst += 1
        self.ops = []
        self.tok = {}
        for e in self.h:
            for c in self.COMPUTE:
                if self.ecount[c] > 0:
                    self._wait(e, self.esem[c], self.ecount[c])
            for q, sems in self.dsems.items():
                for k, sem in enumerate(sems):
                    if self.dlast[q][k] > 0:
                        self._wait(e, sem, self.dlast[q][k])


class Cfg:
    def __init__(self, S, R):
        self.S = S
        self.R = R
        self.NTOK = S // R
        self.HM = NHM // R
        self.HS = NHS // R
        self.RF = 4096 // R
        self.CT = 3072 // R
        self.QM0 = 0
        self.KM0 = self.HM * 256
        self.QS0 = 2 * self.HM * 256
        self.KS0 = self.QS0 + self.HS * 128
        self.VM0 = 0
        self.OM0 = self.HM * 256
        self.VS0 = 2 * self.HM * 256
        self.YR = 2048 // R
        assert self.NTOK % T == 0


def build(cfg, debug=False):
    S, R, NTOK, HM, HS = cfg.S, cfg.R, cfg.NTOK, cfg.HM, cfg.HS
    nc = bass.Bass("TRN2", target_bir_lowering=False)
    dt_in = lambda name, shape: nc.dram_tensor(name, list(shape), F32, kind="ExternalInput").ap()
    x_d = dt_in("x", [NTOK, D])
    p_d = dt_in("p", [NTOK, DPLE])
    w = {}
    for name, shape in (("ffn1_w1", [D, DFF]), ("ffn1_w3", [D, DFF]), ("ffn1_w2", [DFF, D]),
                        ("w_in", [D, INCOLS]), ("w_up_m", [1024, D]), ("w_up_sb", [1024, D]),
                        ("w_out", [D, D]), ("ffn2_w1", [D, DFF]), ("ffn2_w3", [D, DFF]),
                        ("ffn2_w2", [DFF, D]), ("w_ple_gate", [D, D]), ("w_ple_proj", [DPLE, D])):
        w[name] = dt_in(name, shape)
    lnp = {n: dt_in(n, [1, D]) for n in ("ln1_g", "ln1_b", "ln2_g", "ln2_b", "ln3_g", "ln3_b")}
    bg_d = dt_in("bgates", [1, 2 * HM])
    conv_d = dt_in("convT", [128, 2 * HM * 2 * 4])
    norm_d = dt_in("normm", [1, HM * 256])
    out_d = nc.dram_tensor("out", [NTOK, D], F32, kind="ExternalOutput").ap()

    skind = "ExternalOutput" if debug else "Internal"
    dscr = lambda name, shape, dt: nc.dram_tensor(name, list(shape), dt, kind=skind).ap()
    X1 = dscr("X1", [NTOK, D], F32)
    GA = dscr("GA", [D, NTOK], BF16)
    GB = dscr("GB", [D, NTOK], BF16)
    EXF = dscr("EXF", [4096, NTOK], BF16)
    EXT = dscr("EXT", [R * NTOK, cfg.CT], BF16)
    EXG = dscr("EXG", [8, NTOK], F32)
    EY = dscr("EY", [2048, NTOK], BF16)
    GSCR = dscr("GSCR", [4, S], F32)
    EXF_o, EXT_o, EXG_o, EY_o = EXF, EXT, EXG, EY

    with ExitStack() as top:
        sch = Sched(nc, top)
        A = sch.add
        ps_t = top.enter_context(nc.psum_tensor("ps", [128, 8, 512], F32))
        PS = [ps_t[:, b, :] for b in range(8)]
        PSB = [ps_t[:, b, :].bitcast(BF16) for b in range(8)]

        def pstok(b):
            return ("ps", b)

        cst = top.enter_context(nc.sbuf_tensor("cst", [128, 4 * 128], F32))
        ident_f = cst[:, 0:128]
        tri_s = cst[:, 128:256]
        maskM = cst[:, 256:384]
        ones_f = cst[:, 384:512]
        cstb = top.enter_context(nc.sbuf_tensor("cstb", [128, 3 * 128], BF16))
        ident_b = cstb[:, 0:128]
        tri_b = cstb[:, 128:256]
        tric_b = cstb[:, 256:384]
        cm64 = top.enter_context(nc.sbuf_tensor("cm64", [64, 64], F32))
        dmask = top.enter_context(nc.sbuf_tensor("dmask", [128, 4, 512], BF16))
        tmpc = top.enter_context(nc.sbuf_tensor("tmpc", [128, 512], F32))
        CT_ = ("cst",)
        g = nc.gpsimd

        def sel(out, in_, pattern, op, fill, base, cm):
            A("pool", g.affine_select, out=out, in_=in_, pattern=pattern, compare_op=op, fill=fill,
              base=base, channel_multiplier=cm, reads=CT_, writes=CT_)

        def cp(out, in_):
            A("pool", g.tensor_copy, out, in_, reads=CT_, writes=CT_)

        A("pool", g.memset, cst[:, :], 1.0, writes=CT_)
        sel(ident_f, ident_f, [[-1, 128]], ALU.is_ge, 0.0, 0, 1)
        sel(ident_f, ident_f, [[1, 128]], ALU.is_ge, 0.0, 0, -1)
        cp(ident_b, ident_f)
        sel(tri_s, tri_s, [[1, 128]], ALU.is_gt, 0.0, 0, -1)
        A("pool", g.memset, maskM, 0.0, reads=CT_, writes=CT_)
        sel(maskM, maskM, [[-1, 128]], ALU.is_gt, NEG, 0, 1)
        A("pool", g.memset, tmpc[:, :], 1.0, reads=CT_, writes=CT_)
        sel(tmpc[:, 0:128], tmpc[:, 0:128], [[-1, 128]], ALU.is_gt, 0.0, 0, 1)
        cp(tri_b, tmpc[:, 0:128])
        sel(tmpc[:, 128:256], tmpc[:, 128:256], [[1, 128]], ALU.is_ge, 0.0, 0, -1)
        cp(tric_b, tmpc[:, 128:256])
        A("pool", g.memset, cm64[:, :], 0.0, reads=CT_, writes=CT_)
        sel(cm64[:, :], cm64[:, :], [[1, 64]], ALU.is_ge, NEG, 0, -1)
        for i in range(4):
            A("pool", g.memset, tmpc[:, :], 1.0, reads=CT_, writes=CT_)
            sel(tmpc[:, :], tmpc[:, :], [[1, 512]], ALU.is_gt, 0.0, -128 * i, -1)
            cp(dmask[:, i, :], tmpc[:, :])
        sch.barrier()

        class Ring:
            def __init__(self, n):
                self.n, self.i = n, 0

            def next(self):
                k = self.i
                self.i = (k + 1) % self.n
                return k

        def wload(wr, slot, wap, r0, nk, c0, pc, k0=0):
            step = 4
            for ka in range(0, nk, step):
                kb = min(nk, ka + step)
                src = wap[r0 + ka * 128: r0 + kb * 128, c0:c0 + pc].rearrange("(k p) n -> p k n", p=128)
                A("pool", nc.gpsimd.dma_start, out=wr[:, slot, k0 + ka:k0 + kb, 0:pc], in_=src,
                  writes=(("w", slot, (k0 + ka) // step),), dma=True)

        def to_fm(s_tm, xT, xb, xbr, pbank):
            for gi in range(NG):
                k = xbr.next()
                A("dve", nc.vector.tensor_copy, xb[:, k, :], s_tm[:, gi, :],
                  reads=(("s", gi),), writes=(("xb", k),))
                b0 = pbank[gi % 2]
                for half in range(2):
                    pb = PSB[b0 + half]
                    for q in range(8):
                        kc = half * 8 + q
                        A("pe", nc.tensor.transpose, pb[:, q * 128:(q + 1) * 128], xb[:, k, kc * 128:(kc + 1) * 128],
                          ident_b, reads=(("xb", k), "cst"), writes=(pstok(b0 + half),))
                    A("act", nc.scalar.copy, xT[:, half * 8:(half + 1) * 8, gi * 128:(gi + 1) * 128],
                      pb[:, :].rearrange("p (k t) -> p k t", t=128),
                      reads=(pstok(b0 + half),), writes=(("xT", gi, half),))

        WT = [tuple(("w", sl_, q_) for q_ in range(4)) for sl_ in range(8)]
        XT_ALL = tuple(("xT", gi, h) for gi in range(NG) for h in range(2))

        def ffn(xT, gT, s_tm, wr, ring, w1, w3, w2, tmp, tmpr):
            for j4 in range(NJ // 4):
                s1 = ring.next()
                wload(wr, s1, w1, 0, KC, j4 * 512, 512)
                s3 = ring.next()
                wload(wr, s3, w3, 0, KC, j4 * 512, 512)
                for jj in range(4):
                    j = j4 * 4 + jj
                    b1, b3 = (0, 1) if j % 2 == 0 else (2, 3)
                    for kc in range(KC):
                        A("pe", nc.tensor.matmul, PS[b1], lhsT=wr[:, s1, kc, jj * 128:(jj + 1) * 128], rhs=xT[:, kc, :],
                          start=(kc == 0), stop=(kc == KC - 1), reads=WT[s1] + XT_ALL, writes=(pstok(b1),))
                    for kc in range(KC):
                        A("pe", nc.tensor.matmul, PS[b3], lhsT=wr[:, s3, kc, jj * 128:(jj + 1) * 128], rhs=xT[:, kc, :],
                          start=(kc == 0), stop=(kc == KC - 1), reads=WT[s3] + XT_ALL, writes=(pstok(b3),))
                    k = tmpr.next()
                    A("act", nc.scalar.activation, out=tmp[:, k, 0, :], in_=PS[b1], func=AF.Sigmoid,
                      reads=(pstok(b1),), writes=(("tmp", k, 0),))
                    A("dve", nc.vector.tensor_tensor, tmp[:, k, 1, :], PS[b1], tmp[:, k, 0, :], ALU.mult,
                      reads=(pstok(b1), ("tmp", k, 0)), writes=(("tmp", k, 1),))
                    A("dve", nc.vector.tensor_tensor, gT[:, j, :], PS[b3], tmp[:, k, 1, :], ALU.mult,
                      reads=(pstok(b3), ("tmp", k, 1)), writes=(("gT", j),))
            for slab in range(4):
                for piece in range(4):
                    sl = ring.next()
                    wload(wr, sl, w2, piece * 11 * 128, 11, slab * 512, 512)
                    for gi in range(NG):
                        for jj in range(11):
                            j = piece * 11 + jj
                            A("pe", nc.tensor.matmul, PS[4 + gi], lhsT=gT[:, j, gi * 128:(gi + 1) * 128],
                              rhs=wr[:, sl, jj, :], start=(j == 0), stop=(j == NJ - 1),
                              reads=WT[sl] + (("gT", j),), writes=(pstok(4 + gi),))
                for gi in range(NG):
                    A("dve", nc.vector.scalar_tensor_tensor, out=s_tm[:, gi, slab * 512:(slab + 1) * 512],
                      in0=PS[4 + gi], scalar=CRES, in1=s_tm[:, gi, slab * 512:(slab + 1) * 512],
                      op0=ALU.mult, op1=ALU.add, reads=(pstok(4 + gi), ("s", gi)), writes=(("s", gi),))

        def layernorm(s_tm, lnt, gname, bname, st, eps):
            A("sp", nc.sync.dma_start, out=lnt[:, 0, :], in_=lnp[gname].partition_broadcast(128),
              writes=(("lnt", 0),), dma=True)
            A("sp", nc.sync.dma_start, out=lnt[:, 1, :], in_=lnp[bname].partition_broadcast(128),
              writes=(("lnt", 1),), dma=True)
            for gi in range(NG):
                stt = ("st", gi)
                for q in range(4):
                    A("dve", nc.vector.bn_stats, st[:, gi, q * 6:(q + 1) * 6], s_tm[:, gi, q * 512:(q + 1) * 512],
                      reads=(("s", gi),), writes=(stt,))
                A("dve", nc.vector.bn_aggr, st[:, gi, 24:26], st[:, gi, 0:24], reads=(stt,), writes=(stt,))
                A("act", nc.scalar.activation, out=st[:, gi, 26:27], in_=st[:, gi, 25:26], func=AF.Sqrt,
                  bias=eps_t[:, 0:1] if eps == EPS_LN else eps_t[:, 1:2], scale=1.0, reads=(stt, ("eps",)), writes=(stt,))
                A("dve", nc.vector.reciprocal, st[:, gi, 27:28], st[:, gi, 26:27], reads=(stt,), writes=(stt,))
                A("dve", nc.vector.tensor_scalar, s_tm[:, gi, :], s_tm[:, gi, :], st[:, gi, 24:25], st[:, gi, 27:28],
                  ALU.subtract, ALU.mult, reads=(stt, ("s", gi)), writes=(("s", gi),))
                A("dve", nc.vector.tensor_tensor, s_tm[:, gi, :], s_tm[:, gi, :], lnt[:, 0, :], ALU.mult,
                  reads=(("s", gi), ("lnt", 0)), writes=(("s", gi),))
                A("dve", nc.vector.tensor_tensor, s_tm[:, gi, :], s_tm[:, gi, :], lnt[:, 1, :], ALU.add,
                  reads=(("s", gi), ("lnt", 1)), writes=(("s", gi),))

        eps_t = top.enter_context(nc.sbuf_tensor("eps_t", [128, 4], F32))
        A("pool", g.memset, eps_t[:, 0:1], EPS_LN, writes=(("eps",),))
        A("pool", g.memset, eps_t[:, 1:2], EPS_H, reads=(("eps",),), writes=(("eps",),))
        A("pool", g.memset, eps_t[:, 2:3], 1.0, reads=(("eps",),), writes=(("eps",),))
        A("pool", g.memset, eps_t[:, 3:4], 0.0, reads=(("eps",),), writes=(("eps",),))
        sch.barrier()

        NT = NTOK // T

        with ExitStack() as ph:
            sb = lambda name, shape, dt: ph.enter_context(nc.sbuf_tensor(name, list(shape), dt))
            xT = sb("xT", [128, KC, T], BF16)
            gT = sb("gT", [128, NJ, T], BF16)
            s_tm = sb("s_tm", [128, NG, D], F32)
            NSLOT = 4
            wr = sb("wr", [128, NSLOT, KC, 512], BF16)
            ring = Ring(NSLOT)
            lnt = sb("lnt", [128, 2, D], F32)
            st = sb("st", [128, NG, 32], F32)
            xb = sb("xb", [128, 2, D], BF16)
            xbr = Ring(2)
            tmp = sb("tmp", [128, 2, 2, T], F32)
            tmpr = Ring(2)
            stF = sb("stF", [128, 2, 4, T], BF16)
            stFr = Ring(2)
            stT = stF
            stTr = stFr
            stG = sb("stG", [8, T], F32)
            wg = sb("wg", [128, KC, 8], BF16)
            A("pool", nc.gpsimd.dma_start, out=wg[:, :, :],
              in_=w["w_in"][:, C_GT:C_GT + 8].rearrange("(k p) n -> p k n", p=128), writes=(("wg",),), dma=True)

            for ti in range(NT):
                t0 = ti * T
                for gi in range(NG):
                    A("sp", nc.sync.dma_start, out=s_tm[:, gi, :], in_=x_d[t0 + gi * 128:t0 + (gi + 1) * 128, :],
                      writes=(("s", gi),), dma=True)
                to_fm(s_tm, xT, xb, xbr, (4, 6))
                ffn(xT, gT, s_tm, wr, ring, w["ffn1_w1"], w["ffn1_w3"], w["ffn1_w2"], tmp, tmpr)
                layernorm(s_tm, lnt, "ln1_g", "ln1_b", st, EPS_LN)
                for gi in range(NG):
                    A("sp", nc.sync.dma_start, out=X1[t0 + gi * 128:t0 + (gi + 1) * 128, :], in_=s_tm[:, gi, :],
                      reads=(("s", gi),), dma=True)
                to_fm(s_tm, xT, xb, xbr, (4, 6))
                win = w["w_in"]
                fm_pieces = []
                for pi in range(2):
                    fm_pieces.append((C_MQ + pi * 512, "copy", ("F", "QM", pi * 512)))
                for pi in range(2):
                    fm_pieces.append((C_MK + pi * 512, "copy", ("F", "KM", pi * 512)))
                for pi in range(2):
                    fm_pieces.append((C_SQ + pi * 512, "copy", ("F", "QS", pi * 512)))
                for pi in range(2):
                    fm_pieces.append((C_SK + pi * 512, "copy", ("F", "KS", pi * 512)))
                for pi in range(4):
                    fm_pieces.append((C_GA + pi * 512, "sig", ("GA", pi * 512)))
                for pi in range(4):
                    fm_pieces.append((C_GB + pi * 512, "sig", ("GB", pi * 512)))
                bi = 0
                for (c0, kind, dest) in fm_pieces:
                    sl = ring.next()
                    wload(wr, sl, win, 0, KC, c0, 512)
                    k = stFr.next()
                    for cc in range(4):
                        b = bi % 4
                        bi += 1
                        for kc in range(KC):
                            A("pe", nc.tensor.matmul, PS[b], lhsT=wr[:, sl, kc, cc * 128:(cc + 1) * 128], rhs=xT[:, kc, :],
                              start=(kc == 0), stop=(kc == KC - 1), reads=WT[sl] + XT_ALL, writes=(pstok(b),))
                        if kind == "sig":
                            A("act", nc.scalar.activation, out=stF[:, k, cc, :], in_=PS[b], func=AF.Sigmoid,
                              reads=(pstok(b),), writes=(("stF", k, cc),))
                        else:
                            A("act", nc.scalar.copy, stF[:, k, cc, :], PS[b], reads=(pstok(b),), writes=(("stF", k, cc),))
                    rd = tuple(("stF", k, cc) for cc in range(4))
                    if dest[0] == "F":
                        region, off = dest[1], dest[2]
                        if region in ("QM", "KM"):
                            per = HM * 256
                            base = cfg.QM0 if region == "QM" else cfg.KM0
                        else:
                            per = HS * 128
                            base = cfg.QS0 if region == "QS" else cfg.KS0
                        rank, loc = off // per, off % per
                        row0 = rank * cfg.RF + base + loc
                        dst = EXF[row0:row0 + 512, t0:t0 + T].rearrange("(c p) t -> p c t", p=128)
                    else:
                        dstT = GA if dest[0] == "GA" else GB
                        dst = dstT[dest[1]:dest[1] + 512, t0:t0 + T].rearrange("(c p) t -> p c t", p=128)
                    A("sp", nc.sync.dma_start, out=dst, in_=stF[:, k, :, :], reads=rd, dma=True)
                for kc in range(KC):
                    A("pe", nc.tensor.matmul, PS[0][0:8, :], lhsT=wg[:, kc, :], rhs=xT[:, kc, :],
                      start=(kc == 0), stop=(kc == KC - 1), reads=(("wg",),) + XT_ALL, writes=(pstok(0),))
                A("act", nc.scalar.copy, stG[:, :], PS[0][0:8, :], reads=(pstok(0),), writes=(("stG",),))
                for gsel in range(2):
                    for rk in range(R):
                        A("sp", nc.sync.dma_start,
                          out=EXG[rk * 2 * HM + gsel * HM: rk * 2 * HM + gsel * HM + HM, t0:t0 + T],
                          in_=stG[gsel * 4 + rk * HM: gsel * 4 + rk * HM + HM, :], reads=(("stG",),), dma=True)
                tm_pieces = []
                for pi in range(2):
                    tm_pieces.append((C_MV + pi * 512, "copy", cfg.VM0, HM * 256, pi * 512))
                for pi in range(2):
                    tm_pieces.append((C_MO + pi * 512, "sig", cfg.OM0, HM * 256, pi * 512))
                for pi in range(2):
                    tm_pieces.append((C_SV + pi * 512, "copy", cfg.VS0, HS * 128, pi * 512))
                for (c0, kind, base, per, off) in tm_pieces:
                    sl = ring.next()
                    wload(wr, sl, win, 0, KC, c0, 512)
                    k = stTr.next()
                    for gi in range(NG):
                        b = 4 + gi
                        for kc in range(KC):
                            A("pe", nc.tensor.matmul, PS[b], lhsT=xT[:, kc, gi * 128:(gi + 1) * 128], rhs=wr[:, sl, kc, :],
                              start=(kc == 0), stop=(kc == KC - 1), reads=WT[sl] + XT_ALL, writes=(pstok(b),))
                        if kind == "sig":
                            A("act", nc.scalar.activation, out=stT[:, k, gi, :], in_=PS[b], func=AF.Sigmoid,
                              reads=(pstok(b),), writes=(("stF", k, gi),))
                        else:
                            A("act", nc.scalar.copy, stT[:, k, gi, :], PS[b], reads=(pstok(b),), writes=(("stF", k, gi),))
                    rank, loc = off // per, off % per
                    dst = EXT[rank * NTOK + t0: rank * NTOK + t0 + T, base + loc: base + loc + 512].rearrange(
                        "(g p) c -> p g c", p=128)
                    A("sp", nc.sync.dma_start, out=dst, in_=stT[:, k, :, :],
                      reads=tuple(("stF", k, gi) for gi in range(NG)), dma=True)
            sch.barrier()

        phase2(nc, sch, cfg, PS, PSB, dict(ident_f=ident_f, ident_b=ident_b, tri_s=tri_s, maskM=maskM, ones_f=ones_f,
                                            tri_b=tri_b, tric_b=tric_b, cm64=cm64, dmask=dmask, eps_t=eps_t),
               EXF_o, EXT_o, EXG_o, EY, GSCR, bg_d, conv_d, norm_d)

        with ExitStack() as ph:
            sb = lambda name, shape, dt: ph.enter_context(nc.sbuf_tensor(name, list(shape), dt))
            xT = sb("xT3", [128, KC, T], BF16)
            gT = sb("gT3", [128, NJ, T], BF16)
            s_tm = sb("s_tm3", [128, NG, D], F32)
            NSLOT = 4
            wr = sb("wr3", [128, NSLOT, KC, 512], BF16)
            ring = Ring(NSLOT)
            lnt = sb("lnt3", [128, 2, D], F32)
            st = sb("st3", [128, NG, 32], F32)
            xb = sb("xb3", [128, 2, D], BF16)
            xbr = Ring(2)
            tmp = sb("tmp3", [128, 2, 2, T], F32)
            tmpr = Ring(2)
            wple = sb("wple", [128, 2, 2, 512], BF16)
            wpler = Ring(2)
            pbf = sb("pbf", [128, NG, DPLE], BF16)
            pT = sb("pT", [128, 2, T], BF16)
            yT = gT[:, 0:16, :]
            mT = gT[:, 16:32, :]
            for ti in range(NT):
                t0 = ti * T
                for rk in range(R):
                    r0 = rk * cfg.YR
                    A("sp", nc.sync.dma_start, out=yT[:, rk * HM * 2:(rk + 1) * HM * 2, :],
                      in_=EY_o[r0:r0 + HM * 256, t0:t0 + T].rearrange("(c p) t -> p c t", p=128),
                      writes=tuple(("gT", rk * HM * 2 + c) for c in range(HM * 2)), dma=True)
                    A("sp", nc.sync.dma_start, out=yT[:, 8 + rk * HS:8 + (rk + 1) * HS, :],
                      in_=EY_o[r0 + HM * 256:r0 + HM * 256 + HS * 128, t0:t0 + T].rearrange("(c p) t -> p c t", p=128),
                      writes=tuple(("gT", 8 + rk * HS + c) for c in range(HS)), dma=True)
                for gi in range(NG):
                    A("sp", nc.sync.dma_start, out=s_tm[:, gi, :], in_=X1[t0 + gi * 128:t0 + (gi + 1) * 128, :],
                      writes=(("s", gi),), dma=True)
                for piece in range(4):
                    sl = ring.next()
                    wload(wr, sl, w["w_up_m"], 0, 8, piece * 512, 512, k0=0)
                    wload(wr, sl, w["w_up_sb"], 0, 8, piece * 512, 512, k0=8)
                    A("sp", nc.sync.dma_start, out=gT[:, 32:36, :],
                      in_=GA[piece * 512:(piece + 1) * 512, t0:t0 + T].rearrange("(c p) t -> p c t", p=128),
                      writes=tuple(("gT", 32 + c) for c in range(4)), dma=True)
                    A("sp", nc.sync.dma_start, out=gT[:, 36:40, :],
                      in_=GB[piece * 512:(piece + 1) * 512, t0:t0 + T].rearrange("(c p) t -> p c t", p=128),
                      writes=tuple(("gT", 36 + c) for c in range(4)), dma=True)
                    for ff in range(4):
                        f = piece * 4 + ff
                        bA, bB = (0, 1) if f % 2 == 0 else (2, 3)
                        for kc in range(8):
                            A("pe", nc.tensor.matmul, PS[bA], lhsT=wr[:, sl, kc, ff * 128:(ff + 1) * 128], rhs=yT[:, kc, :],
                              start=(kc == 0), stop=(kc == 7), reads=WT[sl] + (("gT", kc),), writes=(pstok(bA),))
                        for kc in range(8, 16):
                            A("pe", nc.tensor.matmul, PS[bB], lhsT=wr[:, sl, kc, ff * 128:(ff + 1) * 128], rhs=yT[:, kc, :],
                              start=(kc == 8), stop=(kc == 15), reads=WT[sl] + (("gT", kc),), writes=(pstok(bB),))
                        k = tmpr.next()
                        A("dve", nc.vector.tensor_tensor, tmp[:, k, 0, :], PS[bA], gT[:, 32 + ff, :], ALU.mult,
                          reads=(pstok(bA), ("gT", 32 + ff)), writes=(("tmp", k, 0),))
                        A("dve", nc.vector.tensor_tensor, tmp[:, k, 1, :], PS[bB], gT[:, 36 + ff, :], ALU.mult,
                          reads=(pstok(bB), ("gT", 36 + ff)), writes=(("tmp", k, 1),))
                        A("dve", nc.vector.tensor_tensor, mT[:, f, :], tmp[:, k, 0, :], tmp[:, k, 1, :], ALU.add,
                          reads=(("tmp", k, 0), ("tmp", k, 1)), writes=(("gT", 16 + f),))
                for slab in range(4):
                    sl = ring.next()
                    wload(wr, sl, w["w_out"], 0, KC, slab * 512, 512)
                    for gi in range(NG):
                        b = 4 + gi
                        for kc in range(KC):
                            A("pe", nc.tensor.matmul, PS[b], lhsT=mT[:, kc, gi * 128:(gi + 1) * 128], rhs=wr[:, sl, kc, :],
                              start=(kc == 0), stop=(kc == KC - 1), reads=WT[sl] + (("gT", 16 + kc),), writes=(pstok(b),))
                        A("dve", nc.vector.scalar_tensor_tensor, out=s_tm[:, gi, slab * 512:(slab + 1) * 512],
                          in0=PS[b], scalar=INVA, in1=s_tm[:, gi, slab * 512:(slab + 1) * 512],
                          op0=ALU.mult, op1=ALU.add, reads=(pstok(b), ("s", gi)), writes=(("s", gi),))
                layernorm(s_tm, lnt, "ln2_g", "ln2_b", st, EPS_LN)
                to_fm(s_tm, xT, xb, xbr, (4, 6))
                ffn(xT, gT, s_tm, wr, ring, w["ffn2_w1"], w["ffn2_w3"], w["ffn2_w2"], tmp, tmpr)
                layernorm(s_tm, lnt, "ln3_g", "ln3_b", st, EPS_LN)
                to_fm(s_tm, xT, xb, xbr, (4, 6))
                for gi in range(NG):
                    A("pool", nc.gpsimd.dma_start, out=pbf[:, gi, :], in_=p_d[t0 + gi * 128:t0 + (gi + 1) * 128, :],
                      writes=(("pbf", gi),), dma=True)
                    pb = PSB[6 + gi % 2]
                    for kc in range(2):
                        A("pe", nc.tensor.transpose, pb[:, kc * 128:(kc + 1) * 128], pbf[:, gi, kc * 128:(kc + 1) * 128],
                          ident_b, reads=(("pbf", gi), "cst"), writes=(pstok(6 + gi % 2),))
                    A("act", nc.scalar.copy, pT[:, :, gi * 128:(gi + 1) * 128],
                      pb[:, 0:256].rearrange("p (k t) -> p k t", t=128), reads=(pstok(6 + gi % 2),), writes=(("pT", gi),))
                PT_ALL = tuple(("pT", gi) for gi in range(NG))
                cnt = 0
                for slab in range(4):
                    sl = ring.next()
                    wload(wr, sl, w["w_ple_gate"], 0, KC, slab * 512, 512)
                    kw_ = wpler.next()
                    A("pool", nc.gpsimd.dma_start, out=wple[:, kw_, :, :],
                      in_=w["w_ple_proj"][:, slab * 512:(slab + 1) * 512].rearrange("(k p) n -> p k n", p=128),
                      writes=(("wple", kw_),), dma=True)
                    for gi in range(NG):
                        bG, bP = (0, 1) if cnt % 2 == 0 else (2, 3)
                        cnt += 1
                        for kc in range(KC):
                            A("pe", nc.tensor.matmul, PS[bG], lhsT=xT[:, kc, gi * 128:(gi + 1) * 128], rhs=wr[:, sl, kc, :],
                              start=(kc == 0), stop=(kc == KC - 1), reads=WT[sl] + XT_ALL, writes=(pstok(bG),))
                        for kc in range(2):
                            A("pe", nc.tensor.matmul, PS[bP], lhsT=pT[:, kc, gi * 128:(gi + 1) * 128],
                              rhs=wple[:, kw_, kc, :],
                              start=(kc == 0), stop=(kc == 1), reads=(("wple", kw_),) + PT_ALL, writes=(pstok(bP),))
                        k = tmpr.next()
                        A("act", nc.scalar.activation, out=tmp[:, k, 0, :], in_=PS[bG], func=AF.Sigmoid,
                          reads=(pstok(bG),), writes=(("tmp", k, 0),))
                        A("dve", nc.vector.tensor_tensor, tmp[:, k, 1, :], PS[bP], tmp[:, k, 0, :], ALU.mult,
                          reads=(pstok(bP), ("tmp", k, 0)), writes=(("tmp", k, 1),))
                        A("dve", nc.vector.tensor_tensor, s_tm[:, gi, slab * 512:(slab + 1) * 512],
                          s_tm[:, gi, slab * 512:(slab + 1) * 512], tmp[:, k, 1, :], ALU.add,
                          reads=(("tmp", k, 1), ("s", gi)), writes=(("s", gi),))
                for gi in range(NG):
                    A("sp", nc.sync.dma_start, out=out_d[t0 + gi * 128:t0 + (gi + 1) * 128, :], in_=s_tm[:, gi, :],
                      reads=(("s", gi),), dma=True)
            sch.barrier()
    return nc


def phase2(nc, sch, cfg, PS, PSB, K, EXF_o, EXT_o, EXG_o, EY, GSCR, bg_d, conv_d, norm_d):
    sb_attention(nc, sch, cfg, PS, K, EXF_o, EXT_o, EY)
    sch.barrier()
    mlstm(nc, sch, cfg, PS, PSB, K, EXF_o, EXT_o, EXG_o, EY, GSCR, bg_d, conv_d, norm_d)
    sch.barrier()


def sb_attention(nc, sch, cfg, PS, K, EXF_o, EXT_o, EY):
    A = sch.add
    S, R, NTOK, HM, HS = cfg.S, cfg.R, cfg.NTOK, cfg.HM, cfg.HS
    NB, NQ = S // 128, S // 512
    tri_b, tric_b, dmask = K["tri_b"], K["tric_b"], K["dmask"]
    with ExitStack() as ph:
        sb = lambda name, shape, dt: ph.enter_context(nc.sbuf_tensor(name, list(shape), dt))
        QT = sb("sbQT", [128, 2, S], BF16)
        KT = sb("sbKT", [128, 2, S], BF16)
        V = sb("sbV", [128, 2, NB, 128], BF16)
        e_t = sb("sb_e", [128, 2, 512], F32)
        sp32 = sb("sb_sp32", [128, 2, 512], F32)
        sp16 = sb("sb_sp16", [128, 4, 512], BF16)
        a_t = sb("sb_a", [128, 2, 512], F32)
        b_t = sb("sb_b", [128, 2, 512], F32)
        att = sb("sb_att", [128, 3, 512], BF16)
        yst = sb("sb_y", [128, 2, 512], BF16)
        VB = 16 if NTOK // 128 >= 16 else NTOK // 128
        qk_tok = {}
        v_tok = {}

        def load_head(hl):
            hr = hl % 2
            qt, vt = [], []
            for i in range(R):
                r0 = i * cfg.RF + cfg.QS0 + hl * 128
                A("sp", nc.sync.dma_start, out=QT[:, hr, i * NTOK:(i + 1) * NTOK], in_=EXF_o[r0:r0 + 128, 0:NTOK],
                  writes=(("sQ", hr, i),), dma=True)
                r0 = i * cfg.RF + cfg.KS0 + hl * 128
                A("sp", nc.sync.dma_start, out=KT[:, hr, i * NTOK:(i + 1) * NTOK], in_=EXF_o[r0:r0 + 128, 0:NTOK],
                  writes=(("sK", hr, i),), dma=True)
                qt += [("sQ", hr, i), ("sK", hr, i)]
                for b0 in range(0, NTOK // 128, VB):
                    src = EXT_o[i * NTOK + b0 * 128:i * NTOK + (b0 + VB) * 128,
                                cfg.VS0 + hl * 128:cfg.VS0 + (hl + 1) * 128].rearrange("(b s) d -> s b d", s=128)
                    gb0 = i * (NTOK // 128) + b0
                    A("sp", nc.sync.dma_start, out=V[:, hr, gb0:gb0 + VB, :], in_=src,
                      writes=(("sV", hr, gb0),), dma=True)
                    vt.append(("sV", hr, gb0))
            qk_tok[hl] = tuple(qt)
            v_tok[hl] = tuple(vt)

        ZB = (0, 1, 3)
        steps = []
        for hl in range(HS):
            for Q in range(NQ):
                for kb in range(4 * Q + 3, -1, -1):
                    steps.append((hl, Q, kb))

        NS = len(steps)

        def info(n):
            hl, Q, kb = steps[n]
            return hl, hl % 2, Q, kb, kb - 4 * Q, kb == 4 * Q + 3, kb == 0, hl * NQ + Q

        def op_qk(n):
            hl, hr, Q, kb, i, first, last, sweep = info(n)
            zb = ZB[n % 3]
            A("pe", nc.tensor.matmul, PS[zb], lhsT=KT[:, hr, kb * 128:(kb + 1) * 128], rhs=QT[:, hr, Q * 512:(Q + 1) * 512],
              start=True, stop=True, reads=qk_tok[hl], writes=(("ps", zb),))

        def op_esp(n):
            zb, k2 = ZB[n % 3], n % 2
            A("act", nc.scalar.activation, out=e_t[:, k2, :], in_=PS[zb], func=AF.Exp, scale=SB_SCALE,
              reads=(("ps", zb),), writes=(("e", k2),))
            A("act", nc.scalar.activation, out=sp32[:, k2, :], in_=e_t[:, k2, :], func=AF.Ln, bias=K["eps_t"][:, 2:3], scale=1.0,
              reads=(("e", k2),), writes=(("sp32", k2),))

        def op_cast_a(n):
            hl, hr, Q, kb, i, first, last, sweep = info(n)
            zb, k2, k4 = ZB[n % 3], n % 2, n % 4
            if i >= 0:
                A("dve", nc.vector.tensor_tensor, sp16[:, k4, :], sp32[:, k2, :], dmask[:, i, :], ALU.mult,
                  reads=(("sp32", k2),), writes=(("sp16", k4),))
            else:
                A("dve", nc.vector.tensor_copy, sp16[:, k4, :], sp32[:, k2, :],
                  reads=(("sp32", k2),), writes=(("sp16", k4),))
            A("dve", nc.vector.scalar_tensor_tensor, out=a_t[:, k2, :], in0=PS[zb], scalar=SB_SCALE, in1=sp32[:, k2, :],
              op0=ALU.mult, op1=ALU.subtract, reads=(("ps", zb), ("sp32", k2)), writes=(("a", k2),))

        def op_p(n):
            hl, hr, Q, kb, i, first, last, sweep = info(n)
            k4 = n % 4
            if not first:
                A("pe", nc.tensor.matmul, PS[2], lhsT=tric_b, rhs=sp16[:, (n - 1) % 4, :], start=False, stop=False,
                  skip_group_check=True, reads=(("sp16", (n - 1) % 4),), writes=(("ps", 2),))
            A("pe", nc.tensor.matmul, PS[2], lhsT=tri_b, rhs=sp16[:, k4, :], start=first, stop=True,
              skip_group_check=True, reads=(("sp16", k4),), writes=(("ps", 2),))

        def op_b(n):
            k2 = n % 2
            A("dve", nc.vector.tensor_tensor, b_t[:, k2, :], a_t[:, k2, :], PS[2], ALU.subtract,
              reads=(("a", k2), ("ps", 2)), writes=(("b", k2),))

        def op_att(n):
            k2, k3 = n % 2, n % 3
            A("act", nc.scalar.activation, out=att[:, k3, :], in_=b_t[:, k2, :], func=AF.Exp,
              reads=(("b", k2),), writes=(("att", k3),))

        def op_attmask(n):
            hl, hr, Q, kb, i, first, last, sweep = info(n)
            k3 = n % 3
            if i >= 0:
                A("dve", nc.vector.tensor_tensor, att[:, k3, :], att[:, k3, :], dmask[:, i, :], ALU.mult,
                  reads=(("att", k3),), writes=(("att", k3),))

        def op_av(n):
            hl, hr, Q, kb, i, first, last, sweep = info(n)
            k3 = n % 3
            ob = 4 + sweep % 2
            A("pe", nc.tensor.matmul, PS[ob], lhsT=V[:, hr, kb, :], rhs=att[:, k3, :], start=first, stop=last,
              reads=(("att", k3),) + v_tok[hl], writes=(("ps", ob),))
            if last:
                yk = sweep % 2
                A("act", nc.scalar.copy, yst[:, yk, :], PS[ob], reads=(("ps", ob),), writes=(("yst", yk),))
                t = Q * 512
                j, tl = t // NTOK, t % NTOK
                row0 = j * cfg.YR + HM * 256 + hl * 128
                A("sp", nc.sync.dma_start, out=EY[row0:row0 + 128, tl:tl + 512], in_=yst[:, yk, :],
                  reads=(("yst", yk),), dma=True)

        loaded = set()
        for n in range(NS + 2):
            if n < NS:
                hl = steps[n][0]
                if hl not in loaded:
                    load_head(hl)
                    loaded.add(hl)
                op_qk(n)
            if 0 <= n - 2 < NS:
                op_att(n - 2)
            if n < NS:
                op_esp(n)
            if 0 <= n - 1 < NS:
                op_p(n - 1)
                op_b(n - 1)
            if 0 <= n - 2 < NS:
                op_attmask(n - 2)
                op_av(n - 2)
            if n < NS:
                op_cast_a(n)


def mlstm(nc, sch, cfg, PS, PSB, K, EXF_o, EXT_o, EXG_o, EY, GSCR, bg_d, conv_d, norm_d):
    A = sch.add
    S, R, NTOK, HM = cfg.S, cfg.R, cfg.NTOK, cfg.HM
    C = S // 64
    Cr = NTOK // 64
    SUB = min(2048, NTOK)
    NSUB = S // SUB
    CS = SUB // 64
    ident_f, ident_b, tri_s, maskM, ones_f, cm64, eps_t = (K["ident_f"], K["ident_b"], K["tri_s"], K["maskM"],
                                                           K["ones_f"], K["cm64"], K["eps_t"])
    with ExitStack() as ph:
        sb = lambda name, shape, dt: ph.enter_context(nc.sbuf_tensor(name, list(shape), dt))
        gt = sb("m_gt", [128, 12, 64], F32)
        IT, FT, SP, NB_, BT_, PMB, AT, IST, EMT, WT_, ONES, TMP = [gt[0:C, i, :] for i in range(12)]
        gsm = sb("m_gsm", [128, 16], F32)
        diagX = sb("m_diagX", [128, 128], F32)
        tmpM = sb("m_tmpM", [128, 128], F32)
        A_bcm = sb("m_Abc", [64, S], F32)
        IS_bc = sb("m_ISbc", [128, S], F32)
        BT = sb("m_BT", [64, 128], F32)
        wT = sb("m_wT", [64, 128], F32)
        emtT = sb("m_emtT", [64, 128], F32)
        dec_bc = sb("m_decbc", [128, 128], F32)
        bgb = sb("m_bgb", [128, 2 * HM], F32)
        convw = sb("m_convw", [128, 2 * HM * 2 * 4], F32)
        normg = sb("m_normg", [64, HM * 256], F32)
        rawp = sb("m_rawp", [128, 2, 4 + SUB], BF16)
        rawr = [0]
        acc = sb("m_acc", [128, SUB], F32)
        sig = sb("m_sig", [128, SUB], F32)
        QTb = sb("m_QTb", [128, 2, SUB], BF16)
        KTb = sb("m_KTb", [128, 2, SUB], BF16)
        QsT = sb("m_QsT", [128, 2, SUB], BF16)
        Vp = sb("m_Vp", [64, CS, 258], BF16)
        OMb = sb("m_OMb", [64, CS, 256], BF16)
        CTf = sb("m_CTf", [128, 2, 257], F32)
        CTb = sb("m_CTb", [128, 2, 258], BF16)
        Dm = sb("m_Dm", [64, 2, 64], F32)
        scT = sb("m_scT", [64, 2, 64], BF16)
        Kw = sb("m_Kw", [64, 2, 256], BF16)
        hraw = sb("m_hraw", [64, 4, 256], F32)
        sm = sb("m_sm", [64, 4, 16], F32)
        yt = sb("m_y", [64, 4, 256], BF16)
        ystg = sb("m_ystg", [128, 2, 2, 512], BF16)

        A("sp", nc.sync.dma_start, out=bgb[:, :], in_=bg_d.partition_broadcast(128), writes=("bgb",), dma=True)
        A("sp", nc.sync.dma_start, out=convw[:, :], in_=conv_d, writes=("convw",), dma=True)
        A("sp", nc.sync.dma_start, out=normg[:, :], in_=norm_d.partition_broadcast(64), writes=("normg",), dma=True)
        A("pool", nc.gpsimd.memset, gt[:, 10, :], 1.0, writes=("ones64",))
        A("pool", nc.gpsimd.memset, Vp[:, :, 256:258], 1.0, writes=("vp1",))

        for hl in range(HM):
            for i in range(R):
                A("sp", nc.sync.dma_start, out=gt[i * Cr:(i + 1) * Cr, 0, :],
                  in_=EXG_o[i * 2 * HM + hl:i * 2 * HM + hl + 1, 0:NTOK].rearrange("o (c t) -> (o c) t", t=64),
                  writes=(("IT", i),), dma=True)
                A("sp", nc.sync.dma_start, out=gt[i * Cr:(i + 1) * Cr, 1, :],
                  in_=EXG_o[i * 2 * HM + HM + hl:i * 2 * HM + HM + hl + 1, 0:NTOK].rearrange("o (c t) -> (o c) t", t=64),
                  writes=(("FT", i),), dma=True)
            ITt = tuple(("IT", i) for i in range(R))
            FTt = tuple(("FT", i) for i in range(R))
            V_ = nc.vector
            A("dve", V_.tensor_scalar, IT, IT, bgb[0:C, hl:hl + 1], None, ALU.add, reads=ITt + ("bgb",), writes=ITt)
            A("dve", V_.tensor_scalar, FT, FT, bgb[0:C, HM + hl:HM + hl + 1], None, ALU.add, reads=FTt + ("bgb",), writes=FTt)
            A("act", nc.scalar.activation, out=SP, in_=FT, func=AF.Exp, scale=-1.0, reads=FTt, writes=("SP",))
            A("act", nc.scalar.activation, out=SP, in_=SP, func=AF.Ln, bias=eps_t[0:C, 2:3], scale=1.0, reads=("SP",), writes=("SP",))
            A("dve", V_.tensor_tensor_scan, NB_, ONES, SP, 0.0, ALU.mult, ALU.add, reads=("SP", "ones64"), writes=("NB",))
            A("dve", V_.tensor_tensor, BT_, IT, NB_, ALU.add, reads=ITt + ("NB",), writes=("B",))
            A("dve", V_.tensor_tensor_scan, PMB, ONES, BT_, NEG, ALU.mult, ALU.max, reads=("B", "ones64"), writes=("PMB",))
            G_ = ("gsm",)
            A("dve", V_.tensor_scalar, gsm[0:C, 0:1], NB_[:, 63:64], -1.0, None, ALU.mult, reads=("NB",), writes=G_)
            A("dve", V_.memset, gsm[0:C, 1:2], 0.0, reads=G_, writes=G_)
            A("pe", nc.tensor.matmul, PS[6][0:C, 0:2], lhsT=tri_s[0:C, 0:C], rhs=gsm[0:C, 0:2], start=True, stop=True,
              reads=G_, writes=(("ps", 6),))
            A("act", nc.scalar.copy, gsm[0:C, 1:2], PS[6][0:C, 0:1], reads=(("ps", 6),) + G_, writes=G_)
            A("dve", V_.tensor_tensor, gsm[0:C, 2:3], PMB[:, 63:64], gsm[0:C, 1:2], ALU.subtract, reads=("PMB",) + G_, writes=G_)
            A("dve", V_.tensor_scalar, diagX[0:C, 0:C], ident_f[0:C, 0:C], gsm[0:C, 2:3], None, ALU.mult, reads=G_, writes=("diagX",))
            A("pe", nc.tensor.matmul, PS[7][0:C, 0:C], lhsT=ones_f[0:C, 0:C], rhs=diagX[0:C, 0:C], start=True, stop=True,
              reads=("diagX",), writes=(("ps", 7),))
            A("dve", V_.tensor_tensor, tmpM[0:C, 0:C], PS[7][0:C, 0:C], maskM[0:C, 0:C], ALU.add, reads=(("ps", 7),), writes=("tmpM",))
            A("dve", V_.tensor_reduce, gsm[0:C, 3:4], tmpM[0:C, 0:C], AX.X, ALU.max, reads=("tmpM",) + G_, writes=G_)
            A("dve", V_.tensor_tensor, gsm[0:C, 4:5], gsm[0:C, 3:4], gsm[0:C, 1:2], ALU.add, reads=G_, writes=G_)
            A("dve", V_.tensor_tensor, gsm[0:C, 5:6], gsm[0:C, 4:5], PMB[:, 63:64], ALU.max, reads=G_ + ("PMB",), writes=G_)
            A("dve", V_.tensor_scalar, gsm[0:C, 6:7], gsm[0:C, 5:6], -1.0, None, ALU.mult, reads=G_, writes=G_)
            A("dve", V_.tensor_tensor, gsm[0:C, 8:9], gsm[0:C, 4:5], gsm[0:C, 5:6], ALU.subtract, reads=G_, writes=G_)
            A("dve", V_.tensor_scalar, AT, PMB, gsm[0:C, 4:5], -1.0, ALU.max, ALU.mult, reads=("PMB",) + G_, writes=("AT",))
            A("act", nc.scalar.activation, out=IST, in_=AT, func=AF.Exp, bias=gsm[0:C, 4:5], scale=1.0, reads=("AT",) + G_, writes=("IST",))
            A("dve", V_.tensor_tensor, TMP, AT, NB_, ALU.add, reads=("AT", "NB"), writes=("TMP",))
            A("act", nc.scalar.activation, out=EMT, in_=TMP, func=AF.Exp, reads=("TMP",), writes=("EMT",))
            A("act", nc.scalar.activation, out=WT_, in_=BT_, func=AF.Exp, bias=gsm[0:C, 6:7], scale=1.0, reads=("B",) + G_, writes=("WT",))
            A("act", nc.scalar.activation, out=gsm[0:C, 7:8], in_=gsm[0:C, 8:9], func=AF.Exp, reads=G_, writes=G_)
            A("sp", nc.sync.dma_start, out=GSCR[0:1, 0:S].rearrange("o (c t) -> (o c) t", t=64), in_=AT,
              reads=("AT",), writes=(("GS", 0),), dma=True)
            A("sp", nc.sync.dma_start, out=GSCR[1:2, 0:S].rearrange("o (c t) -> (o c) t", t=64), in_=IST,
              reads=("IST",), writes=(("GS", 1),), dma=True)
            A("sp", nc.sync.dma_start, out=GSCR[2:3, 0:C].rearrange("o c -> c o"), in_=gsm[0:C, 7:8],
              reads=G_, writes=(("GS", 2),), dma=True)
            A("sp", nc.sync.dma_start, out=A_bcm[:, :], in_=GSCR[0:1, 0:S].partition_broadcast(64),
              reads=(("GS", 0),), writes=("Abc",), dma=True)
            A("sp", nc.sync.dma_start, out=IS_bc[:, :], in_=GSCR[1:2, 0:S].partition_broadcast(128),
              reads=(("GS", 1),), writes=("ISbc",), dma=True)
            A("sp", nc.sync.dma_start, out=dec_bc[:, 0:C], in_=GSCR[2:3, 0:C].partition_broadcast(128),
              reads=(("GS", 2),), writes=("decbc",), dma=True)
            A3 = A_bcm[:, :].rearrange("p (c t) -> p c t", t=64)
            A("dve", V_.tensor_tensor, A3, A3, cm64[:, :].unsqueeze(1).broadcast_to([64, C, 64]), ALU.add,
              reads=("Abc",), writes=("Abc",))
            for (src, dst, nm) in ((BT_, BT, "BT"), (WT_, wT, "wT"), (EMT, emtT, "emtT")):
                A("pe", nc.tensor.transpose, PS[6][0:64, 0:C], src, ident_f[0:C, 0:C],
                  reads=("B", "WT", "EMT"), writes=(("ps", 6),))
                A("act", nc.scalar.copy, dst[:, 0:C], PS[6][0:64, 0:C], reads=(("ps", 6),), writes=(nm,))
            A("pool", nc.gpsimd.memset, CTf[:, :, :], 0.0, writes=(("CTf", 0), ("CTf", 1)))
            A("pool", nc.gpsimd.memset, CTb[:, :, :], 0.0, writes=("CTb",))

            for sbk in range(NSUB):
                t0 = sbk * SUB
                irank, tl = t0 // NTOK, t0 % NTOK
                for qk, dstT, base, scl, nm in ((0, QTb, cfg.QM0, 1.0, "QTb"), (1, KTb, cfg.KM0, K_SCALE, "KTb")):
                    for dc in range(2):
                        rr = rawr[0] % 2
                        rawr[0] += 1
                        row0 = irank * cfg.RF + base + (hl * 2 + dc) * 128
                        A("sp", nc.sync.dma_start, out=rawp[:, rr, 3:3 + SUB], in_=EXF_o[row0:row0 + 128, tl:tl + SUB],
                          writes=(("rawp", rr),), dma=True)
                        if t0 == 0:
                            A("pool", nc.gpsimd.memset, rawp[:, rr, 0:3], 0.0, writes=(("rawh", rr),))
                        elif tl >= 3:
                            A("sp", nc.sync.dma_start, out=rawp[:, rr, 0:3], in_=EXF_o[row0:row0 + 128, tl - 3:tl],
                              writes=(("rawh", rr),), dma=True)
                        else:
                            rowp = (irank - 1) * cfg.RF + base + (hl * 2 + dc) * 128
                            A("sp", nc.sync.dma_start, out=rawp[:, rr, 0:3], in_=EXF_o[rowp:rowp + 128, NTOK - 3:NTOK],
                              writes=(("rawh", rr),), dma=True)
                        wi = ((qk * HM + hl) * 2 + dc) * 4
                        RD = (("rawp", rr), ("rawh", rr), "convw")
                        A("dve", V_.tensor_scalar, acc[:, :], rawp[:, rr, 3:3 + SUB], convw[:, wi + 3:wi + 4], None, ALU.mult,
                          reads=RD, writes=("acc",))
                        for j in range(3):
                            A("dve", V_.scalar_tensor_tensor, out=acc[:, :], in0=rawp[:, rr, j:j + SUB],
                              scalar=convw[:, wi + j:wi + j + 1], in1=acc[:, :], op0=ALU.mult, op1=ALU.add,
                              reads=RD + ("acc",), writes=("acc",))
                        A("act", nc.scalar.activation, out=sig[:, :], in_=acc[:, :], func=AF.Sigmoid, reads=("acc",), writes=("sig",))
                        A("dve", V_.scalar_tensor_tensor, out=dstT[:, dc, :], in0=acc[:, :], scalar=scl, in1=sig[:, :],
                          op0=ALU.mult, op1=ALU.mult, reads=("acc", "sig"), writes=((nm, dc),))
                for dc in range(2):
                    A("dve", V_.tensor_tensor, QsT[:, dc, :], QTb[:, dc, :], IS_bc[:, t0:t0 + SUB], ALU.mult,
                      reads=(("QTb", dc), "ISbc"), writes=(("QsT", dc),))
                r0 = irank * NTOK + tl
                A("sp", nc.sync.dma_start, out=Vp[:, :, 0:256],
                  in_=EXT_o[r0:r0 + SUB, cfg.VM0 + hl * 256:cfg.VM0 + (hl + 1) * 256].rearrange("(c s) v -> s c v", s=64),
                  writes=("Vp",), dma=True)
                A("sp", nc.sync.dma_start, out=OMb[:, :, :],
                  in_=EXT_o[r0:r0 + SUB, cfg.OM0 + hl * 256:cfg.OM0 + (hl + 1) * 256].rearrange("(c s) v -> s c v", s=64),
                  writes=("OMb",), dma=True)
                def ci(cl):
                    c = sbk * CS + cl
                    return c, c % 2, c % 4, slice(cl * 64, (cl + 1) * 64)

                def pe_front(cl):
                    c, r, r4, csl = ci(cl)
                    ps_s = PS[0][0:64, 0:64]
                    for dc in range(2):
                        A("pe", nc.tensor.matmul, ps_s, lhsT=KTb[:, dc, csl], rhs=QTb[:, dc, csl], start=(dc == 0), stop=(dc == 1),
                          reads=(("KTb", dc), ("QTb", dc)), writes=(("ps", 0),))
                    for dc in range(2):
                        A("pe", nc.tensor.transpose, PSB[1][0:64, dc * 128:(dc + 1) * 128], KTb[:, dc, csl], ident_b,
                          reads=(("KTb", dc),), writes=(("ps", 1),))

                def act_d(cl):
                    c, r, r4, csl = ci(cl)
                    A("act", nc.scalar.activation, out=Dm[:, r, :], in_=A_bcm[:, c * 64:(c + 1) * 64], func=AF.Exp,
                      bias=BT[:, c:c + 1], scale=1.0, reads=("Abc", "BT"), writes=(("Dm", r),))

                def dve_front(cl):
                    c, r, r4, csl = ci(cl)
                    A("dve", V_.tensor_tensor, scT[:, r, :], PS[0][0:64, 0:64], Dm[:, r, :], ALU.mult,
                      reads=(("ps", 0), ("Dm", r)), writes=(("scT", r),))
                    A("dve", V_.tensor_scalar, Kw[:, r, :], PSB[1][0:64, 0:256], wT[:, c:c + 1], None, ALU.mult,
                      reads=(("ps", 1), "wT"), writes=(("Kw", r),))

                def pe_core(cl):
                    c, r, r4, csl = ci(cl)
                    ub = 4 + 2 * r
                    for dc in range(2):
                        A("pe", nc.tensor.matmul, PS[ub + dc][:, 0:257], lhsT=Kw[:, r, dc * 128:(dc + 1) * 128], rhs=Vp[:, cl, 0:257],
                          start=True, stop=True, reads=(("Kw", r), "Vp", "vp1"), writes=(("ps", ub + dc),))
                    ps_n = PS[2][0:64, 0:257]
                    A("pe", nc.tensor.matmul, ps_n, lhsT=scT[:, r, :], rhs=Vp[:, cl, 0:257], start=True, stop=False,
                      reads=(("scT", r), "Vp", "vp1"), writes=(("ps", 2),))
                    for dc in range(2):
                        A("pe", nc.tensor.matmul, ps_n, lhsT=QsT[:, dc, csl], rhs=CTb[:, dc, 0:257], start=False, stop=(dc == 1),
                          reads=(("QsT", dc), "CTb"), writes=(("ps", 2),))

                def dve_core(cl):
                    c, r, r4, csl = ci(cl)
                    ub = 4 + 2 * r
                    ps_n = PS[2][0:64, 0:257]
                    SMT = (("sm", r4),)
                    A("dve", V_.tensor_scalar, sm[:, r4, 6:7], ps_n[:, 256:257], -1.0, emtT[:, c:c + 1], ALU.mult, ALU.max,
                      reads=(("ps", 2), "emtT"), writes=SMT)
                    A("dve", V_.scalar_tensor_tensor, out=CTf[:, 0, :], in0=CTf[:, 0, :], scalar=dec_bc[:, c:c + 1],
                      in1=PS[ub][:, 0:257], op0=ALU.mult, op1=ALU.add,
                      reads=(("CTf", 0), "decbc", ("ps", ub)), writes=(("CTf", 0),))
                    A("dve", V_.tensor_tensor, sm[:, r4, 0:1], sm[:, r4, 6:7], ps_n[:, 256:257], ALU.max,
                      reads=(("ps", 2),) + SMT, writes=SMT)
                    A("dve", V_.scalar_tensor_tensor, out=CTf[:, 1, :], in0=CTf[:, 1, :], scalar=dec_bc[:, c:c + 1],
                      in1=PS[ub + 1][:, 0:257], op0=ALU.mult, op1=ALU.add,
                      reads=(("CTf", 1), "decbc", ("ps", ub + 1)), writes=(("CTf", 1),))
                    A("dve", V_.reciprocal, sm[:, r4, 1:2], sm[:, r4, 0:1], reads=SMT, writes=SMT)

                def act_core(cl):
                    A("act", nc.scalar.copy, CTb[:, :, 0:257], CTf[:, :, :], reads=(("CTf", 0), ("CTf", 1)), writes=("CTb",))

                def dve_h(cl):
                    c, r, r4, csl = ci(cl)
                    SMT = (("sm", r4),)
                    A("dve", V_.tensor_scalar, hraw[:, r4, :], PS[2][0:64, 0:256], sm[:, r4, 1:2], None, ALU.mult,
                      reads=(("ps", 2),) + SMT, writes=(("hraw", r4),))

                def dve_bn(cl):
                    c, r, r4, csl = ci(cl)
                    SMT = (("sm", r4),)
                    A("dve", V_.bn_stats, sm[:, r4, 8:14], hraw[:, r4, :], reads=(("hraw", r4),), writes=SMT)
                    A("dve", V_.bn_aggr, sm[:, r4, 2:4], sm[:, r4, 8:14], reads=SMT, writes=SMT)

                def act_sqrt(cl):
                    c, r, r4, csl = ci(cl)
                    SMT = (("sm", r4),)
                    A("act", nc.scalar.activation, out=sm[:, r4, 4:5], in_=sm[:, r4, 3:4], func=AF.Sqrt, bias=eps_t[0:64, 1:2], scale=1.0,
                      reads=SMT, writes=SMT)

                def dve_norm(cl):
                    c, r, r4, csl = ci(cl)
                    SMT = (("sm", r4),)
                    A("dve", V_.reciprocal, sm[:, r4, 5:6], sm[:, r4, 4:5], reads=SMT, writes=SMT)
                    A("dve", V_.tensor_scalar, hraw[:, r4, :], hraw[:, r4, :], sm[:, r4, 2:3], sm[:, r4, 5:6], ALU.subtract, ALU.mult,
                      reads=(("hraw", r4),) + SMT, writes=(("hraw", r4),))
                    A("pool", nc.gpsimd.tensor_tensor, hraw[:, r4, :], hraw[:, r4, :], normg[:, hl * 256:(hl + 1) * 256], ALU.mult,
                      reads=(("hraw", r4), "normg"), writes=(("hraw", r4),))
                    A("pool", nc.gpsimd.tensor_tensor, yt[:, r4, :], hraw[:, r4, :], OMb[:, cl, :], ALU.mult,
                      reads=(("hraw", r4), "OMb"), writes=(("yt", r4),))

                def tail(cl):
                    c, r, r4, csl = ci(cl)
                    yb = 3
                    for dc in range(2):
                        A("pe", nc.tensor.transpose, PSB[yb][:, dc * 64:(dc + 1) * 64], yt[:, r4, dc * 128:(dc + 1) * 128],
                          ident_b[0:64, 0:64], reads=(("yt", r4),), writes=(("ps", yb),))
                    yr = (c // 8) % 2
                    A("act", nc.scalar.copy, ystg[:, yr, :, (c % 8) * 64:(c % 8 + 1) * 64],
                      PSB[yb][:, 0:128].rearrange("p (d t) -> p d t", t=64), reads=(("ps", yb),), writes=(("ystg", yr, c % 8),))
                    if c % 8 == 7:
                        tt = (c - 7) * 64
                        j, tlo = tt // NTOK, tt % NTOK
                        row0 = j * cfg.YR + hl * 256
                        A("sp", nc.sync.dma_start, out=EY[row0:row0 + 256, tlo:tlo + 512].rearrange("(d p) t -> p d t", p=128),
                          in_=ystg[:, yr, :, :], reads=tuple(("ystg", yr, q) for q in range(8)), dma=True)

                ok = lambda k: 0 <= k < CS
                for it in range(CS + 5):
                    if ok(it):
                        pe_front(it)
                        act_d(it)
                    if ok(it - 1):
                        pe_core(it - 1)
                    if ok(it):
                        dve_front(it)
                    if ok(it - 1):
                        dve_core(it - 1)
                        act_core(it - 1)
                        dve_h(it - 1)
                    if ok(it - 2):
                        dve_bn(it - 2)
                        act_sqrt(it - 2)
                    if ok(it - 3):
                        dve_norm(it - 3)
                    if ok(it - 4):
                        tail(it - 4)


def make_in_maps(inputs, cfg, core_assign):
    maps = []
    R, NTOK, HM = cfg.R, cfg.NTOK, cfg.HM
    wnames = ("ffn1_w1", "ffn1_w3", "ffn1_w2", "w_in", "w_up_m", "w_up_sb", "w_out", "ffn2_w1", "ffn2_w3",
              "ffn2_w2", "w_ple_gate", "w_ple_proj")
    shared = {n: np.ascontiguousarray(np.asarray(inputs[n])[0], dtype=np.float32) for n in wnames}
    for n in ("ln1_g", "ln1_b", "ln2_g", "ln2_b", "ln3_g", "ln3_b"):
        shared[n] = np.ascontiguousarray(np.asarray(inputs[n])[0].reshape(1, D), dtype=np.float32)
    bgm = np.asarray(inputs["b_gates_m"])[0]
    convm = np.asarray(inputs["conv_m"])[0]
    normm = np.asarray(inputs["norm_m"])[0]
    x = np.asarray(inputs["x"])
    p = np.asarray(inputs["p"])[0]
    for (b, rk) in core_assign:
        m = dict(shared)
        m["x"] = np.ascontiguousarray(x[b, rk * NTOK:(rk + 1) * NTOK], dtype=np.float32)
        m["p"] = np.ascontiguousarray(p[b, rk * NTOK:(rk + 1) * NTOK], dtype=np.float32)
        h0 = rk * HM
        m["bgates"] = np.ascontiguousarray(
            np.concatenate([bgm[h0:h0 + HM], bgm[NHM + h0:NHM + h0 + HM]]).reshape(1, 2 * HM), dtype=np.float32)
        cv = convm.reshape(4, 2, NHM, 2, 128)[:, :, h0:h0 + HM]
        m["convT"] = np.ascontiguousarray(cv.transpose(4, 1, 2, 3, 0).reshape(128, 2 * HM * 2 * 4), dtype=np.float32)
        m["normm"] = np.ascontiguousarray(normm[h0 * 256:(h0 + HM) * 256].reshape(1, HM * 256), dtype=np.float32)
        maps.append(m)
    return maps


S_FULL = 8192
_NC_CACHE = {}


def kernel(**inputs):
    cfg = Cfg(S_FULL, 1)
    if "nc" not in _NC_CACHE:
        _NC_CACHE["nc"] = build(cfg)
    nc = _NC_CACHE["nc"]
    B = np.asarray(inputs["x"]).shape[0]
    assign = [(c // 2, 0) for c in range(8)]
    maps = make_in_maps(inputs, cfg, assign)
    res = run_bass_kernel_spmd(nc, maps, core_ids=list(range(8)))
    out = np.stack([res.results[2 * b]["out"] for b in range(B)], axis=0)
    return out.astype(np.float32, copy=False)
```

```python
import numpy as np
from contextlib import ExitStack
import concourse.bass as bass
import concourse.mybir as mybir
from concourse.bass_utils import run_bass_kernel_spmd

F32 = mybir.dt.float32
BF16 = mybir.dt.bfloat16
ALU = mybir.AluOpType
AF = mybir.ActivationFunctionType
AX = mybir.AxisListType

D = 2048
DFF = 5632
NJ = DFF // 128
KC = D // 128
T = 512
NG = T // 128
DPLE = 256
NHM, DHM = 4, 256
NHS, DHS = 8, 128
INCOLS = 11272
C_MQ, C_MK, C_MV, C_MO, C_GT, C_SQ, C_SK, C_SV, C_GA, C_GB = 0, 1024, 2048, 3072, 4096, 4104, 5128, 6152, 7176, 9224
ALPHA = 2.0 ** 0.25
CRES = 0.5 / ALPHA
INVA = 1.0 / ALPHA
EPS_LN = 1e-5 / (ALPHA * ALPHA)
EPS_H = 1e-5
NEG = -1e30
SB_SCALE = DHS ** -0.5
K_SCALE = DHM ** -0.5
SAME_ENGINE_SYNC = True


class Op:
    __slots__ = ("eng", "fn", "args", "kw", "dma", "deps", "signal", "sem", "val")

    def __init__(self, eng, fn, args, kw, dma):
        self.eng, self.fn, self.args, self.kw, self.dma = eng, fn, args, kw, dma
        self.deps = []
        self.signal = False
        self.sem = None
        self.val = 0


class Sched:
    COMPUTE = ("pe", "act", "dve", "pool")

    def __init__(self, nc, stack):
        self.nc = nc
        self.h = {"pe": nc.tensor, "act": nc.scalar, "dve": nc.vector, "pool": nc.gpsimd, "sp": nc.sync}
        self.esem = {e: stack.enter_context(nc.semaphore("s_" + e)) for e in self.COMPUTE}
        self.ecount = {e: 0 for e in self.COMPUTE}
        self.dsems = {}
        for q, n in (("sp", 20), ("pool", 12), ("act", 4)):
            self.dsems[q] = [stack.enter_context(nc.semaphore("d_%s_%d" % (q, i))) for i in range(n)]
        self.dlast = {q: [0] * len(v) for q, v in self.dsems.items()}
        self.dnext = {q: 0 for q in self.dsems}
        self.waited = {e: {} for e in self.h}
        self.ops = []
        self.tok = {}
        self.n_inst = 0

    def add(self, eng, fn, *args, reads=(), writes=(), dma=False, **kw):
        op = Op(eng, fn, args, kw, dma)
        tok = self.tok
        deps = []
        for t in reads:
            st = tok.get(t)
            if st is None:
                st = tok[t] = [None, {}, []]
            if st[0] is not None:
                deps.append(st[0])
        for t in writes:
            st = tok.get(t)
            if st is None:
                st = tok[t] = [None, {}, []]
            if st[0] is not None:
                deps.append(st[0])
            deps.extend(st[1].values())
            deps.extend(st[2])
        for t in reads:
            st = tok[t]
            if dma:
                st[2].append(op)
            else:
                st[1][eng] = op
        for t in writes:
            st = tok[t]
            st[0] = op
            st[1] = {}
            st[2] = []
        seen = set()
        for d in deps:
            if d is op or id(d) in seen:
                continue
            seen.add(id(d))
            if (not d.dma) and (not dma) and d.eng == eng and (eng == "pe" or not SAME_ENGINE_SYNC):
                continue
            d.signal = True
            op.deps.append(d)
        self.ops.append(op)
        return op

    def _wait(self, eng, sem, val):
        w = self.waited[eng]
        if w.get(id(sem), 0) < val:
            self.h[eng].wait_ge(sem, val)
            w[id(sem)] = val
            self.n_inst += 1

    def barrier(self):
        last = {}
        for op in self.ops:
            if not op.dma:
                last[op.eng] = op
        for op in last.values():
            op.signal = True
        for op in self.ops:
            for d in op.deps:
                self._wait(op.eng, d.sem, d.val)
            if op.dma:
                q = op.eng
                k = self.dnext[q]
                self.dnext[q] = (k + 1) % len(self.dsems[q])
                sem = self.dsems[q][k]
                prev = self.dlast[q][k]
                if prev > 0:
                    self._wait(q, sem, prev)
                ins = op.fn(*op.args, **op.kw)
                ins.then_inc(sem, 16)
                self.dlast[q][k] = prev + 16
                op.sem, op.val = sem, prev + 16
            else:
                ins = op.fn(*op.args, **op.kw)
                if op.signal:
                    self.ecount[op.eng] += 1
                    ins.then_inc(self.esem[op.eng], 1)
                    op.sem, op.val = self.esem[op.eng], self.ecount[op.eng]
            self.n_inst += 1
        self.ops = []
        self.tok = {}
        for e in self.h:
            for c in self.COMPUTE:
                if self.ecount[c] > 0:
                    self._wait(e, self.esem[c], self.ecount[c])
            for q, sems in self.dsems.items():
                for k, sem in enumerate(sems):
                    if self.dlast[q][k] > 0:
                        self._wait(e, sem, self.dlast[q][k])


class Cfg:
    def __init__(self, S, R, split=False):
        self.split = split
        self.P = S // 2 if split else 0
        self.NOWN = S - self.P
        self.S = S
        self.R = R
        self.NTOK = S // R
        self.HM = NHM // R
        self.HS = NHS // R
        self.RF = 4096 // R
        self.CT = 3072 // R
        self.QM0 = 0
        self.KM0 = self.HM * 256
        self.QS0 = 2 * self.HM * 256
        self.KS0 = self.QS0 + self.HS * 128
        self.VM0 = 0
        self.OM0 = self.HM * 256
        self.VS0 = 2 * self.HM * 256
        self.YR = 2048 // R
        assert self.NTOK % T == 0 and (not split or self.P % T == 0)


def build(cfg, debug=False):
    S, R, NTOK, HM, HS = cfg.S, cfg.R, cfg.NTOK, cfg.HM, cfg.HS
    nc = bass.Bass("TRN2", target_bir_lowering=False)
    dt_in = lambda name, shape: nc.dram_tensor(name, list(shape), F32, kind="ExternalInput").ap()
    P, NOWN, SPLIT = cfg.P, cfg.NOWN, cfg.split
    x_d = dt_in("x", [NOWN, D])
    p_d = dt_in("p", [NOWN, DPLE])
    xp_d = dt_in("xpre", [P, D]) if SPLIT else None
    flg_d = dt_in("flg", [128, 2]) if SPLIT else None
    w = {}
    for name, shape in (("ffn1_w1", [D, DFF]), ("ffn1_w3", [D, DFF]), ("ffn1_w2", [DFF, D]),
                        ("w_in", [D, INCOLS]), ("w_up_m", [1024, D]), ("w_up_sb", [1024, D]),
                        ("w_out", [D, D]), ("ffn2_w1", [D, DFF]), ("ffn2_w3", [D, DFF]),
                        ("ffn2_w2", [DFF, D]), ("w_ple_gate", [D, D]), ("w_ple_proj", [DPLE, D])):
        w[name] = dt_in(name, shape)
    lnp = {n: dt_in(n, [1, D]) for n in ("ln1_g", "ln1_b", "ln2_g", "ln2_b", "ln3_g", "ln3_b")}
    bg_d = dt_in("bgates", [1, 2 * HM])
    conv_d = dt_in("convT", [128, 2 * HM * 2 * 4])
    norm_d = dt_in("normm", [1, HM * 256])
    out_d = nc.dram_tensor("out", [NOWN, D], F32, kind="ExternalOutput").ap()

    skind = "ExternalOutput" if debug else "Internal"
    dscr = lambda name, shape, dt: nc.dram_tensor(name, list(shape), dt, kind=skind).ap()
    X1 = dscr("X1", [NOWN, D], F32)
    GA = dscr("GA", [D, NOWN], BF16)
    GB = dscr("GB", [D, NOWN], BF16)
    EXF = dscr("EXF", [4096, NTOK], BF16)
    EXT = dscr("EXT", [R * NTOK, cfg.CT], BF16)
    EXG = dscr("EXG", [8, NTOK], F32)
    EY = dscr("EY", [2048, NOWN], BF16)
    GSCR = dscr("GSCR", [4, S], F32)
    EXF_o, EXT_o, EXG_o, EY_o = EXF, EXT, EXG, EY

    with ExitStack() as top:
        sch = Sched(nc, top)
        A = sch.add
        ps_t = top.enter_context(nc.psum_tensor("ps", [128, 8, 512], F32))
        PS = [ps_t[:, b, :] for b in range(8)]
        PSB = [ps_t[:, b, :].bitcast(BF16) for b in range(8)]

        def pstok(b):
            return ("ps", b)

        cst = top.enter_context(nc.sbuf_tensor("cst", [128, 4 * 128], F32))
        ident_f = cst[:, 0:128]
        tri_s = cst[:, 128:256]
        maskM = cst[:, 256:384]
        ones_f = cst[:, 384:512]
        cstb = top.enter_context(nc.sbuf_tensor("cstb", [128, 3 * 128], BF16))
        ident_b = cstb[:, 0:128]
        tri_b = cstb[:, 128:256]
        tric_b = cstb[:, 256:384]
        cm64 = top.enter_context(nc.sbuf_tensor("cm64", [64, 64], F32))
        dmask = top.enter_context(nc.sbuf_tensor("dmask", [128, 4, 512], BF16))
        tmpc = top.enter_context(nc.sbuf_tensor("tmpc", [128, 512], F32))
        CT_ = ("cst",)
        g = nc.gpsimd

        def sel(out, in_, pattern, op, fill, base, cm):
            A("pool", g.affine_select, out=out, in_=in_, pattern=pattern, compare_op=op, fill=fill,
              base=base, channel_multiplier=cm, reads=CT_, writes=CT_)

        def cp(out, in_):
            A("pool", g.tensor_copy, out, in_, reads=CT_, writes=CT_)

        A("pool", g.memset, cst[:, :], 1.0, writes=CT_)
        sel(ident_f, ident_f, [[-1, 128]], ALU.is_ge, 0.0, 0, 1)
        sel(ident_f, ident_f, [[1, 128]], ALU.is_ge, 0.0, 0, -1)
        cp(ident_b, ident_f)
        sel(tri_s, tri_s, [[1, 128]], ALU.is_gt, 0.0, 0, -1)
        A("pool", g.memset, maskM, 0.0, reads=CT_, writes=CT_)
        sel(maskM, maskM, [[-1, 128]], ALU.is_gt, NEG, 0, 1)
        A("pool", g.memset, tmpc[:, :], 1.0, reads=CT_, writes=CT_)
        sel(tmpc[:, 0:128], tmpc[:, 0:128], [[-1, 128]], ALU.is_gt, 0.0, 0, 1)
        cp(tri_b, tmpc[:, 0:128])
        sel(tmpc[:, 128:256], tmpc[:, 128:256], [[1, 128]], ALU.is_ge, 0.0, 0, -1)
        cp(tric_b, tmpc[:, 128:256])
        A("pool", g.memset, cm64[:, :], 0.0, reads=CT_, writes=CT_)
        sel(cm64[:, :], cm64[:, :], [[1, 64]], ALU.is_ge, NEG, 0, -1)
        for i in range(4):
            A("pool", g.memset, tmpc[:, :], 1.0, reads=CT_, writes=CT_)
            sel(tmpc[:, :], tmpc[:, :], [[1, 512]], ALU.is_gt, 0.0, -128 * i, -1)
            cp(dmask[:, i, :], tmpc[:, :])
        sch.barrier()

        class Ring:
            def __init__(self, n):
                self.n, self.i = n, 0

            def next(self):
                k = self.i
                self.i = (k + 1) % self.n
                return k

        def wload(wr, slot, wap, r0, nk, c0, pc, k0=0):
            step = 4
            for ka in range(0, nk, step):
                kb = min(nk, ka + step)
                src = wap[r0 + ka * 128: r0 + kb * 128, c0:c0 + pc].rearrange("(k p) n -> p k n", p=128)
                A("pool", nc.gpsimd.dma_start, out=wr[:, slot, k0 + ka:k0 + kb, 0:pc], in_=src,
                  writes=(("w", slot, (k0 + ka) // step),), dma=True)

        def to_fm(s_tm, xT, xb, xbr, pbank):
            for gi in range(NG):
                k = xbr.next()
                A("dve", nc.vector.tensor_copy, xb[:, k, :], s_tm[:, gi, :],
                  reads=(("s", gi),), writes=(("xb", k),))
                b0 = pbank[gi % 2]
                for half in range(2):
                    pb = PSB[b0 + half]
                    for q in range(8):
                        kc = half * 8 + q
                        A("pe", nc.tensor.transpose, pb[:, q * 128:(q + 1) * 128], xb[:, k, kc * 128:(kc + 1) * 128],
                          ident_b, reads=(("xb", k), "cst"), writes=(pstok(b0 + half),))
                    A("act", nc.scalar.copy, xT[:, half * 8:(half + 1) * 8, gi * 128:(gi + 1) * 128],
                      pb[:, :].rearrange("p (k t) -> p k t", t=128),
                      reads=(pstok(b0 + half),), writes=(("xT", gi, half),))

        WT = [tuple(("w", sl_, q_) for q_ in range(4)) for sl_ in range(8)]
        XT_ALL = tuple(("xT", gi, h) for gi in range(NG) for h in range(2))

        def ffn(xT, gT, s_tm, wr, ring, w1, w3, w2, tmp, tmpr):
            for j4 in range(NJ // 4):
                s1 = ring.next()
                wload(wr, s1, w1, 0, KC, j4 * 512, 512)
                s3 = ring.next()
                wload(wr, s3, w3, 0, KC, j4 * 512, 512)
                for jj in range(4):
                    j = j4 * 4 + jj
                    b1, b3 = (0, 1) if j % 2 == 0 else (2, 3)
                    for kc in range(KC):
                        A("pe", nc.tensor.matmul, PS[b1], lhsT=wr[:, s1, kc, jj * 128:(jj + 1) * 128], rhs=xT[:, kc, :],
                          start=(kc == 0), stop=(kc == KC - 1), reads=WT[s1] + XT_ALL, writes=(pstok(b1),))
                    for kc in range(KC):
                        A("pe", nc.tensor.matmul, PS[b3], lhsT=wr[:, s3, kc, jj * 128:(jj + 1) * 128], rhs=xT[:, kc, :],
                          start=(kc == 0), stop=(kc == KC - 1), reads=WT[s3] + XT_ALL, writes=(pstok(b3),))
                    k = tmpr.next()
                    A("act", nc.scalar.activation, out=tmp[:, k, 0, :], in_=PS[b1], func=AF.Sigmoid,
                      reads=(pstok(b1),), writes=(("tmp", k, 0),))
                    A("dve", nc.vector.tensor_tensor, tmp[:, k, 1, :], PS[b1], tmp[:, k, 0, :], ALU.mult,
                      reads=(pstok(b1), ("tmp", k, 0)), writes=(("tmp", k, 1),))
                    A("dve", nc.vector.tensor_tensor, gT[:, j, :], PS[b3], tmp[:, k, 1, :], ALU.mult,
                      reads=(pstok(b3), ("tmp", k, 1)), writes=(("gT", j),))
            for slab in range(4):
                for piece in range(4):
                    sl = ring.next()
                    wload(wr, sl, w2, piece * 11 * 128, 11, slab * 512, 512)
                    for gi in range(NG):
                        for jj in range(11):
                            j = piece * 11 + jj
                            A("pe", nc.tensor.matmul, PS[4 + gi], lhsT=gT[:, j, gi * 128:(gi + 1) * 128],
                              rhs=wr[:, sl, jj, :], start=(j == 0), stop=(j == NJ - 1),
                              reads=WT[sl] + (("gT", j),), writes=(pstok(4 + gi),))
                for gi in range(NG):
                    A("dve", nc.vector.scalar_tensor_tensor, out=s_tm[:, gi, slab * 512:(slab + 1) * 512],
                      in0=PS[4 + gi], scalar=CRES, in1=s_tm[:, gi, slab * 512:(slab + 1) * 512],
                      op0=ALU.mult, op1=ALU.add, reads=(pstok(4 + gi), ("s", gi)), writes=(("s", gi),))

        def layernorm(s_tm, lnt, gname, bname, st, eps):
            A("sp", nc.sync.dma_start, out=lnt[:, 0, :], in_=lnp[gname].partition_broadcast(128),
              writes=(("lnt", 0),), dma=True)
            A("sp", nc.sync.dma_start, out=lnt[:, 1, :], in_=lnp[bname].partition_broadcast(128),
              writes=(("lnt", 1),), dma=True)
            for gi in range(NG):
                stt = ("st", gi)
                for q in range(4):
                    A("dve", nc.vector.bn_stats, st[:, gi, q * 6:(q + 1) * 6], s_tm[:, gi, q * 512:(q + 1) * 512],
                      reads=(("s", gi),), writes=(stt,))
                A("dve", nc.vector.bn_aggr, st[:, gi, 24:26], st[:, gi, 0:24], reads=(stt,), writes=(stt,))
                A("pool", nc.gpsimd.tensor_scalar, st[:, gi, 26:27], st[:, gi, 25:26], eps, None, ALU.add,
                  reads=(stt,), writes=(stt,))
                A("pool", nc.gpsimd.tensor_tensor, st[:, gi, 27:28], st[:, gi, 26:27], eps_t[:, 3:4], ALU.pow,
                  reads=(stt, ("eps",)), writes=(stt,))
                A("dve", nc.vector.tensor_scalar, s_tm[:, gi, :], s_tm[:, gi, :], st[:, gi, 24:25], st[:, gi, 27:28],
                  ALU.subtract, ALU.mult, reads=(stt, ("s", gi)), writes=(("s", gi),))
                A("dve", nc.vector.tensor_tensor, s_tm[:, gi, :], s_tm[:, gi, :], lnt[:, 0, :], ALU.mult,
                  reads=(("s", gi), ("lnt", 0)), writes=(("s", gi),))
                A("dve", nc.vector.tensor_tensor, s_tm[:, gi, :], s_tm[:, gi, :], lnt[:, 1, :], ALU.add,
                  reads=(("s", gi), ("lnt", 1)), writes=(("s", gi),))

        eps_t = top.enter_context(nc.sbuf_tensor("eps_t", [128, 4], F32))
        A("pool", g.memset, eps_t[:, 0:1], EPS_LN, writes=(("eps",),))
        A("pool", g.memset, eps_t[:, 1:2], EPS_H, reads=(("eps",),), writes=(("eps",),))
        A("pool", g.memset, eps_t[:, 2:3], 1.0, reads=(("eps",),), writes=(("eps",),))
        A("pool", g.memset, eps_t[:, 3:4], -0.5, reads=(("eps",),), writes=(("eps",),))
        sch.barrier()

        NT = NTOK // T

        with ExitStack() as ph:
            sb = lambda name, shape, dt: ph.enter_context(nc.sbuf_tensor(name, list(shape), dt))
            xT = sb("xT", [128, KC, T], BF16)
            gT = sb("gT", [128, NJ, T], BF16)
            s_tm = sb("s_tm", [128, NG, D], F32)
            NSLOT = 4
            wr = sb("wr", [128, NSLOT, KC, 512], BF16)
            ring = Ring(NSLOT)
            lnt = sb("lnt", [128, 2, D], F32)
            st = sb("st", [128, NG, 32], F32)
            xb = sb("xb", [128, 2, D], BF16)
            xbr = Ring(2)
            tmp = sb("tmp", [128, 2, 2, T], F32)
            tmpr = Ring(2)
            stF = sb("stF", [128, 2, 4, T], BF16)
            stFr = Ring(2)
            stT = stF
            stTr = stFr
            stG = sb("stG", [8, T], F32)
            wg = sb("wg", [128, KC, 8], BF16)
            A("pool", nc.gpsimd.dma_start, out=wg[:, :, :],
              in_=w["w_in"][:, C_GT:C_GT + 8].rearrange("(k p) n -> p k n", p=128), writes=(("wg",),), dma=True)

            if SPLIT:
                flg = sb("flg_sb", [128, 2], F32)
                A("sp", nc.sync.dma_start, out=flg[:, :], in_=flg_d, writes=("flg",), dma=True)
            for ti in range(S // T):
                t0 = ti * T
                pre = t0 < P
                o0 = t0 - P
                last_pre = pre and (t0 + T == P)
                xsrc, xo = (xp_d, t0) if pre else (x_d, o0)
                for gi in range(NG):
                    A("sp", nc.sync.dma_start, out=s_tm[:, gi, :], in_=xsrc[xo + gi * 128:xo + (gi + 1) * 128, :],
                      writes=(("s", gi),), dma=True)
                to_fm(s_tm, xT, xb, xbr, (4, 6))
                ffn(xT, gT, s_tm, wr, ring, w["ffn1_w1"], w["ffn1_w3"], w["ffn1_w2"], tmp, tmpr)
                layernorm(s_tm, lnt, "ln1_g", "ln1_b", st, EPS_LN)
                if not pre:
                    for gi in range(NG):
                        A("sp", nc.sync.dma_start, out=X1[o0 + gi * 128:o0 + (gi + 1) * 128, :], in_=s_tm[:, gi, :],
                          reads=(("s", gi),), dma=True)
                to_fm(s_tm, xT, xb, xbr, (4, 6))
                win = w["w_in"]
                fm_pieces = []
                kq = "flag" if pre else "copy"
                if (not pre) or last_pre:
                    for pi in range(2):
                        fm_pieces.append((C_MQ + pi * 512, kq, ("F", "QM", pi * 512)))
                for pi in range(2):
                    fm_pieces.append((C_MK + pi * 512, kq, ("F", "KM", pi * 512)))
                if not pre:
                    for pi in range(2):
                        fm_pieces.append((C_SQ + pi * 512, "copy", ("F", "QS", pi * 512)))
                for pi in range(2):
                    fm_pieces.append((C_SK + pi * 512, "copy", ("F", "KS", pi * 512)))
                if not pre:
                    for pi in range(4):
                        fm_pieces.append((C_GA + pi * 512, "sig", ("GA", pi * 512)))
                    for pi in range(4):
                        fm_pieces.append((C_GB + pi * 512, "sig", ("GB", pi * 512)))
                bi = 0
                for (c0, kind, dest) in fm_pieces:
                    sl = ring.next()
                    wload(wr, sl, win, 0, KC, c0, 512)
                    k = stFr.next()
                    for cc in range(4):
                        b = bi % 4
                        bi += 1
                        for kc in range(KC):
                            A("pe", nc.tensor.matmul, PS[b], lhsT=wr[:, sl, kc, cc * 128:(cc + 1) * 128], rhs=xT[:, kc, :],
                              start=(kc == 0), stop=(kc == KC - 1), reads=WT[sl] + XT_ALL, writes=(pstok(b),))
                        if kind == "sig":
                            A("act", nc.scalar.activation, out=stF[:, k, cc, :], in_=PS[b], func=AF.Sigmoid,
                              reads=(pstok(b),), writes=(("stF", k, cc),))
                        elif kind == "flag":
                            A("act", nc.scalar.activation, out=stF[:, k, cc, :], in_=PS[b], func=AF.Copy, scale=flg[:, 0:1],
                              reads=(pstok(b), "flg"), writes=(("stF", k, cc),))
                        else:
                            A("act", nc.scalar.copy, stF[:, k, cc, :], PS[b], reads=(pstok(b),), writes=(("stF", k, cc),))
                    rd = tuple(("stF", k, cc) for cc in range(4))
                    if dest[0] == "F":
                        region, off = dest[1], dest[2]
                        if region in ("QM", "KM"):
                            per = HM * 256
                            base = cfg.QM0 if region == "QM" else cfg.KM0
                        else:
                            per = HS * 128
                            base = cfg.QS0 if region == "QS" else cfg.KS0
                        rank, loc = off // per, off % per
                        row0 = rank * cfg.RF + base + loc
                        dst = EXF[row0:row0 + 512, t0:t0 + T].rearrange("(c p) t -> p c t", p=128)
                    else:
                        dstT = GA if dest[0] == "GA" else GB
                        dst = dstT[dest[1]:dest[1] + 512, o0:o0 + T].rearrange("(c p) t -> p c t", p=128)
                    A("sp", nc.sync.dma_start, out=dst, in_=stF[:, k, :, :], reads=rd, dma=True)
                for kc in range(KC):
                    A("pe", nc.tensor.matmul, PS[0][0:8, :], lhsT=wg[:, kc, :], rhs=xT[:, kc, :],
                      start=(kc == 0), stop=(kc == KC - 1), reads=(("wg",),) + XT_ALL, writes=(pstok(0),))
                A("act", nc.scalar.copy, stG[:, :], PS[0][0:8, :], reads=(pstok(0),), writes=(("stG",),))
                if pre:
                    A("dve", nc.vector.tensor_scalar, stG[0:4, :], stG[0:4, :], flg[0:4, 0:1], flg[0:4, 1:2], ALU.mult, ALU.add,
                      reads=(("stG",), "flg"), writes=(("stG",),))
                for gsel in range(2):
                    for rk in range(R):
                        A("sp", nc.sync.dma_start,
                          out=EXG[rk * 2 * HM + gsel * HM: rk * 2 * HM + gsel * HM + HM, t0:t0 + T],
                          in_=stG[gsel * 4 + rk * HM: gsel * 4 + rk * HM + HM, :], reads=(("stG",),), dma=True)
                tm_pieces = []
                for pi in range(2):
                    tm_pieces.append((C_MV + pi * 512, kq, cfg.VM0, HM * 256, pi * 512))
                if not pre:
                    for pi in range(2):
                        tm_pieces.append((C_MO + pi * 512, "sig", cfg.OM0, HM * 256, pi * 512))
                for pi in range(2):
                    tm_pieces.append((C_SV + pi * 512, kq, cfg.VS0, HS * 128, pi * 512))
                for (c0, kind, base, per, off) in tm_pieces:
                    sl = ring.next()
                    wload(wr, sl, win, 0, KC, c0, 512)
                    k = stTr.next()
                    for gi in range(NG):
                        b = 4 + gi
                        for kc in range(KC):
                            A("pe", nc.tensor.matmul, PS[b], lhsT=xT[:, kc, gi * 128:(gi + 1) * 128], rhs=wr[:, sl, kc, :],
                              start=(kc == 0), stop=(kc == KC - 1), reads=WT[sl] + XT_ALL, writes=(pstok(b),))
                        if kind == "sig":
                            A("act", nc.scalar.activation, out=stT[:, k, gi, :], in_=PS[b], func=AF.Sigmoid,
                              reads=(pstok(b),), writes=(("stF", k, gi),))
                        elif kind == "flag":
                            A("act", nc.scalar.activation, out=stT[:, k, gi, :], in_=PS[b], func=AF.Copy, scale=flg[:, 0:1],
                              reads=(pstok(b), "flg"), writes=(("stF", k, gi),))
                        else:
                            A("act", nc.scalar.copy, stT[:, k, gi, :], PS[b], reads=(pstok(b),), writes=(("stF", k, gi),))
                    rank, loc = off // per, off % per
                    dst = EXT[rank * NTOK + t0: rank * NTOK + t0 + T, base + loc: base + loc + 512].rearrange(
                        "(g p) c -> p g c", p=128)
                    A("sp", nc.sync.dma_start, out=dst, in_=stT[:, k, :, :],
                      reads=tuple(("stF", k, gi) for gi in range(NG)), dma=True)
            sch.barrier()

        phase2(nc, sch, cfg, PS, PSB, dict(ident_f=ident_f, ident_b=ident_b, tri_s=tri_s, maskM=maskM, ones_f=ones_f,
                                            tri_b=tri_b, tric_b=tric_b, cm64=cm64, dmask=dmask, eps_t=eps_t),
               EXF_o, EXT_o, EXG_o, EY, GSCR, bg_d, conv_d, norm_d)

        with ExitStack() as ph:
            sb = lambda name, shape, dt: ph.enter_context(nc.sbuf_tensor(name, list(shape), dt))
            xT = sb("xT3", [128, KC, T], BF16)
            gT = sb("gT3", [128, NJ, T], BF16)
            s_tm = sb("s_tm3", [128, NG, D], F32)
            NSLOT = 4
            wr = sb("wr3", [128, NSLOT, KC, 512], BF16)
            ring = Ring(NSLOT)
            lnt = sb("lnt3", [128, 2, D], F32)
            st = sb("st3", [128, NG, 32], F32)
            xb = sb("xb3", [128, 2, D], BF16)
            xbr = Ring(2)
            tmp = sb("tmp3", [128, 2, 2, T], F32)
            tmpr = Ring(2)
            wple = sb("wple", [128, 2, 2, 512], BF16)
            wpler = Ring(2)
            pbf = sb("pbf", [128, NG, DPLE], BF16)
            pT = sb("pT", [128, 2, T], BF16)
            yT = gT[:, 0:16, :]
            mT = gT[:, 16:32, :]
            for ti in range(NOWN // T):
                t0 = ti * T
                for rk in range(R):
                    r0 = rk * cfg.YR
                    A("sp", nc.sync.dma_start, out=yT[:, rk * HM * 2:(rk + 1) * HM * 2, :],
                      in_=EY_o[r0:r0 + HM * 256, t0:t0 + T].rearrange("(c p) t -> p c t", p=128),
                      writes=tuple(("gT", rk * HM * 2 + c) for c in range(HM * 2)), dma=True)
                    A("sp", nc.sync.dma_start, out=yT[:, 8 + rk * HS:8 + (rk + 1) * HS, :],
                      in_=EY_o[r0 + HM * 256:r0 + HM * 256 + HS * 128, t0:t0 + T].rearrange("(c p) t -> p c t", p=128),
                      writes=tuple(("gT", 8 + rk * HS + c) for c in range(HS)), dma=True)
                for gi in range(NG):
                    A("sp", nc.sync.dma_start, out=s_tm[:, gi, :], in_=X1[t0 + gi * 128:t0 + (gi + 1) * 128, :],
                      writes=(("s", gi),), dma=True)
                for piece in range(4):
                    sl = ring.next()
                    wload(wr, sl, w["w_up_m"], 0, 8, piece * 512, 512, k0=0)
                    wload(wr, sl, w["w_up_sb"], 0, 8, piece * 512, 512, k0=8)
                    A("sp", nc.sync.dma_start, out=gT[:, 32:36, :],
                      in_=GA[piece * 512:(piece + 1) * 512, t0:t0 + T].rearrange("(c p) t -> p c t", p=128),
                      writes=tuple(("gT", 32 + c) for c in range(4)), dma=True)
                    A("sp", nc.sync.dma_start, out=gT[:, 36:40, :],
                      in_=GB[piece * 512:(piece + 1) * 512, t0:t0 + T].rearrange("(c p) t -> p c t", p=128),
                      writes=tuple(("gT", 36 + c) for c in range(4)), dma=True)
                    for ff in range(4):
                        f = piece * 4 + ff
                        bA, bB = (0, 1) if f % 2 == 0 else (2, 3)
                        for kc in range(8):
                            A("pe", nc.tensor.matmul, PS[bA], lhsT=wr[:, sl, kc, ff * 128:(ff + 1) * 128], rhs=yT[:, kc, :],
                              start=(kc == 0), stop=(kc == 7), reads=WT[sl] + (("gT", kc),), writes=(pstok(bA),))
                        for kc in range(8, 16):
                            A("pe", nc.tensor.matmul, PS[bB], lhsT=wr[:, sl, kc, ff * 128:(ff + 1) * 128], rhs=yT[:, kc, :],
                              start=(kc == 8), stop=(kc == 15), reads=WT[sl] + (("gT", kc),), writes=(pstok(bB),))
                        k = tmpr.next()
                        A("dve", nc.vector.tensor_tensor, tmp[:, k, 0, :], PS[bA], gT[:, 32 + ff, :], ALU.mult,
                          reads=(pstok(bA), ("gT", 32 + ff)), writes=(("tmp", k, 0),))
                        A("dve", nc.vector.tensor_tensor, tmp[:, k, 1, :], PS[bB], gT[:, 36 + ff, :], ALU.mult,
                          reads=(pstok(bB), ("gT", 36 + ff)), writes=(("tmp", k, 1),))
                        A("dve", nc.vector.tensor_tensor, mT[:, f, :], tmp[:, k, 0, :], tmp[:, k, 1, :], ALU.add,
                          reads=(("tmp", k, 0), ("tmp", k, 1)), writes=(("gT", 16 + f),))
                for slab in range(4):
                    sl = ring.next()
                    wload(wr, sl, w["w_out"], 0, KC, slab * 512, 512)
                    for gi in range(NG):
                        b = 4 + gi
                        for kc in range(KC):
                            A("pe", nc.tensor.matmul, PS[b], lhsT=mT[:, kc, gi * 128:(gi + 1) * 128], rhs=wr[:, sl, kc, :],
                              start=(kc == 0), stop=(kc == KC - 1), reads=WT[sl] + (("gT", 16 + kc),), writes=(pstok(b),))
                        A("dve", nc.vector.scalar_tensor_tensor, out=s_tm[:, gi, slab * 512:(slab + 1) * 512],
                          in0=PS[b], scalar=INVA, in1=s_tm[:, gi, slab * 512:(slab + 1) * 512],
                          op0=ALU.mult, op1=ALU.add, reads=(pstok(b), ("s", gi)), writes=(("s", gi),))
                layernorm(s_tm, lnt, "ln2_g", "ln2_b", st, EPS_LN)
                to_fm(s_tm, xT, xb, xbr, (4, 6))
                ffn(xT, gT, s_tm, wr, ring, w["ffn2_w1"], w["ffn2_w3"], w["ffn2_w2"], tmp, tmpr)
                layernorm(s_tm, lnt, "ln3_g", "ln3_b", st, EPS_LN)
                to_fm(s_tm, xT, xb, xbr, (4, 6))
                for gi in range(NG):
                    A("pool", nc.gpsimd.dma_start, out=pbf[:, gi, :], in_=p_d[t0 + gi * 128:t0 + (gi + 1) * 128, :],
                      writes=(("pbf", gi),), dma=True)
                    pb = PSB[6 + gi % 2]
                    for kc in range(2):
                        A("pe", nc.tensor.transpose, pb[:, kc * 128:(kc + 1) * 128], pbf[:, gi, kc * 128:(kc + 1) * 128],
                          ident_b, reads=(("pbf", gi), "cst"), writes=(pstok(6 + gi % 2),))
                    A("act", nc.scalar.copy, pT[:, :, gi * 128:(gi + 1) * 128],
                      pb[:, 0:256].rearrange("p (k t) -> p k t", t=128), reads=(pstok(6 + gi % 2),), writes=(("pT", gi),))
                PT_ALL = tuple(("pT", gi) for gi in range(NG))
                cnt = 0
                for slab in range(4):
                    sl = ring.next()
                    wload(wr, sl, w["w_ple_gate"], 0, KC, slab * 512, 512)
                    kw_ = wpler.next()
                    A("pool", nc.gpsimd.dma_start, out=wple[:, kw_, :, :],
                      in_=w["w_ple_proj"][:, slab * 512:(slab + 1) * 512].rearrange("(k p) n -> p k n", p=128),
                      writes=(("wple", kw_),), dma=True)
                    for gi in range(NG):
                        bG, bP = (0, 1) if cnt % 2 == 0 else (2, 3)
                        cnt += 1
                        for kc in range(KC):
                            A("pe", nc.tensor.matmul, PS[bG], lhsT=xT[:, kc, gi * 128:(gi + 1) * 128], rhs=wr[:, sl, kc, :],
                              start=(kc == 0), stop=(kc == KC - 1), reads=WT[sl] + XT_ALL, writes=(pstok(bG),))
                        for kc in range(2):
                            A("pe", nc.tensor.matmul, PS[bP], lhsT=pT[:, kc, gi * 128:(gi + 1) * 128],
                              rhs=wple[:, kw_, kc, :],
                              start=(kc == 0), stop=(kc == 1), reads=(("wple", kw_),) + PT_ALL, writes=(pstok(bP),))
                        k = tmpr.next()
                        A("act", nc.scalar.activation, out=tmp[:, k, 0, :], in_=PS[bG], func=AF.Sigmoid,
                          reads=(pstok(bG),), writes=(("tmp", k, 0),))
                        A("dve", nc.vector.tensor_tensor, tmp[:, k, 1, :], PS[bP], tmp[:, k, 0, :], ALU.mult,
                          reads=(pstok(bP), ("tmp", k, 0)), writes=(("tmp", k, 1),))
                        A("dve", nc.vector.tensor_tensor, s_tm[:, gi, slab * 512:(slab + 1) * 512],
                          s_tm[:, gi, slab * 512:(slab + 1) * 512], tmp[:, k, 1, :], ALU.add,
                          reads=(("tmp", k, 1), ("s", gi)), writes=(("s", gi),))
                for gi in range(NG):
                    A("sp", nc.sync.dma_start, out=out_d[t0 + gi * 128:t0 + (gi + 1) * 128, :], in_=s_tm[:, gi, :],
                      reads=(("s", gi),), dma=True)
            sch.barrier()
    return nc


def phase2(nc, sch, cfg, PS, PSB, K, EXF_o, EXT_o, EXG_o, EY, GSCR, bg_d, conv_d, norm_d):
    sb_attention(nc, sch, cfg, PS, K, EXF_o, EXT_o, EY)
    sch.barrier()
    mlstm(nc, sch, cfg, PS, PSB, K, EXF_o, EXT_o, EXG_o, EY, GSCR, bg_d, conv_d, norm_d)
    sch.barrier()


def sb_attention(nc, sch, cfg, PS, K, EXF_o, EXT_o, EY):
    A = sch.add
    S, R, NTOK, HM, HS = cfg.S, cfg.R, cfg.NTOK, cfg.HM, cfg.HS
    NB, NQ = S // 128, S // 512
    tri_b, tric_b, dmask = K["tri_b"], K["tric_b"], K["dmask"]
    with ExitStack() as ph:
        sb = lambda name, shape, dt: ph.enter_context(nc.sbuf_tensor(name, list(shape), dt))
        QT = sb("sbQT", [128, 2, S], BF16)
        KT = sb("sbKT", [128, 2, S], BF16)
        V = sb("sbV", [128, 2, NB, 128], BF16)
        e_t = sb("sb_e", [128, 2, 512], F32)
        sp32 = sb("sb_sp32", [128, 2, 512], F32)
        sp16 = sb("sb_sp16", [128, 4, 512], BF16)
        a_t = sb("sb_a", [128, 2, 512], F32)
        b_t = sb("sb_b", [128, 2, 512], F32)
        att = sb("sb_att", [128, 3, 512], BF16)
        yst = sb("sb_y", [128, 2, 512], BF16)
        VB = 16 if NTOK // 128 >= 16 else NTOK // 128
        qk_tok = {}
        v_tok = {}

        def load_head(hl):
            hr = hl % 2
            qt, vt = [], []
            for i in range(R):
                r0 = i * cfg.RF + cfg.QS0 + hl * 128
                A("sp", nc.sync.dma_start, out=QT[:, hr, i * NTOK + cfg.P:(i + 1) * NTOK], in_=EXF_o[r0:r0 + 128, cfg.P:NTOK],
                  writes=(("sQ", hr, i),), dma=True)
                r0 = i * cfg.RF + cfg.KS0 + hl * 128
                A("sp", nc.sync.dma_start, out=KT[:, hr, i * NTOK:(i + 1) * NTOK], in_=EXF_o[r0:r0 + 128, 0:NTOK],
                  writes=(("sK", hr, i),), dma=True)
                qt += [("sQ", hr, i), ("sK", hr, i)]
                for b0 in range(0, NTOK // 128, VB):
                    src = EXT_o[i * NTOK + b0 * 128:i * NTOK + (b0 + VB) * 128,
                                cfg.VS0 + hl * 128:cfg.VS0 + (hl + 1) * 128].rearrange("(b s) d -> s b d", s=128)
                    gb0 = i * (NTOK // 128) + b0
                    A("sp", nc.sync.dma_start, out=V[:, hr, gb0:gb0 + VB, :], in_=src,
                      writes=(("sV", hr, gb0),), dma=True)
                    vt.append(("sV", hr, gb0))
            qk_tok[hl] = tuple(qt)
            v_tok[hl] = tuple(vt)

        ZB = (0, 1, 3)
        steps = []
        for hl in range(HS):
            for Q in range(cfg.P // 512, NQ):
                for kb in range(4 * Q + 3, -1, -1):
                    steps.append((hl, Q, kb))

        NS = len(steps)

        def info(n):
            hl, Q, kb = steps[n]
            return hl, hl % 2, Q, kb, kb - 4 * Q, kb == 4 * Q + 3, kb == 0, hl * NQ + Q

        def op_qk(n):
            hl, hr, Q, kb, i, first, last, sweep = info(n)
            zb = ZB[n % 3]
            A("pe", nc.tensor.matmul, PS[zb], lhsT=KT[:, hr, kb * 128:(kb + 1) * 128], rhs=QT[:, hr, Q * 512:(Q + 1) * 512],
              start=True, stop=True, reads=qk_tok[hl], writes=(("ps", zb),))

        def op_esp(n):
            zb, k2 = ZB[n % 3], n % 2
            A("act", nc.scalar.activation, out=e_t[:, k2, :], in_=PS[zb], func=AF.Exp, scale=SB_SCALE,
              reads=(("ps", zb),), writes=(("e", k2),))
            A("act", nc.scalar.activation, out=sp32[:, k2, :], in_=e_t[:, k2, :], func=AF.Ln, bias=K["eps_t"][:, 2:3], scale=1.0,
              reads=(("e", k2),), writes=(("sp32", k2),))

        def op_cast_a(n):
            hl, hr, Q, kb, i, first, last, sweep = info(n)
            zb, k2, k4 = ZB[n % 3], n % 2, n % 4
            if i >= 0:
                A("dve", nc.vector.tensor_tensor, sp16[:, k4, :], sp32[:, k2, :], dmask[:, i, :], ALU.mult,
                  reads=(("sp32", k2),), writes=(("sp16", k4),))
            else:
                A("dve", nc.vector.tensor_copy, sp16[:, k4, :], sp32[:, k2, :],
                  reads=(("sp32", k2),), writes=(("sp16", k4),))
            A("dve", nc.vector.scalar_tensor_tensor, out=a_t[:, k2, :], in0=PS[zb], scalar=SB_SCALE, in1=sp32[:, k2, :],
              op0=ALU.mult, op1=ALU.subtract, reads=(("ps", zb), ("sp32", k2)), writes=(("a", k2),))

        def op_p(n):
            hl, hr, Q, kb, i, first, last, sweep = info(n)
            k4 = n % 4
            if not first:
                A("pe", nc.tensor.matmul, PS[2], lhsT=tric_b, rhs=sp16[:, (n - 1) % 4, :], start=False, stop=False,
                  skip_group_check=True, reads=(("sp16", (n - 1) % 4),), writes=(("ps", 2),))
            A("pe", nc.tensor.matmul, PS[2], lhsT=tri_b, rhs=sp16[:, k4, :], start=first, stop=True,
              skip_group_check=True, reads=(("sp16", k4),), writes=(("ps", 2),))

        def op_b(n):
            k2 = n % 2
            A("dve", nc.vector.tensor_tensor, b_t[:, k2, :], a_t[:, k2, :], PS[2], ALU.subtract,
              reads=(("a", k2), ("ps", 2)), writes=(("b", k2),))

        def op_att(n):
            k2, k3 = n % 2, n % 3
            A("act", nc.scalar.activation, out=att[:, k3, :], in_=b_t[:, k2, :], func=AF.Exp,
              reads=(("b", k2),), writes=(("att", k3),))

        def op_attmask(n):
            hl, hr, Q, kb, i, first, last, sweep = info(n)
            k3 = n % 3
            if i >= 0:
                A("dve", nc.vector.tensor_tensor, att[:, k3, :], att[:, k3, :], dmask[:, i, :], ALU.mult,
                  reads=(("att", k3),), writes=(("att", k3),))

        def op_av(n):
            hl, hr, Q, kb, i, first, last, sweep = info(n)
            k3 = n % 3
            ob = 4 + sweep % 2
            A("pe", nc.tensor.matmul, PS[ob], lhsT=V[:, hr, kb, :], rhs=att[:, k3, :], start=first, stop=last,
              reads=(("att", k3),) + v_tok[hl], writes=(("ps", ob),))
            if last:
                yk = sweep % 2
                A("act", nc.scalar.copy, yst[:, yk, :], PS[ob], reads=(("ps", ob),), writes=(("yst", yk),))
                t = Q * 512 - cfg.P
                j, tl = 0, t
                row0 = j * cfg.YR + HM * 256 + hl * 128
                A("sp", nc.sync.dma_start, out=EY[row0:row0 + 128, tl:tl + 512], in_=yst[:, yk, :],
                  reads=(("yst", yk),), dma=True)

        loaded = set()
        for n in range(NS + 2):
            if n < NS:
                hl = steps[n][0]
                if hl not in loaded:
                    load_head(hl)
                    loaded.add(hl)
                op_qk(n)
            if 0 <= n - 2 < NS:
                op_att(n - 2)
            if n < NS:
                op_esp(n)
            if 0 <= n - 1 < NS:
                op_p(n - 1)
                op_b(n - 1)
            if 0 <= n - 2 < NS:
                op_attmask(n - 2)
                op_av(n - 2)
            if n < NS:
                op_cast_a(n)


def mlstm(nc, sch, cfg, PS, PSB, K, EXF_o, EXT_o, EXG_o, EY, GSCR, bg_d, conv_d, norm_d):
    A = sch.add
    S, R, NTOK, HM = cfg.S, cfg.R, cfg.NTOK, cfg.HM
    C = S // 64
    Cr = NTOK // 64
    SUB = min(2048, cfg.P) if cfg.split else min(2048, NTOK)
    NSUB = S // SUB
    CS = SUB // 64
    ident_f, ident_b, tri_s, maskM, ones_f, cm64, eps_t = (K["ident_f"], K["ident_b"], K["tri_s"], K["maskM"],
                                                           K["ones_f"], K["cm64"], K["eps_t"])
    with ExitStack() as ph:
        sb = lambda name, shape, dt: ph.enter_context(nc.sbuf_tensor(name, list(shape), dt))
        gt = sb("m_gt", [128, 12, 64], F32)
        IT, FT, SP, NB_, BT_, PMB, AT, IST, EMT, WT_, ONES, TMP = [gt[0:C, i, :] for i in range(12)]
        gsm = sb("m_gsm", [128, 16], F32)
        diagX = sb("m_diagX", [128, 128], F32)
        tmpM = sb("m_tmpM", [128, 128], F32)
        A_bcm = sb("m_Abc", [64, S], F32)
        IS_bc = sb("m_ISbc", [128, S], F32)
        BT = sb("m_BT", [64, 128], F32)
        wT = sb("m_wT", [64, 128], F32)
        emtT = sb("m_emtT", [64, 128], F32)
        dec_bc = sb("m_decbc", [128, 128], F32)
        bgb = sb("m_bgb", [128, 2 * HM], F32)
        convw = sb("m_convw", [128, 2 * HM * 2 * 4], F32)
        normg = sb("m_normg", [64, HM * 256], F32)
        rawp = sb("m_rawp", [128, 2, 4 + SUB], BF16)
        rawr = [0]
        acc = sb("m_acc", [128, SUB], F32)
        sig = sb("m_sig", [128, SUB], F32)
        QTb = sb("m_QTb", [128, 2, SUB], BF16)
        KTb = sb("m_KTb", [128, 2, SUB], BF16)
        QsT = sb("m_QsT", [128, 2, SUB], BF16)
        Vp = sb("m_Vp", [64, CS, 258], BF16)
        OMb = sb("m_OMb", [64, CS, 256], BF16)
        CTf = sb("m_CTf", [128, 2, 257], F32)
        CTb = sb("m_CTb", [128, 2, 258], BF16)
        Dm = sb("m_Dm", [64, 2, 64], F32)
        scT = sb("m_scT", [64, 2, 64], BF16)
        Kw = sb("m_Kw", [64, 2, 256], BF16)
        hraw = sb("m_hraw", [64, 4, 256], F32)
        sm = sb("m_sm", [64, 4, 16], F32)
        yt = sb("m_y", [64, 4, 256], BF16)
        ystg = sb("m_ystg", [128, 2, 2, 512], BF16)

        A("sp", nc.sync.dma_start, out=bgb[:, :], in_=bg_d.partition_broadcast(128), writes=("bgb",), dma=True)
        A("sp", nc.sync.dma_start, out=convw[:, :], in_=conv_d, writes=("convw",), dma=True)
        A("sp", nc.sync.dma_start, out=normg[:, :], in_=norm_d.partition_broadcast(64), writes=("normg",), dma=True)
        A("pool", nc.gpsimd.memset, gt[:, 10, :], 1.0, writes=("ones64",))
        A("pool", nc.gpsimd.memset, Vp[:, :, 256:258], 1.0, writes=("vp1",))

        for hl in range(HM):
            for i in range(R):
                A("sp", nc.sync.dma_start, out=gt[i * Cr:(i + 1) * Cr, 0, :],
                  in_=EXG_o[i * 2 * HM + hl:i * 2 * HM + hl + 1, 0:NTOK].rearrange("o (c t) -> (o c) t", t=64),
                  writes=(("IT", i),), dma=True)
                A("sp", nc.sync.dma_start, out=gt[i * Cr:(i + 1) * Cr, 1, :],
                  in_=EXG_o[i * 2 * HM + HM + hl:i * 2 * HM + HM + hl + 1, 0:NTOK].rearrange("o (c t) -> (o c) t", t=64),
                  writes=(("FT", i),), dma=True)
            ITt = tuple(("IT", i) for i in range(R))
            FTt = tuple(("FT", i) for i in range(R))
            V_ = nc.vector
            A("dve", V_.tensor_scalar, IT, IT, bgb[0:C, hl:hl + 1], None, ALU.add, reads=ITt + ("bgb",), writes=ITt)
            A("dve", V_.tensor_scalar, FT, FT, bgb[0:C, HM + hl:HM + hl + 1], None, ALU.add, reads=FTt + ("bgb",), writes=FTt)
            A("act", nc.scalar.activation, out=SP, in_=FT, func=AF.Exp, scale=-1.0, reads=FTt, writes=("SP",))
            A("act", nc.scalar.activation, out=SP, in_=SP, func=AF.Ln, bias=eps_t[0:C, 2:3], scale=1.0, reads=("SP",), writes=("SP",))
            A("dve", V_.tensor_tensor_scan, NB_, ONES, SP, 0.0, ALU.mult, ALU.add, reads=("SP", "ones64"), writes=("NB",))
            A("dve", V_.tensor_tensor, BT_, IT, NB_, ALU.add, reads=ITt + ("NB",), writes=("B",))
            A("dve", V_.tensor_tensor_scan, PMB, ONES, BT_, NEG, ALU.mult, ALU.max, reads=("B", "ones64"), writes=("PMB",))
            G_ = ("gsm",)
            A("dve", V_.tensor_scalar, gsm[0:C, 0:1], NB_[:, 63:64], -1.0, None, ALU.mult, reads=("NB",), writes=G_)
            A("dve", V_.memset, gsm[0:C, 1:2], 0.0, reads=G_, writes=G_)
            A("pe", nc.tensor.matmul, PS[6][0:C, 0:2], lhsT=tri_s[0:C, 0:C], rhs=gsm[0:C, 0:2], start=True, stop=True,
              reads=G_, writes=(("ps", 6),))
            A("act", nc.scalar.copy, gsm[0:C, 1:2], PS[6][0:C, 0:1], reads=(("ps", 6),) + G_, writes=G_)
            A("dve", V_.tensor_tensor, gsm[0:C, 2:3], PMB[:, 63:64], gsm[0:C, 1:2], ALU.subtract, reads=("PMB",) + G_, writes=G_)
            A("dve", V_.tensor_scalar, diagX[0:C, 0:C], ident_f[0:C, 0:C], gsm[0:C, 2:3], None, ALU.mult, reads=G_, writes=("diagX",))
            A("pe", nc.tensor.matmul, PS[7][0:C, 0:C], lhsT=ones_f[0:C, 0:C], rhs=diagX[0:C, 0:C], start=True, stop=True,
              reads=("diagX",), writes=(("ps", 7),))
            A("dve", V_.tensor_tensor, tmpM[0:C, 0:C], PS[7][0:C, 0:C], maskM[0:C, 0:C], ALU.add, reads=(("ps", 7),), writes=("tmpM",))
            A("dve", V_.tensor_reduce, gsm[0:C, 3:4], tmpM[0:C, 0:C], AX.X, ALU.max, reads=("tmpM",) + G_, writes=G_)
            A("dve", V_.tensor_tensor, gsm[0:C, 4:5], gsm[0:C, 3:4], gsm[0:C, 1:2], ALU.add, reads=G_, writes=G_)
            A("dve", V_.tensor_tensor, gsm[0:C, 5:6], gsm[0:C, 4:5], PMB[:, 63:64], ALU.max, reads=G_ + ("PMB",), writes=G_)
            A("dve", V_.tensor_scalar, gsm[0:C, 6:7], gsm[0:C, 5:6], -1.0, None, ALU.mult, reads=G_, writes=G_)
            A("dve", V_.tensor_tensor, gsm[0:C, 8:9], gsm[0:C, 4:5], gsm[0:C, 5:6], ALU.subtract, reads=G_, writes=G_)
            A("dve", V_.tensor_scalar, AT, PMB, gsm[0:C, 4:5], -1.0, ALU.max, ALU.mult, reads=("PMB",) + G_, writes=("AT",))
            A("act", nc.scalar.activation, out=IST, in_=AT, func=AF.Exp, bias=gsm[0:C, 4:5], scale=1.0, reads=("AT",) + G_, writes=("IST",))
            A("dve", V_.tensor_tensor, TMP, AT, NB_, ALU.add, reads=("AT", "NB"), writes=("TMP",))
            A("act", nc.scalar.activation, out=EMT, in_=TMP, func=AF.Exp, reads=("TMP",), writes=("EMT",))
            A("act", nc.scalar.activation, out=WT_, in_=BT_, func=AF.Exp, bias=gsm[0:C, 6:7], scale=1.0, reads=("B",) + G_, writes=("WT",))
            A("act", nc.scalar.activation, out=gsm[0:C, 7:8], in_=gsm[0:C, 8:9], func=AF.Exp, reads=G_, writes=G_)
            A("sp", nc.sync.dma_start, out=GSCR[0:1, 0:S].rearrange("o (c t) -> (o c) t", t=64), in_=AT,
              reads=("AT",), writes=(("GS", 0),), dma=True)
            A("sp", nc.sync.dma_start, out=GSCR[1:2, 0:S].rearrange("o (c t) -> (o c) t", t=64), in_=IST,
              reads=("IST",), writes=(("GS", 1),), dma=True)
            A("sp", nc.sync.dma_start, out=GSCR[2:3, 0:C].rearrange("o c -> c o"), in_=gsm[0:C, 7:8],
              reads=G_, writes=(("GS", 2),), dma=True)
            A("sp", nc.sync.dma_start, out=A_bcm[:, :], in_=GSCR[0:1, 0:S].partition_broadcast(64),
              reads=(("GS", 0),), writes=("Abc",), dma=True)
            A("sp", nc.sync.dma_start, out=IS_bc[:, :], in_=GSCR[1:2, 0:S].partition_broadcast(128),
              reads=(("GS", 1),), writes=("ISbc",), dma=True)
            A("sp", nc.sync.dma_start, out=dec_bc[:, 0:C], in_=GSCR[2:3, 0:C].partition_broadcast(128),
              reads=(("GS", 2),), writes=("decbc",), dma=True)
            A3 = A_bcm[:, :].rearrange("p (c t) -> p c t", t=64)
            A("dve", V_.tensor_tensor, A3, A3, cm64[:, :].unsqueeze(1).broadcast_to([64, C, 64]), ALU.add,
              reads=("Abc",), writes=("Abc",))
            for (src, dst, nm) in ((BT_, BT, "BT"), (WT_, wT, "wT"), (EMT, emtT, "emtT")):
                A("pe", nc.tensor.transpose, PS[6][0:64, 0:C], src, ident_f[0:C, 0:C],
                  reads=("B", "WT", "EMT"), writes=(("ps", 6),))
                A("act", nc.scalar.copy, dst[:, 0:C], PS[6][0:64, 0:C], reads=(("ps", 6),), writes=(nm,))
            A("pool", nc.gpsimd.memset, CTf[:, :, :], 0.0, writes=(("CTf", 0), ("CTf", 1)))
            A("pool", nc.gpsimd.memset, CTb[:, :, :], 0.0, writes=("CTb",))

            for sbk in range(NSUB):
                t0 = sbk * SUB
                irank, tl = t0 // NTOK, t0 % NTOK
                so = t0 < cfg.P
                for qk, dstT, base, scl, nm in ((0, QTb, cfg.QM0, 1.0, "QTb"), (1, KTb, cfg.KM0, K_SCALE, "KTb")):
                    if so and qk == 0:
                        continue
                    for dc in range(2):
                        rr = rawr[0] % 2
                        rawr[0] += 1
                        row0 = irank * cfg.RF + base + (hl * 2 + dc) * 128
                        A("sp", nc.sync.dma_start, out=rawp[:, rr, 3:3 + SUB], in_=EXF_o[row0:row0 + 128, tl:tl + SUB],
                          writes=(("rawp", rr),), dma=True)
                        if t0 == 0:
                            A("pool", nc.gpsimd.memset, rawp[:, rr, 0:3], 0.0, writes=(("rawh", rr),))
                        elif tl >= 3:
                            A("sp", nc.sync.dma_start, out=rawp[:, rr, 0:3], in_=EXF_o[row0:row0 + 128, tl - 3:tl],
                              writes=(("rawh", rr),), dma=True)
                        else:
                            rowp = (irank - 1) * cfg.RF + base + (hl * 2 + dc) * 128
                            A("sp", nc.sync.dma_start, out=rawp[:, rr, 0:3], in_=EXF_o[rowp:rowp + 128, NTOK - 3:NTOK],
                              writes=(("rawh", rr),), dma=True)
                        wi = ((qk * HM + hl) * 2 + dc) * 4
                        RD = (("rawp", rr), ("rawh", rr), "convw")
                        A("dve", V_.tensor_scalar, acc[:, :], rawp[:, rr, 3:3 + SUB], convw[:, wi + 3:wi + 4], None, ALU.mult,
                          reads=RD, writes=("acc",))
                        for j in range(3):
                            A("dve", V_.scalar_tensor_tensor, out=acc[:, :], in0=rawp[:, rr, j:j + SUB],
                              scalar=convw[:, wi + j:wi + j + 1], in1=acc[:, :], op0=ALU.mult, op1=ALU.add,
                              reads=RD + ("acc",), writes=("acc",))
                        A("act", nc.scalar.activation, out=sig[:, :], in_=acc[:, :], func=AF.Sigmoid, reads=("acc",), writes=("sig",))
                        A("dve", V_.scalar_tensor_tensor, out=dstT[:, dc, :], in0=acc[:, :], scalar=scl, in1=sig[:, :],
                          op0=ALU.mult, op1=ALU.mult, reads=("acc", "sig"), writes=((nm, dc),))
                for dc in range(2):
                    if so:
                        continue
                    A("dve", V_.tensor_tensor, QsT[:, dc, :], QTb[:, dc, :], IS_bc[:, t0:t0 + SUB], ALU.mult,
                      reads=(("QTb", dc), "ISbc"), writes=(("QsT", dc),))
                r0 = irank * NTOK + tl
                A("sp", nc.sync.dma_start, out=Vp[:, :, 0:256],
                  in_=EXT_o[r0:r0 + SUB, cfg.VM0 + hl * 256:cfg.VM0 + (hl + 1) * 256].rearrange("(c s) v -> s c v", s=64),
                  writes=("Vp",), dma=True)
                if not so:
                    A("sp", nc.sync.dma_start, out=OMb[:, :, :],
                      in_=EXT_o[r0:r0 + SUB, cfg.OM0 + hl * 256:cfg.OM0 + (hl + 1) * 256].rearrange("(c s) v -> s c v", s=64),
                      writes=("OMb",), dma=True)
                def ci(cl):
                    c = sbk * CS + cl
                    return c, c % 2, c % 4, slice(cl * 64, (cl + 1) * 64)

                def pe_front(cl):
                    c, r, r4, csl = ci(cl)
                    ps_s = PS[0][0:64, 0:64]
                    for dc in range(2):
                        if so:
                            continue
                        A("pe", nc.tensor.matmul, ps_s, lhsT=KTb[:, dc, csl], rhs=QTb[:, dc, csl], start=(dc == 0), stop=(dc == 1),
                          reads=(("KTb", dc), ("QTb", dc)), writes=(("ps", 0),))
                    for dc in range(2):
                        A("pe", nc.tensor.transpose, PSB[1][0:64, dc * 128:(dc + 1) * 128], KTb[:, dc, csl], ident_b,
                          reads=(("KTb", dc),), writes=(("ps", 1),))

                def act_d(cl):
                    c, r, r4, csl = ci(cl)
                    A("act", nc.scalar.activation, out=Dm[:, r, :], in_=A_bcm[:, c * 64:(c + 1) * 64], func=AF.Exp,
                      bias=BT[:, c:c + 1], scale=1.0, reads=("Abc", "BT"), writes=(("Dm", r),))

                def dve_front(cl):
                    c, r, r4, csl = ci(cl)
                    if not so:
                        A("dve", V_.tensor_tensor, scT[:, r, :], PS[0][0:64, 0:64], Dm[:, r, :], ALU.mult,
                          reads=(("ps", 0), ("Dm", r)), writes=(("scT", r),))
                    A("dve", V_.tensor_scalar, Kw[:, r, :], PSB[1][0:64, 0:256], wT[:, c:c + 1], None, ALU.mult,
                      reads=(("ps", 1), "wT"), writes=(("Kw", r),))

                def pe_core(cl):
                    c, r, r4, csl = ci(cl)
                    ub = 4 + 2 * r
                    for dc in range(2):
                        A("pe", nc.tensor.matmul, PS[ub + dc][:, 0:257], lhsT=Kw[:, r, dc * 128:(dc + 1) * 128], rhs=Vp[:, cl, 0:257],
                          start=True, stop=True, reads=(("Kw", r), "Vp", "vp1"), writes=(("ps", ub + dc),))
                    ps_n = PS[2][0:64, 0:257]
                    if so:
                        return
                    A("pe", nc.tensor.matmul, ps_n, lhsT=scT[:, r, :], rhs=Vp[:, cl, 0:257], start=True, stop=False,
                      reads=(("scT", r), "Vp", "vp1"), writes=(("ps", 2),))
                    for dc in range(2):
                        A("pe", nc.tensor.matmul, ps_n, lhsT=QsT[:, dc, csl], rhs=CTb[:, dc, 0:257], start=False, stop=(dc == 1),
                          reads=(("QsT", dc), "CTb"), writes=(("ps", 2),))

                def dve_core(cl):
                    c, r, r4, csl = ci(cl)
                    ub = 4 + 2 * r
                    ps_n = PS[2][0:64, 0:257]
                    SMT = (("sm", r4),)
                    if so:
                        for dc in range(2):
                            A("dve", V_.scalar_tensor_tensor, out=CTf[:, dc, :], in0=CTf[:, dc, :], scalar=dec_bc[:, c:c + 1],
                              in1=PS[ub + dc][:, 0:257], op0=ALU.mult, op1=ALU.add,
                              reads=(("CTf", dc), "decbc", ("ps", ub + dc)), writes=(("CTf", dc),))
                        return
                    A("dve", V_.tensor_scalar, sm[:, r4, 6:7], ps_n[:, 256:257], -1.0, emtT[:, c:c + 1], ALU.mult, ALU.max,
                      reads=(("ps", 2), "emtT"), writes=SMT)
                    A("dve", V_.scalar_tensor_tensor, out=CTf[:, 0, :], in0=CTf[:, 0, :], scalar=dec_bc[:, c:c + 1],
                      in1=PS[ub][:, 0:257], op0=ALU.mult, op1=ALU.add,
                      reads=(("CTf", 0), "decbc", ("ps", ub)), writes=(("CTf", 0),))
                    A("dve", V_.tensor_tensor, sm[:, r4, 0:1], sm[:, r4, 6:7], ps_n[:, 256:257], ALU.max,
                      reads=(("ps", 2),) + SMT, writes=SMT)
                    A("dve", V_.scalar_tensor_tensor, out=CTf[:, 1, :], in0=CTf[:, 1, :], scalar=dec_bc[:, c:c + 1],
                      in1=PS[ub + 1][:, 0:257], op0=ALU.mult, op1=ALU.add,
                      reads=(("CTf", 1), "decbc", ("ps", ub + 1)), writes=(("CTf", 1),))
                    A("dve", V_.reciprocal, sm[:, r4, 1:2], sm[:, r4, 0:1], reads=SMT, writes=SMT)

                def act_core(cl):
                    A("act", nc.scalar.copy, CTb[:, :, 0:257], CTf[:, :, :], reads=(("CTf", 0), ("CTf", 1)), writes=("CTb",))

                def dve_h(cl):
                    c, r, r4, csl = ci(cl)
                    SMT = (("sm", r4),)
                    A("dve", V_.tensor_scalar, hraw[:, r4, :], PS[2][0:64, 0:256], sm[:, r4, 1:2], None, ALU.mult,
                      reads=(("ps", 2),) + SMT, writes=(("hraw", r4),))

                def dve_bn(cl):
                    c, r, r4, csl = ci(cl)
                    SMT = (("sm", r4),)
                    A("dve", V_.bn_stats, sm[:, r4, 8:14], hraw[:, r4, :], reads=(("hraw", r4),), writes=SMT)
                    A("dve", V_.bn_aggr, sm[:, r4, 2:4], sm[:, r4, 8:14], reads=SMT, writes=SMT)

                def act_sqrt(cl):
                    c, r, r4, csl = ci(cl)
                    SMT = (("sm", r4),)
                    A("pool", nc.gpsimd.tensor_scalar, sm[:, r4, 4:5], sm[:, r4, 3:4], EPS_H, None, ALU.add,
                      reads=SMT, writes=SMT)
                    A("pool", nc.gpsimd.tensor_tensor, sm[:, r4, 5:6], sm[:, r4, 4:5], eps_t[0:64, 3:4], ALU.pow,
                      reads=SMT, writes=SMT)

                def dve_norm(cl):
                    c, r, r4, csl = ci(cl)
                    SMT = (("sm", r4),)
                    A("dve", V_.tensor_scalar, hraw[:, r4, :], hraw[:, r4, :], sm[:, r4, 2:3], sm[:, r4, 5:6], ALU.subtract, ALU.mult,
                      reads=(("hraw", r4),) + SMT, writes=(("hraw", r4),))
                    A("pool", nc.gpsimd.tensor_tensor, hraw[:, r4, :], hraw[:, r4, :], normg[:, hl * 256:(hl + 1) * 256], ALU.mult,
                      reads=(("hraw", r4), "normg"), writes=(("hraw", r4),))
                    A("pool", nc.gpsimd.tensor_tensor, yt[:, r4, :], hraw[:, r4, :], OMb[:, cl, :], ALU.mult,
                      reads=(("hraw", r4), "OMb"), writes=(("yt", r4),))

                def tail(cl):
                    c, r, r4, csl = ci(cl)
                    yb = 3
                    for dc in range(2):
                        A("pe", nc.tensor.transpose, PSB[yb][:, dc * 64:(dc + 1) * 64], yt[:, r4, dc * 128:(dc + 1) * 128],
                          ident_b[0:64, 0:64], reads=(("yt", r4),), writes=(("ps", yb),))
                    yr = (c // 8) % 2
                    A("act", nc.scalar.copy, ystg[:, yr, :, (c % 8) * 64:(c % 8 + 1) * 64],
                      PSB[yb][:, 0:128].rearrange("p (d t) -> p d t", t=64), reads=(("ps", yb),), writes=(("ystg", yr, c % 8),))
                    if c % 8 == 7:
                        tt = (c - 7) * 64 - cfg.P
                        j, tlo = 0, tt
                        row0 = j * cfg.YR + hl * 256
                        A("sp", nc.sync.dma_start, out=EY[row0:row0 + 256, tlo:tlo + 512].rearrange("(d p) t -> p d t", p=128),
                          in_=ystg[:, yr, :, :], reads=tuple(("ystg", yr, q) for q in range(8)), dma=True)

                ok = lambda k: 0 <= k < CS
                for it in range(CS + 5):
                    if ok(it):
                        pe_front(it)
                        if not so:
                            act_d(it)
                    if ok(it - 1):
                        pe_core(it - 1)
                    if ok(it):
                        dve_front(it)
                    if ok(it - 1):
                        dve_core(it - 1)
                        act_core(it - 1)
                        if not so:
                            dve_h(it - 1)
                    if so:
                        continue
                    if ok(it - 2):
                        dve_bn(it - 2)
                        act_sqrt(it - 2)
                    if ok(it - 3):
                        dve_norm(it - 3)
                    if ok(it - 4):
                        tail(it - 4)


def make_in_maps(inputs, cfg, core_assign):
    maps = []
    R, NTOK, HM = cfg.R, cfg.NTOK, cfg.HM
    wnames = ("ffn1_w1", "ffn1_w3", "ffn1_w2", "w_in", "w_up_m", "w_up_sb", "w_out", "ffn2_w1", "ffn2_w3",
              "ffn2_w2", "w_ple_gate", "w_ple_proj")
    shared = {n: np.ascontiguousarray(np.asarray(inputs[n])[0], dtype=np.float32) for n in wnames}
    for n in ("ln1_g", "ln1_b", "ln2_g", "ln2_b", "ln3_g", "ln3_b"):
        shared[n] = np.ascontiguousarray(np.asarray(inputs[n])[0].reshape(1, D), dtype=np.float32)
    bgm = np.asarray(inputs["b_gates_m"])[0]
    convm = np.asarray(inputs["conv_m"])[0]
    normm = np.asarray(inputs["norm_m"])[0]
    x = np.asarray(inputs["x"])
    p = np.asarray(inputs["p"])[0]
    for (b, rk) in core_assign:
        m = dict(shared)
        if cfg.split:
            P, NOWN = cfg.P, cfg.NOWN
            m["x"] = np.ascontiguousarray(x[b, rk * NOWN:(rk + 1) * NOWN], dtype=np.float32)
            m["p"] = np.ascontiguousarray(p[b, rk * NOWN:(rk + 1) * NOWN], dtype=np.float32)
            m["xpre"] = np.ascontiguousarray(x[b, 0:P], dtype=np.float32)
            fl = np.zeros((128, 2), np.float32)
            fl[:, 0] = 1.0 if rk == 1 else 0.0
            fl[:, 1] = 0.0 if rk == 1 else NEG
            m["flg"] = fl
            rk = 0
        else:
            m["x"] = np.ascontiguousarray(x[b, rk * NTOK:(rk + 1) * NTOK], dtype=np.float32)
            m["p"] = np.ascontiguousarray(p[b, rk * NTOK:(rk + 1) * NTOK], dtype=np.float32)
        h0 = rk * HM
        m["bgates"] = np.ascontiguousarray(
            np.concatenate([bgm[h0:h0 + HM], bgm[NHM + h0:NHM + h0 + HM]]).reshape(1, 2 * HM), dtype=np.float32)
        cv = convm.reshape(4, 2, NHM, 2, 128)[:, :, h0:h0 + HM]
        m["convT"] = np.ascontiguousarray(cv.transpose(4, 1, 2, 3, 0).reshape(128, 2 * HM * 2 * 4), dtype=np.float32)
        m["normm"] = np.ascontiguousarray(normm[h0 * 256:(h0 + HM) * 256].reshape(1, HM * 256), dtype=np.float32)
        maps.append(m)
    return maps


S_FULL = 8192
_NC_CACHE = {}


def kernel(**inputs):
    cfg = Cfg(S_FULL, 1, split=True)
    if "nc" not in _NC_CACHE:
        _NC_CACHE["nc"] = build(cfg)
    nc = _NC_CACHE["nc"]
    B = np.asarray(inputs["x"]).shape[0]
    assign = [(c // 2, c % 2) for c in range(8)]
    maps = make_in_maps(inputs, cfg, assign)
    res = run_bass_kernel_spmd(nc, maps, core_ids=list(range(8)))
    out = np.empty((B, S_FULL, D), np.float32)
    for c, (b, h) in enumerate(assign):
        out[b, h * cfg.NOWN:(h + 1) * cfg.NOWN] = res.results[c]["out"]
    return out
```

```python
import numpy as np
from contextlib import ExitStack
import concourse.bass as bass
import concourse.mybir as mybir
from concourse.bass_utils import run_bass_kernel_spmd

F32 = mybir.dt.float32
BF16 = mybir.dt.bfloat16
ALU = mybir.AluOpType
AF = mybir.ActivationFunctionType
AX = mybir.AxisListType

D = 2048
DFF = 5632
NJ = DFF // 128
KC = D // 128
T = 512
NG = T // 128
DPLE = 256
NHM, DHM = 4, 256
NHS, DHS = 8, 128
INCOLS = 11272
C_MQ, C_MK, C_MV, C_MO, C_GT, C_SQ, C_SK, C_SV, C_GA, C_GB = 0, 1024, 2048, 3072, 4096, 4104, 5128, 6152, 7176, 9224
ALPHA = 2.0 ** 0.25
CRES = 0.5 / ALPHA
INVA = 1.0 / ALPHA
EPS_LN = 1e-5 / (ALPHA * ALPHA)
EPS_H = 1e-5
NEG = -1e30
SB_SCALE = DHS ** -0.5
K_SCALE = DHM ** -0.5
SAME_ENGINE_SYNC = True


class Op:
    __slots__ = ("eng", "fn", "args", "kw", "dma", "deps", "signal", "sem", "val")

    def __init__(self, eng, fn, args, kw, dma):
        self.eng, self.fn, self.args, self.kw, self.dma = eng, fn, args, kw, dma
        self.deps = []
        self.signal = False
        self.sem = None
        self.val = 0


class Sched:
    COMPUTE = ("pe", "act", "dve", "pool")

    def __init__(self, nc, stack):
        self.nc = nc
        self.h = {"pe": nc.tensor, "act": nc.scalar, "dve": nc.vector, "pool": nc.gpsimd, "sp": nc.sync}
        self.esem = {e: stack.enter_context(nc.semaphore("s_" + e)) for e in self.COMPUTE}
        self.ecount = {e: 0 for e in self.COMPUTE}
        self.dsems = {}
        for q, n in (("sp", 20), ("pool", 12), ("act", 4)):
            self.dsems[q] = [stack.enter_context(nc.semaphore("d_%s_%d" % (q, i))) for i in range(n)]
        self.dlast = {q: [0] * len(v) for q, v in self.dsems.items()}
        self.dnext = {q: 0 for q in self.dsems}
        self.waited = {e: {} for e in self.h}
        self.ops = []
        self.tok = {}
        self.n_inst = 0

    def add(self, eng, fn, *args, reads=(), writes=(), dma=False, **kw):
        op = Op(eng, fn, args, kw, dma)
        tok = self.tok
        deps = []
        for t in reads:
            st = tok.get(t)
            if st is None:
                st = tok[t] = [None, {}, []]
            if st[0] is not None:
                deps.append(st[0])
        for t in writes:
            st = tok.get(t)
            if st is None:
                st = tok[t] = [None, {}, []]
            if st[0] is not None:
                deps.append(st[0])
            deps.extend(st[1].values())
            deps.extend(st[2])
        for t in reads:
            st = tok[t]
            if dma:
                st[2].append(op)
            else:
                st[1][eng] = op
        for t in writes:
            st = tok[t]
            st[0] = op
            st[1] = {}
            st[2] = []
        seen = set()
        for d in deps:
            if d is op or id(d) in seen:
                continue
            seen.add(id(d))
            if (not d.dma) and (not dma) and d.eng == eng and (eng == "pe" or not SAME_ENGINE_SYNC):
                continue
            d.signal = True
            op.deps.append(d)
        self.ops.append(op)
        return op

    def _wait(self, eng, sem, val):
        w = self.waited[eng]
        if w.get(id(sem), 0) < val:
            self.h[eng].wait_ge(sem, val)
            w[id(sem)] = val
            self.n_inst += 1

    def barrier(self):
        last = {}
        for op in self.ops:
            if not op.dma:
                last[op.eng] = op
        for op in last.values():
            op.signal = True
        for op in self.ops:
            for d in op.deps:
                self._wait(op.eng, d.sem, d.val)
            if op.dma:
                q = op.eng
                k = self.dnext[q]
                self.dnext[q] = (k + 1) % len(self.dsems[q])
                sem = self.dsems[q][k]
                prev = self.dlast[q][k]
                if prev > 0:
                    self._wait(q, sem, prev)
                ins = op.fn(*op.args, **op.kw)
                ins.then_inc(sem, 16)
                self.dlast[q][k] = prev + 16
                op.sem, op.val = sem, prev + 16
            else:
                ins = op.fn(*op.args, **op.kw)
                if op.signal:
                    self.ecount[op.eng] += 1
                    ins.then_inc(self.esem[op.eng], 1)
                    op.sem, op.val = self.esem[op.eng], self.ecount[op.eng]
            self.n_inst += 1
        self.ops = []
        self.tok = {}
        for e in self.h:
            for c in self.COMPUTE:
                if self.ecount[c] > 0:
                    self._wait(e, self.esem[c], self.ecount[c])
            for q, sems in self.dsems.items():
                for k, sem in enumerate(sems):
                    if self.dlast[q][k] > 0:
                        self._wait(e, sem, self.dlast[q][k])


class Cfg:
    def __init__(self, S, R, split=False):
        self.split = split
        self.P = S // 2 if split else 0
        self.NOWN = S - self.P
        self.S = S
        self.R = R
        self.NTOK = S // R
        self.HM = NHM // R
        self.HS = NHS // R
        self.RF = 4096 // R
        self.CT = 3072 // R
        self.QM0 = 0
        self.KM0 = self.HM * 256
        self.QS0 = 2 * self.HM * 256
        self.KS0 = self.QS0 + self.HS * 128
        self.VM0 = 0
        self.OM0 = self.HM * 256
        self.VS0 = 2 * self.HM * 256
        self.YR = 2048 // R
        assert self.NTOK % T == 0 and (not split or self.P % T == 0)


def build(cfg, debug=False):
    S, R, NTOK, HM, HS = cfg.S, cfg.R, cfg.NTOK, cfg.HM, cfg.HS
    nc = bass.Bass("TRN2", target_bir_lowering=False)
    dt_in = lambda name, shape: nc.dram_tensor(name, list(shape), F32, kind="ExternalInput").ap()
    P, NOWN, SPLIT = cfg.P, cfg.NOWN, cfg.split
    x_d = dt_in("x", [NOWN, D])
    p_d = dt_in("p", [NOWN, DPLE])
    xp_d = dt_in("xpre", [P, D]) if SPLIT else None
    flg_d = dt_in("flg", [128, 2]) if SPLIT else None
    w = {}
    for name, shape in (("ffn1_w1", [D, DFF]), ("ffn1_w3", [D, DFF]), ("ffn1_w2", [DFF, D]),
                        ("w_in", [D, INCOLS]), ("w_up_m", [1024, D]), ("w_up_sb", [1024, D]),
                        ("w_out", [D, D]), ("ffn2_w1", [D, DFF]), ("ffn2_w3", [D, DFF]),
                        ("ffn2_w2", [DFF, D]), ("w_ple_gate", [D, D]), ("w_ple_proj", [DPLE, D])):
        w[name] = dt_in(name, shape)
    lnp = {n: dt_in(n, [1, D]) for n in ("ln1_g", "ln1_b", "ln2_g", "ln2_b", "ln3_g", "ln3_b")}
    bg_d = dt_in("bgates", [1, 2 * HM])
    conv_d = dt_in("convT", [128, 2 * HM * 2 * 4])
    norm_d = dt_in("normm", [1, HM * 256])
    out_d = nc.dram_tensor("out", [NOWN, D], F32, kind="ExternalOutput").ap()

    skind = "ExternalOutput" if debug else "Internal"
    dscr = lambda name, shape, dt: nc.dram_tensor(name, list(shape), dt, kind=skind).ap()
    X1 = dscr("X1", [NOWN, D], F32)
    GA = dscr("GA", [D, NOWN], BF16)
    GB = dscr("GB", [D, NOWN], BF16)
    EXF = dscr("EXF", [4096, NTOK], BF16)
    EXT = dscr("EXT", [R * NTOK, cfg.CT], BF16)
    EXG = dscr("EXG", [8, NTOK], F32)
    EY = dscr("EY", [2048, NOWN], BF16)
    GSCR = dscr("GSCR", [4, S], F32)
    WB = {n_: nc.dram_tensor("wb_" + n_, list(w[n_].shape), BF16, kind="Internal").ap() for n_ in w}
    EXF_o, EXT_o, EXG_o, EY_o = EXF, EXT, EXG, EY

    with ExitStack() as top:
        sch = Sched(nc, top)
        A = sch.add
        ps_t = top.enter_context(nc.psum_tensor("ps", [128, 8, 512], F32))
        PS = [ps_t[:, b, :] for b in range(8)]
        PSB = [ps_t[:, b, :].bitcast(BF16) for b in range(8)]

        def pstok(b):
            return ("ps", b)

        cst = top.enter_context(nc.sbuf_tensor("cst", [128, 4 * 128], F32))
        ident_f = cst[:, 0:128]
        tri_s = cst[:, 128:256]
        maskM = cst[:, 256:384]
        ones_f = cst[:, 384:512]
        cstb = top.enter_context(nc.sbuf_tensor("cstb", [128, 3 * 128], BF16))
        ident_b = cstb[:, 0:128]
        tri_b = cstb[:, 128:256]
        tric_b = cstb[:, 256:384]
        cm64 = top.enter_context(nc.sbuf_tensor("cm64", [64, 64], F32))
        dmask = top.enter_context(nc.sbuf_tensor("dmask", [128, 4, 512], BF16))
        tmpc = top.enter_context(nc.sbuf_tensor("tmpc", [128, 512], F32))
        CT_ = ("cst",)
        g = nc.gpsimd

        def sel(out, in_, pattern, op, fill, base, cm):
            A("pool", g.affine_select, out=out, in_=in_, pattern=pattern, compare_op=op, fill=fill,
              base=base, channel_multiplier=cm, reads=CT_, writes=CT_)

        def cp(out, in_):
            A("pool", g.tensor_copy, out, in_, reads=CT_, writes=CT_)

        A("pool", g.memset, cst[:, :], 1.0, writes=CT_)
        sel(ident_f, ident_f, [[-1, 128]], ALU.is_ge, 0.0, 0, 1)
        sel(ident_f, ident_f, [[1, 128]], ALU.is_ge, 0.0, 0, -1)
        cp(ident_b, ident_f)
        sel(tri_s, tri_s, [[1, 128]], ALU.is_gt, 0.0, 0, -1)
        A("pool", g.memset, maskM, 0.0, reads=CT_, writes=CT_)
        sel(maskM, maskM, [[-1, 128]], ALU.is_gt, NEG, 0, 1)
        A("pool", g.memset, tmpc[:, :], 1.0, reads=CT_, writes=CT_)
        sel(tmpc[:, 0:128], tmpc[:, 0:128], [[-1, 128]], ALU.is_gt, 0.0, 0, 1)
        cp(tri_b, tmpc[:, 0:128])
        sel(tmpc[:, 128:256], tmpc[:, 128:256], [[1, 128]], ALU.is_ge, 0.0, 0, -1)
        cp(tric_b, tmpc[:, 128:256])
        A("pool", g.memset, cm64[:, :], 0.0, reads=CT_, writes=CT_)
        sel(cm64[:, :], cm64[:, :], [[1, 64]], ALU.is_ge, NEG, 0, -1)
        for i in range(4):
            A("pool", g.memset, tmpc[:, :], 1.0, reads=CT_, writes=CT_)
            sel(tmpc[:, :], tmpc[:, :], [[1, 512]], ALU.is_gt, 0.0, -128 * i, -1)
            cp(dmask[:, i, :], tmpc[:, :])
        sch.barrier()

        class Ring:
            def __init__(self, n):
                self.n, self.i = n, 0

            def next(self):
                k = self.i
                self.i = (k + 1) % self.n
                return k

        conv_done = {}
        conv_pending = []

        def conv_plan(names):
            for n_ in names:
                rows = w[n_].shape[0]
                nch = 8 if rows * w[n_].shape[1] > 8e6 else 2
                step = rows // nch
                for ci_ in range(nch):
                    conv_pending.append((n_, ci_ * step, rows if ci_ == nch - 1 else (ci_ + 1) * step, ci_, nch))

        def conv_emit(k=1):
            for _ in range(k):
                if not conv_pending:
                    return
                n_, r0, r1, ci_, nch = conv_pending.pop(0)
                A("pool", nc.gpsimd.dma_start, out=WB[n_][r0:r1, :], in_=w[n_][r0:r1, :],
                  writes=(("wb", n_, ci_),), dma=True)
                if ci_ == nch - 1:
                    conv_done[n_] = tuple(("wb", n_, q_) for q_ in range(nch))

        def wload(wr, slot, wname, r0, nk, c0, pc, k0=0):
            step = 4
            if wname in conv_done:
                wap, rd = WB[wname], conv_done[wname]
            else:
                wap, rd = w[wname], ()
            for ka in range(0, nk, step):
                kb = min(nk, ka + step)
                src = wap[r0 + ka * 128: r0 + kb * 128, c0:c0 + pc].rearrange("(k p) n -> p k n", p=128)
                A("pool", nc.gpsimd.dma_start, out=wr[:, slot, k0 + ka:k0 + kb, 0:pc], in_=src,
                  reads=rd, writes=(("w", slot, (k0 + ka) // step),), dma=True)
            conv_emit(1)

        def to_fm(s_tm, xT, xb, xbr, pbank):
            for gi in range(NG):
                k = xbr.next()
                A("dve", nc.vector.tensor_copy, xb[:, k, :], s_tm[:, gi, :],
                  reads=(("s", gi),), writes=(("xb", k),))
                b0 = pbank[gi % 2]
                for half in range(2):
                    pb = PSB[b0 + half]
                    for q in range(8):
                        kc = half * 8 + q
                        A("pe", nc.tensor.transpose, pb[:, q * 128:(q + 1) * 128], xb[:, k, kc * 128:(kc + 1) * 128],
                          ident_b, reads=(("xb", k), "cst"), writes=(pstok(b0 + half),))
                    A("act", nc.scalar.copy, xT[:, half * 8:(half + 1) * 8, gi * 128:(gi + 1) * 128],
                      pb[:, :].rearrange("p (k t) -> p k t", t=128),
                      reads=(pstok(b0 + half),), writes=(("xT", gi, half),))

        WT = [tuple(("w", sl_, q_) for q_ in range(4)) for sl_ in range(8)]
        XT_ALL = tuple(("xT", gi, h) for gi in range(NG) for h in range(2))

        def ffn(xT, gT, s_tm, wr, ring, w1, w3, w2, tmp, tmpr):
            for j4 in range(NJ // 4):
                s1 = ring.next()
                wload(wr, s1, w1, 0, KC, j4 * 512, 512)
                s3 = ring.next()
                wload(wr, s3, w3, 0, KC, j4 * 512, 512)
                for jj in range(4):
                    j = j4 * 4 + jj
                    b1, b3 = (0, 1) if j % 2 == 0 else (2, 3)
                    for kc in range(KC):
                        A("pe", nc.tensor.matmul, PS[b1], lhsT=wr[:, s1, kc, jj * 128:(jj + 1) * 128], rhs=xT[:, kc, :],
                          start=(kc == 0), stop=(kc == KC - 1), reads=WT[s1] + XT_ALL, writes=(pstok(b1),))
                    for kc in range(KC):
                        A("pe", nc.tensor.matmul, PS[b3], lhsT=wr[:, s3, kc, jj * 128:(jj + 1) * 128], rhs=xT[:, kc, :],
                          start=(kc == 0), stop=(kc == KC - 1), reads=WT[s3] + XT_ALL, writes=(pstok(b3),))
                    k = tmpr.next()
                    A("act", nc.scalar.activation, out=tmp[:, k, 0, :], in_=PS[b1], func=AF.Sigmoid,
                      reads=(pstok(b1),), writes=(("tmp", k, 0),))
                    A("dve", nc.vector.tensor_tensor, tmp[:, k, 1, :], PS[b1], tmp[:, k, 0, :], ALU.mult,
                      reads=(pstok(b1), ("tmp", k, 0)), writes=(("tmp", k, 1),))
                    A("dve", nc.vector.tensor_tensor, gT[:, j, :], PS[b3], tmp[:, k, 1, :], ALU.mult,
                      reads=(pstok(b3), ("tmp", k, 1)), writes=(("gT", j),))
            for slab in range(4):
                for piece in range(4):
                    sl = ring.next()
                    wload(wr, sl, w2, piece * 11 * 128, 11, slab * 512, 512)
                    for gi in range(NG):
                        for jj in range(11):
                            j = piece * 11 + jj
                            A("pe", nc.tensor.matmul, PS[4 + gi], lhsT=gT[:, j, gi * 128:(gi + 1) * 128],
                              rhs=wr[:, sl, jj, :], start=(j == 0), stop=(j == NJ - 1),
                              reads=WT[sl] + (("gT", j),), writes=(pstok(4 + gi),))
                for gi in range(NG):
                    A("dve", nc.vector.scalar_tensor_tensor, out=s_tm[:, gi, slab * 512:(slab + 1) * 512],
                      in0=PS[4 + gi], scalar=CRES, in1=s_tm[:, gi, slab * 512:(slab + 1) * 512],
                      op0=ALU.mult, op1=ALU.add, reads=(pstok(4 + gi), ("s", gi)), writes=(("s", gi),))

        def layernorm(s_tm, lnt, gname, bname, st, eps):
            A("sp", nc.sync.dma_start, out=lnt[:, 0, :], in_=lnp[gname].partition_broadcast(128),
              writes=(("lnt", 0),), dma=True)
            A("sp", nc.sync.dma_start, out=lnt[:, 1, :], in_=lnp[bname].partition_broadcast(128),
              writes=(("lnt", 1),), dma=True)
            for gi in range(NG):
                stt = ("st", gi)
                for q in range(4):
                    A("dve", nc.vector.bn_stats, st[:, gi, q * 6:(q + 1) * 6], s_tm[:, gi, q * 512:(q + 1) * 512],
                      reads=(("s", gi),), writes=(stt,))
                A("dve", nc.vector.bn_aggr, st[:, gi, 24:26], st[:, gi, 0:24], reads=(stt,), writes=(stt,))
                A("pool", nc.gpsimd.tensor_scalar, st[:, gi, 26:27], st[:, gi, 25:26], eps, None, ALU.add,
                  reads=(stt,), writes=(stt,))
                A("pool", nc.gpsimd.tensor_tensor, st[:, gi, 27:28], st[:, gi, 26:27], eps_t[:, 3:4], ALU.pow,
                  reads=(stt, ("eps",)), writes=(stt,))
                A("dve", nc.vector.tensor_scalar, s_tm[:, gi, :], s_tm[:, gi, :], st[:, gi, 24:25], st[:, gi, 27:28],
                  ALU.subtract, ALU.mult, reads=(stt, ("s", gi)), writes=(("s", gi),))
                A("dve", nc.vector.tensor_tensor, s_tm[:, gi, :], s_tm[:, gi, :], lnt[:, 0, :], ALU.mult,
                  reads=(("s", gi), ("lnt", 0)), writes=(("s", gi),))
                A("dve", nc.vector.tensor_tensor, s_tm[:, gi, :], s_tm[:, gi, :], lnt[:, 1, :], ALU.add,
                  reads=(("s", gi), ("lnt", 1)), writes=(("s", gi),))

        eps_t = top.enter_context(nc.sbuf_tensor("eps_t", [128, 4], F32))
        A("pool", g.memset, eps_t[:, 0:1], EPS_LN, writes=(("eps",),))
        A("pool", g.memset, eps_t[:, 1:2], EPS_H, reads=(("eps",),), writes=(("eps",),))
        A("pool", g.memset, eps_t[:, 2:3], 1.0, reads=(("eps",),), writes=(("eps",),))
        A("pool", g.memset, eps_t[:, 3:4], -0.5, reads=(("eps",),), writes=(("eps",),))
        sch.barrier()

        NT = NTOK // T

        with ExitStack() as ph:
            sb = lambda name, shape, dt: ph.enter_context(nc.sbuf_tensor(name, list(shape), dt))
            xT = sb("xT", [128, KC, T], BF16)
            gT = sb("gT", [128, NJ, T], BF16)
            s_tm = sb("s_tm", [128, NG, D], F32)
            NSLOT = 4
            wr = sb("wr", [128, NSLOT, KC, 512], BF16)
            ring = Ring(NSLOT)
            lnt = sb("lnt", [128, 2, D], F32)
            st = sb("st", [128, NG, 32], F32)
            xb = sb("xb", [128, 2, D], BF16)
            xbr = Ring(2)
            tmp = sb("tmp", [128, 2, 2, T], F32)
            tmpr = Ring(2)
            stF = sb("stF", [128, 2, 4, T], BF16)
            stFr = Ring(2)
            stT = stF
            stTr = stFr
            stG = sb("stG", [8, T], F32)
            wg = sb("wg", [128, KC, 8], BF16)
            A("pool", nc.gpsimd.dma_start, out=wg[:, :, :],
              in_=w["w_in"][:, C_GT:C_GT + 8].rearrange("(k p) n -> p k n", p=128), writes=(("wg",),), dma=True)

            if SPLIT:
                flg = sb("flg_sb", [128, 2], F32)
                A("sp", nc.sync.dma_start, out=flg[:, :], in_=flg_d, writes=("flg",), dma=True)
            conv_plan(("ffn1_w1", "ffn1_w3", "ffn1_w2", "w_in"))
            for ti in range(S // T):
                t0 = ti * T
                pre = t0 < P
                o0 = t0 - P
                last_pre = pre and (t0 + T == P)
                xsrc, xo = (xp_d, t0) if pre else (x_d, o0)
                for gi in range(NG):
                    A("sp", nc.sync.dma_start, out=s_tm[:, gi, :], in_=xsrc[xo + gi * 128:xo + (gi + 1) * 128, :],
                      writes=(("s", gi),), dma=True)
                to_fm(s_tm, xT, xb, xbr, (4, 6))
                ffn(xT, gT, s_tm, wr, ring, "ffn1_w1", "ffn1_w3", "ffn1_w2", tmp, tmpr)
                layernorm(s_tm, lnt, "ln1_g", "ln1_b", st, EPS_LN)
                if not pre:
                    for gi in range(NG):
                        A("sp", nc.sync.dma_start, out=X1[o0 + gi * 128:o0 + (gi + 1) * 128, :], in_=s_tm[:, gi, :],
                          reads=(("s", gi),), dma=True)
                to_fm(s_tm, xT, xb, xbr, (4, 6))
                win = "w_in"
                fm_pieces = []
                kq = "flag" if pre else "copy"
                if (not pre) or last_pre:
                    for pi in range(2):
                        fm_pieces.append((C_MQ + pi * 512, kq, ("F", "QM", pi * 512)))
                for pi in range(2):
                    fm_pieces.append((C_MK + pi * 512, kq, ("F", "KM", pi * 512)))
                if not pre:
                    for pi in range(2):
                        fm_pieces.append((C_SQ + pi * 512, "copy", ("F", "QS", pi * 512)))
                for pi in range(2):
                    fm_pieces.append((C_SK + pi * 512, "copy", ("F", "KS", pi * 512)))
                if not pre:
                    for pi in range(4):
                        fm_pieces.append((C_GA + pi * 512, "sig", ("GA", pi * 512)))
                    for pi in range(4):
                        fm_pieces.append((C_GB + pi * 512, "sig", ("GB", pi * 512)))
                bi = 0
                for (c0, kind, dest) in fm_pieces:
                    sl = ring.next()
                    wload(wr, sl, win, 0, KC, c0, 512)
                    k = stFr.next()
                    for cc in range(4):
                        b = bi % 4
                        bi += 1
                        for kc in range(KC):
                            A("pe", nc.tensor.matmul, PS[b], lhsT=wr[:, sl, kc, cc * 128:(cc + 1) * 128], rhs=xT[:, kc, :],
                              start=(kc == 0), stop=(kc == KC - 1), reads=WT[sl] + XT_ALL, writes=(pstok(b),))
                        if kind == "sig":
                            A("act", nc.scalar.activation, out=stF[:, k, cc, :], in_=PS[b], func=AF.Sigmoid,
                              reads=(pstok(b),), writes=(("stF", k, cc),))
                        elif kind == "flag":
                            A("act", nc.scalar.activation, out=stF[:, k, cc, :], in_=PS[b], func=AF.Copy, scale=flg[:, 0:1],
                              reads=(pstok(b), "flg"), writes=(("stF", k, cc),))
                        else:
                            A("act", nc.scalar.copy, stF[:, k, cc, :], PS[b], reads=(pstok(b),), writes=(("stF", k, cc),))
                    rd = tuple(("stF", k, cc) for cc in range(4))
                    if dest[0] == "F":
                        region, off = dest[1], dest[2]
                        if region in ("QM", "KM"):
                            per = HM * 256
                            base = cfg.QM0 if region == "QM" else cfg.KM0
                        else:
                            per = HS * 128
                            base = cfg.QS0 if region == "QS" else cfg.KS0
                        rank, loc = off // per, off % per
                        row0 = rank * cfg.RF + base + loc
                        dst = EXF[row0:row0 + 512, t0:t0 + T].rearrange("(c p) t -> p c t", p=128)
                    else:
                        dstT = GA if dest[0] == "GA" else GB
                        dst = dstT[dest[1]:dest[1] + 512, o0:o0 + T].rearrange("(c p) t -> p c t", p=128)
                    A("sp", nc.sync.dma_start, out=dst, in_=stF[:, k, :, :], reads=rd, dma=True)
                for kc in range(KC):
                    A("pe", nc.tensor.matmul, PS[0][0:8, :], lhsT=wg[:, kc, :], rhs=xT[:, kc, :],
                      start=(kc == 0), stop=(kc == KC - 1), reads=(("wg",),) + XT_ALL, writes=(pstok(0),))
                A("act", nc.scalar.copy, stG[:, :], PS[0][0:8, :], reads=(pstok(0),), writes=(("stG",),))
                if pre:
                    A("dve", nc.vector.tensor_scalar, stG[0:4, :], stG[0:4, :], flg[0:4, 0:1], flg[0:4, 1:2], ALU.mult, ALU.add,
                      reads=(("stG",), "flg"), writes=(("stG",),))
                for gsel in range(2):
                    for rk in range(R):
                        A("sp", nc.sync.dma_start,
                          out=EXG[rk * 2 * HM + gsel * HM: rk * 2 * HM + gsel * HM + HM, t0:t0 + T],
                          in_=stG[gsel * 4 + rk * HM: gsel * 4 + rk * HM + HM, :], reads=(("stG",),), dma=True)
                tm_pieces = []
                for pi in range(2):
                    tm_pieces.append((C_MV + pi * 512, kq, cfg.VM0, HM * 256, pi * 512))
                if not pre:
                    for pi in range(2):
                        tm_pieces.append((C_MO + pi * 512, "sig", cfg.OM0, HM * 256, pi * 512))
                for pi in range(2):
                    tm_pieces.append((C_SV + pi * 512, kq, cfg.VS0, HS * 128, pi * 512))
                for (c0, kind, base, per, off) in tm_pieces:
                    sl = ring.next()
                    wload(wr, sl, win, 0, KC, c0, 512)
                    k = stTr.next()
                    for gi in range(NG):
                        b = 4 + gi
                        for kc in range(KC):
                            A("pe", nc.tensor.matmul, PS[b], lhsT=xT[:, kc, gi * 128:(gi + 1) * 128], rhs=wr[:, sl, kc, :],
                              start=(kc == 0), stop=(kc == KC - 1), reads=WT[sl] + XT_ALL, writes=(pstok(b),))
                        if kind == "sig":
                            A("act", nc.scalar.activation, out=stT[:, k, gi, :], in_=PS[b], func=AF.Sigmoid,
                              reads=(pstok(b),), writes=(("stF", k, gi),))
                        elif kind == "flag":
                            A("act", nc.scalar.activation, out=stT[:, k, gi, :], in_=PS[b], func=AF.Copy, scale=flg[:, 0:1],
                              reads=(pstok(b), "flg"), writes=(("stF", k, gi),))
                        else:
                            A("act", nc.scalar.copy, stT[:, k, gi, :], PS[b], reads=(pstok(b),), writes=(("stF", k, gi),))
                    rank, loc = off // per, off % per
                    dst = EXT[rank * NTOK + t0: rank * NTOK + t0 + T, base + loc: base + loc + 512].rearrange(
                        "(g p) c -> p g c", p=128)
                    A("sp", nc.sync.dma_start, out=dst, in_=stT[:, k, :, :],
                      reads=tuple(("stF", k, gi) for gi in range(NG)), dma=True)
            conv_emit(len(conv_pending))
            sch.barrier()

        conv_plan(("w_up_m", "w_up_sb", "w_out", "ffn2_w1", "ffn2_w3", "ffn2_w2", "w_ple_gate"))
        conv_emit(len(conv_pending))

        phase2(nc, sch, cfg, PS, PSB, dict(ident_f=ident_f, ident_b=ident_b, tri_s=tri_s, maskM=maskM, ones_f=ones_f,
                                            tri_b=tri_b, tric_b=tric_b, cm64=cm64, dmask=dmask, eps_t=eps_t),
               EXF_o, EXT_o, EXG_o, EY, GSCR, bg_d, conv_d, norm_d)

        with ExitStack() as ph:
            sb = lambda name, shape, dt: ph.enter_context(nc.sbuf_tensor(name, list(shape), dt))
            xT = sb("xT3", [128, KC, T], BF16)
            gT = sb("gT3", [128, NJ, T], BF16)
            s_tm = sb("s_tm3", [128, NG, D], F32)
            NSLOT = 4
            wr = sb("wr3", [128, NSLOT, KC, 512], BF16)
            ring = Ring(NSLOT)
            lnt = sb("lnt3", [128, 2, D], F32)
            st = sb("st3", [128, NG, 32], F32)
            xb = sb("xb3", [128, 2, D], BF16)
            xbr = Ring(2)
            tmp = sb("tmp3", [128, 2, 2, T], F32)
            tmpr = Ring(2)
            wple = sb("wple", [128, 2, 2, 512], BF16)
            wpler = Ring(2)
            pbf = sb("pbf", [128, NG, DPLE], BF16)
            pT = sb("pT", [128, 2, T], BF16)
            yT = gT[:, 0:16, :]
            mT = gT[:, 16:32, :]
            for ti in range(NOWN // T):
                t0 = ti * T
                for rk in range(R):
                    r0 = rk * cfg.YR
                    A("sp", nc.sync.dma_start, out=yT[:, rk * HM * 2:(rk + 1) * HM * 2, :],
                      in_=EY_o[r0:r0 + HM * 256, t0:t0 + T].rearrange("(c p) t -> p c t", p=128),
                      writes=tuple(("gT", rk * HM * 2 + c) for c in range(HM * 2)), dma=True)
                    A("sp", nc.sync.dma_start, out=yT[:, 8 + rk * HS:8 + (rk + 1) * HS, :],
                      in_=EY_o[r0 + HM * 256:r0 + HM * 256 + HS * 128, t0:t0 + T].rearrange("(c p) t -> p c t", p=128),
                      writes=tuple(("gT", 8 + rk * HS + c) for c in range(HS)), dma=True)
                for gi in range(NG):
                    A("sp", nc.sync.dma_start, out=s_tm[:, gi, :], in_=X1[t0 + gi * 128:t0 + (gi + 1) * 128, :],
                      writes=(("s", gi),), dma=True)
                for piece in range(4):
                    sl = ring.next()
                    wload(wr, sl, "w_up_m", 0, 8, piece * 512, 512, k0=0)
                    wload(wr, sl, "w_up_sb", 0, 8, piece * 512, 512, k0=8)
                    A("sp", nc.sync.dma_start, out=gT[:, 32:36, :],
                      in_=GA[piece * 512:(piece + 1) * 512, t0:t0 + T].rearrange("(c p) t -> p c t", p=128),
                      writes=tuple(("gT", 32 + c) for c in range(4)), dma=True)
                    A("sp", nc.sync.dma_start, out=gT[:, 36:40, :],
                      in_=GB[piece * 512:(piece + 1) * 512, t0:t0 + T].rearrange("(c p) t -> p c t", p=128),
                      writes=tuple(("gT", 36 + c) for c in range(4)), dma=True)
                    for ff in range(4):
                        f = piece * 4 + ff
                        bA, bB = (0, 1) if f % 2 == 0 else (2, 3)
                        for kc in range(8):
                            A("pe", nc.tensor.matmul, PS[bA], lhsT=wr[:, sl, kc, ff * 128:(ff + 1) * 128], rhs=yT[:, kc, :],
                              start=(kc == 0), stop=(kc == 7), reads=WT[sl] + (("gT", kc),), writes=(pstok(bA),))
                        for kc in range(8, 16):
                            A("pe", nc.tensor.matmul, PS[bB], lhsT=wr[:, sl, kc, ff * 128:(ff + 1) * 128], rhs=yT[:, kc, :],
                              start=(kc == 8), stop=(kc == 15), reads=WT[sl] + (("gT", kc),), writes=(pstok(bB),))
                        k = tmpr.next()
                        A("dve", nc.vector.tensor_tensor, tmp[:, k, 0, :], PS[bA], gT[:, 32 + ff, :], ALU.mult,
                          reads=(pstok(bA), ("gT", 32 + ff)), writes=(("tmp", k, 0),))
                        A("dve", nc.vector.tensor_tensor, tmp[:, k, 1, :], PS[bB], gT[:, 36 + ff, :], ALU.mult,
                          reads=(pstok(bB), ("gT", 36 + ff)), writes=(("tmp", k, 1),))
                        A("dve", nc.vector.tensor_tensor, mT[:, f, :], tmp[:, k, 0, :], tmp[:, k, 1, :], ALU.add,
                          reads=(("tmp", k, 0), ("tmp", k, 1)), writes=(("gT", 16 + f),))
                for slab in range(4):
                    sl = ring.next()
                    wload(wr, sl, "w_out", 0, KC, slab * 512, 512)
                    for gi in range(NG):
                        b = 4 + gi
                        for kc in range(KC):
                            A("pe", nc.tensor.matmul, PS[b], lhsT=mT[:, kc, gi * 128:(gi + 1) * 128], rhs=wr[:, sl, kc, :],
                              start=(kc == 0), stop=(kc == KC - 1), reads=WT[sl] + (("gT", 16 + kc),), writes=(pstok(b),))
                        A("dve", nc.vector.scalar_tensor_tensor, out=s_tm[:, gi, slab * 512:(slab + 1) * 512],
                          in0=PS[b], scalar=INVA, in1=s_tm[:, gi, slab * 512:(slab + 1) * 512],
                          op0=ALU.mult, op1=ALU.add, reads=(pstok(b), ("s", gi)), writes=(("s", gi),))
                layernorm(s_tm, lnt, "ln2_g", "ln2_b", st, EPS_LN)
                to_fm(s_tm, xT, xb, xbr, (4, 6))
                ffn(xT, gT, s_tm, wr, ring, "ffn2_w1", "ffn2_w3", "ffn2_w2", tmp, tmpr)
                layernorm(s_tm, lnt, "ln3_g", "ln3_b", st, EPS_LN)
                to_fm(s_tm, xT, xb, xbr, (4, 6))
                for gi in range(NG):
                    A("pool", nc.gpsimd.dma_start, out=pbf[:, gi, :], in_=p_d[t0 + gi * 128:t0 + (gi + 1) * 128, :],
                      writes=(("pbf", gi),), dma=True)
                    pb = PSB[6 + gi % 2]
                    for kc in range(2):
                        A("pe", nc.tensor.transpose, pb[:, kc * 128:(kc + 1) * 128], pbf[:, gi, kc * 128:(kc + 1) * 128],
                          ident_b, reads=(("pbf", gi), "cst"), writes=(pstok(6 + gi % 2),))
                    A("act", nc.scalar.copy, pT[:, :, gi * 128:(gi + 1) * 128],
                      pb[:, 0:256].rearrange("p (k t) -> p k t", t=128), reads=(pstok(6 + gi % 2),), writes=(("pT", gi),))
                PT_ALL = tuple(("pT", gi) for gi in range(NG))
                cnt = 0
                for slab in range(4):
                    sl = ring.next()
                    wload(wr, sl, "w_ple_gate", 0, KC, slab * 512, 512)
                    kw_ = wpler.next()
                    A("pool", nc.gpsimd.dma_start, out=wple[:, kw_, :, :],
                      in_=w["w_ple_proj"][:, slab * 512:(slab + 1) * 512].rearrange("(k p) n -> p k n", p=128),
                      writes=(("wple", kw_),), dma=True)
                    for gi in range(NG):
                        bG, bP = (0, 1) if cnt % 2 == 0 else (2, 3)
                        cnt += 1
                        for kc in range(KC):
                            A("pe", nc.tensor.matmul, PS[bG], lhsT=xT[:, kc, gi * 128:(gi + 1) * 128], rhs=wr[:, sl, kc, :],
                              start=(kc == 0), stop=(kc == KC - 1), reads=WT[sl] + XT_ALL, writes=(pstok(bG),))
                        for kc in range(2):
                            A("pe", nc.tensor.matmul, PS[bP], lhsT=pT[:, kc, gi * 128:(gi + 1) * 128],
                              rhs=wple[:, kw_, kc, :],
                              start=(kc == 0), stop=(kc == 1), reads=(("wple", kw_),) + PT_ALL, writes=(pstok(bP),))
                        k = tmpr.next()
                        A("act", nc.scalar.activation, out=tmp[:, k, 0, :], in_=PS[bG], func=AF.Sigmoid,
                          reads=(pstok(bG),), writes=(("tmp", k, 0),))
                        A("dve", nc.vector.tensor_tensor, tmp[:, k, 1, :], PS[bP], tmp[:, k, 0, :], ALU.mult,
                          reads=(pstok(bP), ("tmp", k, 0)), writes=(("tmp", k, 1),))
                        A("dve", nc.vector.tensor_tensor, s_tm[:, gi, slab * 512:(slab + 1) * 512],
                          s_tm[:, gi, slab * 512:(slab + 1) * 512], tmp[:, k, 1, :], ALU.add,
                          reads=(("tmp", k, 1), ("s", gi)), writes=(("s", gi),))
                for gi in range(NG):
                    A("sp", nc.sync.dma_start, out=out_d[t0 + gi * 128:t0 + (gi + 1) * 128, :], in_=s_tm[:, gi, :],
                      reads=(("s", gi),), dma=True)
            sch.barrier()
    return nc


def phase2(nc, sch, cfg, PS, PSB, K, EXF_o, EXT_o, EXG_o, EY, GSCR, bg_d, conv_d, norm_d):
    sb_attention(nc, sch, cfg, PS, K, EXF_o, EXT_o, EY)
    sch.barrier()
    mlstm(nc, sch, cfg, PS, PSB, K, EXF_o, EXT_o, EXG_o, EY, GSCR, bg_d, conv_d, norm_d)
    sch.barrier()


def sb_attention(nc, sch, cfg, PS, K, EXF_o, EXT_o, EY):
    A = sch.add
    S, R, NTOK, HM, HS = cfg.S, cfg.R, cfg.NTOK, cfg.HM, cfg.HS
    NB, NQ = S // 128, S // 512
    tri_b, tric_b, dmask = K["tri_b"], K["tric_b"], K["dmask"]
    with ExitStack() as ph:
        sb = lambda name, shape, dt: ph.enter_context(nc.sbuf_tensor(name, list(shape), dt))
        QT = sb("sbQT", [128, 2, S], BF16)
        KT = sb("sbKT", [128, 2, S], BF16)
        V = sb("sbV", [128, 2, NB, 128], BF16)
        e_t = sb("sb_e", [128, 2, 512], F32)
        sp32 = sb("sb_sp32", [128, 2, 512], F32)
        sp16 = sb("sb_sp16", [128, 4, 512], BF16)
        a_t = sb("sb_a", [128, 2, 512], F32)
        b_t = sb("sb_b", [128, 2, 512], F32)
        att = sb("sb_att", [128, 3, 512], BF16)
        yst = sb("sb_y", [128, 2, 512], BF16)
        VB = 16 if NTOK // 128 >= 16 else NTOK // 128
        qk_tok = {}
        v_tok = {}

        def load_head(hl):
            hr = hl % 2
            qt, vt = [], []
            for i in range(R):
                r0 = i * cfg.RF + cfg.QS0 + hl * 128
                A("sp", nc.sync.dma_start, out=QT[:, hr, i * NTOK + cfg.P:(i + 1) * NTOK], in_=EXF_o[r0:r0 + 128, cfg.P:NTOK],
                  writes=(("sQ", hr, i),), dma=True)
                r0 = i * cfg.RF + cfg.KS0 + hl * 128
                A("sp", nc.sync.dma_start, out=KT[:, hr, i * NTOK:(i + 1) * NTOK], in_=EXF_o[r0:r0 + 128, 0:NTOK],
                  writes=(("sK", hr, i),), dma=True)
                qt += [("sQ", hr, i), ("sK", hr, i)]
                for b0 in range(0, NTOK // 128, VB):
                    src = EXT_o[i * NTOK + b0 * 128:i * NTOK + (b0 + VB) * 128,
                                cfg.VS0 + hl * 128:cfg.VS0 + (hl + 1) * 128].rearrange("(b s) d -> s b d", s=128)
                    gb0 = i * (NTOK // 128) + b0
                    A("sp", nc.sync.dma_start, out=V[:, hr, gb0:gb0 + VB, :], in_=src,
                      writes=(("sV", hr, gb0),), dma=True)
                    vt.append(("sV", hr, gb0))
            qk_tok[hl] = tuple(qt)
            v_tok[hl] = tuple(vt)

        ZB = (0, 1, 3)
        steps = []
        for hl in range(HS):
            for Q in range(cfg.P // 512, NQ):
                for kb in range(4 * Q + 3, -1, -1):
                    steps.append((hl, Q, kb))

        NS = len(steps)

        def info(n):
            hl, Q, kb = steps[n]
            return hl, hl % 2, Q, kb, kb - 4 * Q, kb == 4 * Q + 3, kb == 0, hl * NQ + Q

        def op_qk(n):
            hl, hr, Q, kb, i, first, last, sweep = info(n)
            zb = ZB[n % 3]
            A("pe", nc.tensor.matmul, PS[zb], lhsT=KT[:, hr, kb * 128:(kb + 1) * 128], rhs=QT[:, hr, Q * 512:(Q + 1) * 512],
              start=True, stop=True, reads=qk_tok[hl], writes=(("ps", zb),))

        def op_esp(n):
            zb, k2 = ZB[n % 3], n % 2
            A("act", nc.scalar.activation, out=e_t[:, k2, :], in_=PS[zb], func=AF.Exp, scale=SB_SCALE,
              reads=(("ps", zb),), writes=(("e", k2),))
            A("act", nc.scalar.activation, out=sp32[:, k2, :], in_=e_t[:, k2, :], func=AF.Ln, bias=K["eps_t"][:, 2:3], scale=1.0,
              reads=(("e", k2),), writes=(("sp32", k2),))

        def op_cast_a(n):
            hl, hr, Q, kb, i, first, last, sweep = info(n)
            zb, k2, k4 = ZB[n % 3], n % 2, n % 4
            if i >= 0:
                A("dve", nc.vector.tensor_tensor, sp16[:, k4, :], sp32[:, k2, :], dmask[:, i, :], ALU.mult,
                  reads=(("sp32", k2),), writes=(("sp16", k4),))
            else:
                A("dve", nc.vector.tensor_copy, sp16[:, k4, :], sp32[:, k2, :],
                  reads=(("sp32", k2),), writes=(("sp16", k4),))
            A("dve", nc.vector.scalar_tensor_tensor, out=a_t[:, k2, :], in0=PS[zb], scalar=SB_SCALE, in1=sp32[:, k2, :],
              op0=ALU.mult, op1=ALU.subtract, reads=(("ps", zb), ("sp32", k2)), writes=(("a", k2),))

        def op_p(n):
            hl, hr, Q, kb, i, first, last, sweep = info(n)
            k4 = n % 4
            if not first:
                A("pe", nc.tensor.matmul, PS[2], lhsT=tric_b, rhs=sp16[:, (n - 1) % 4, :], start=False, stop=False,
                  skip_group_check=True, reads=(("sp16", (n - 1) % 4),), writes=(("ps", 2),))
            A("pe", nc.tensor.matmul, PS[2], lhsT=tri_b, rhs=sp16[:, k4, :], start=first, stop=True,
              skip_group_check=True, reads=(("sp16", k4),), writes=(("ps", 2),))

        def op_b(n):
            k2 = n % 2
            A("dve", nc.vector.tensor_tensor, b_t[:, k2, :], a_t[:, k2, :], PS[2], ALU.subtract,
              reads=(("a", k2), ("ps", 2)), writes=(("b", k2),))

        def op_att(n):
            k2, k3 = n % 2, n % 3
            A("act", nc.scalar.activation, out=att[:, k3, :], in_=b_t[:, k2, :], func=AF.Exp,
              reads=(("b", k2),), writes=(("att", k3),))

        def op_attmask(n):
            hl, hr, Q, kb, i, first, last, sweep = info(n)
            k3 = n % 3
            if i >= 0:
                A("dve", nc.vector.tensor_tensor, att[:, k3, :], att[:, k3, :], dmask[:, i, :], ALU.mult,
                  reads=(("att", k3),), writes=(("att", k3),))

        def op_av(n):
            hl, hr, Q, kb, i, first, last, sweep = info(n)
            k3 = n % 3
            ob = 4 + sweep % 2
            A("pe", nc.tensor.matmul, PS[ob], lhsT=V[:, hr, kb, :], rhs=att[:, k3, :], start=first, stop=last,
              reads=(("att", k3),) + v_tok[hl], writes=(("ps", ob),))
            if last:
                yk = sweep % 2
                A("act", nc.scalar.copy, yst[:, yk, :], PS[ob], reads=(("ps", ob),), writes=(("yst", yk),))
                t = Q * 512 - cfg.P
                j, tl = 0, t
                row0 = j * cfg.YR + HM * 256 + hl * 128
                A("sp", nc.sync.dma_start, out=EY[row0:row0 + 128, tl:tl + 512], in_=yst[:, yk, :],
                  reads=(("yst", yk),), dma=True)

        loaded = set()
        for n in range(NS + 2):
            if n < NS:
                hl = steps[n][0]
                if hl not in loaded:
                    load_head(hl)
                    loaded.add(hl)
                op_qk(n)
            if 0 <= n - 2 < NS:
                op_att(n - 2)
            if n < NS:
                op_esp(n)
            if 0 <= n - 1 < NS:
                op_p(n - 1)
                op_b(n - 1)
            if 0 <= n - 2 < NS:
                op_attmask(n - 2)
                op_av(n - 2)
            if n < NS:
                op_cast_a(n)


def mlstm(nc, sch, cfg, PS, PSB, K, EXF_o, EXT_o, EXG_o, EY, GSCR, bg_d, conv_d, norm_d):
    A = sch.add
    S, R, NTOK, HM = cfg.S, cfg.R, cfg.NTOK, cfg.HM
    C = S // 64
    Cr = NTOK // 64
    SUB = min(2048, cfg.P) if cfg.split else min(2048, NTOK)
    NSUB = S // SUB
    CS = SUB // 64
    ident_f, ident_b, tri_s, maskM, ones_f, cm64, eps_t = (K["ident_f"], K["ident_b"], K["tri_s"], K["maskM"],
                                                           K["ones_f"], K["cm64"], K["eps_t"])
    with ExitStack() as ph:
        sb = lambda name, shape, dt: ph.enter_context(nc.sbuf_tensor(name, list(shape), dt))
        gt = sb("m_gt", [128, 12, 64], F32)
        IT, FT, SP, NB_, BT_, PMB, AT, IST, EMT, WT_, ONES, TMP = [gt[0:C, i, :] for i in range(12)]
        gsm = sb("m_gsm", [128, 16], F32)
        diagX = sb("m_diagX", [128, 128], F32)
        tmpM = sb("m_tmpM", [128, 128], F32)
        A_bcm = sb("m_Abc", [64, S], F32)
        IS_bc = sb("m_ISbc", [128, S], F32)
        BT = sb("m_BT", [64, 128], F32)
        wT = sb("m_wT", [64, 128], F32)
        emtT = sb("m_emtT", [64, 128], F32)
        dec_bc = sb("m_decbc", [128, 128], F32)
        bgb = sb("m_bgb", [128, 2 * HM], F32)
        convw = sb("m_convw", [128, 2 * HM * 2 * 4], F32)
        normg = sb("m_normg", [64, HM * 256], F32)
        rawp = sb("m_rawp", [128, 2, 4 + SUB], BF16)
        rawr = [0]
        acc = sb("m_acc", [128, SUB], F32)
        sig = sb("m_sig", [128, SUB], F32)
        QTb = sb("m_QTb", [128, 2, SUB], BF16)
        KTb = sb("m_KTb", [128, 2, SUB], BF16)
        QsT = sb("m_QsT", [128, 2, SUB], BF16)
        Vp = sb("m_Vp", [64, CS, 258], BF16)
        OMb = sb("m_OMb", [64, CS, 256], BF16)
        CTf = sb("m_CTf", [128, 2, 257], F32)
        CTb = sb("m_CTb", [128, 2, 258], BF16)
        Dm = sb("m_Dm", [64, 2, 64], F32)
        scT = sb("m_scT", [64, 2, 64], BF16)
        Kw = sb("m_Kw", [64, 2, 256], BF16)
        hraw = sb("m_hraw", [64, 4, 256], F32)
        sm = sb("m_sm", [64, 4, 16], F32)
        yt = sb("m_y", [64, 4, 256], BF16)
        ystg = sb("m_ystg", [128, 2, 2, 512], BF16)

        A("sp", nc.sync.dma_start, out=bgb[:, :], in_=bg_d.partition_broadcast(128), writes=("bgb",), dma=True)
        A("sp", nc.sync.dma_start, out=convw[:, :], in_=conv_d, writes=("convw",), dma=True)
        A("sp", nc.sync.dma_start, out=normg[:, :], in_=norm_d.partition_broadcast(64), writes=("normg",), dma=True)
        A("pool", nc.gpsimd.memset, gt[:, 10, :], 1.0, writes=("ones64",))
        A("pool", nc.gpsimd.memset, Vp[:, :, 256:258], 1.0, writes=("vp1",))

        for hl in range(HM):
            for i in range(R):
                A("sp", nc.sync.dma_start, out=gt[i * Cr:(i + 1) * Cr, 0, :],
                  in_=EXG_o[i * 2 * HM + hl:i * 2 * HM + hl + 1, 0:NTOK].rearrange("o (c t) -> (o c) t", t=64),
                  writes=(("IT", i),), dma=True)
                A("sp", nc.sync.dma_start, out=gt[i * Cr:(i + 1) * Cr, 1, :],
                  in_=EXG_o[i * 2 * HM + HM + hl:i * 2 * HM + HM + hl + 1, 0:NTOK].rearrange("o (c t) -> (o c) t", t=64),
                  writes=(("FT", i),), dma=True)
            ITt = tuple(("IT", i) for i in range(R))
            FTt = tuple(("FT", i) for i in range(R))
            V_ = nc.vector
            A("dve", V_.tensor_scalar, IT, IT, bgb[0:C, hl:hl + 1], None, ALU.add, reads=ITt + ("bgb",), writes=ITt)
            A("dve", V_.tensor_scalar, FT, FT, bgb[0:C, HM + hl:HM + hl + 1], None, ALU.add, reads=FTt + ("bgb",), writes=FTt)
            A("act", nc.scalar.activation, out=SP, in_=FT, func=AF.Exp, scale=-1.0, reads=FTt, writes=("SP",))
            A("act", nc.scalar.activation, out=SP, in_=SP, func=AF.Ln, bias=eps_t[0:C, 2:3], scale=1.0, reads=("SP",), writes=("SP",))
            A("dve", V_.tensor_tensor_scan, NB_, ONES, SP, 0.0, ALU.mult, ALU.add, reads=("SP", "ones64"), writes=("NB",))
            A("dve", V_.tensor_tensor, BT_, IT, NB_, ALU.add, reads=ITt + ("NB",), writes=("B",))
            A("dve", V_.tensor_tensor_scan, PMB, ONES, BT_, NEG, ALU.mult, ALU.max, reads=("B", "ones64"), writes=("PMB",))
            G_ = ("gsm",)
            A("dve", V_.tensor_scalar, gsm[0:C, 0:1], NB_[:, 63:64], -1.0, None, ALU.mult, reads=("NB",), writes=G_)
            A("dve", V_.memset, gsm[0:C, 1:2], 0.0, reads=G_, writes=G_)
            A("pe", nc.tensor.matmul, PS[6][0:C, 0:2], lhsT=tri_s[0:C, 0:C], rhs=gsm[0:C, 0:2], start=True, stop=True,
              reads=G_, writes=(("ps", 6),))
            A("act", nc.scalar.copy, gsm[0:C, 1:2], PS[6][0:C, 0:1], reads=(("ps", 6),) + G_, writes=G_)
            A("dve", V_.tensor_tensor, gsm[0:C, 2:3], PMB[:, 63:64], gsm[0:C, 1:2], ALU.subtract, reads=("PMB",) + G_, writes=G_)
            A("dve", V_.tensor_scalar, diagX[0:C, 0:C], ident_f[0:C, 0:C], gsm[0:C, 2:3], None, ALU.mult, reads=G_, writes=("diagX",))
            A("pe", nc.tensor.matmul, PS[7][0:C, 0:C], lhsT=ones_f[0:C, 0:C], rhs=diagX[0:C, 0:C], start=True, stop=True,
              reads=("diagX",), writes=(("ps", 7),))
            A("dve", V_.tensor_tensor, tmpM[0:C, 0:C], PS[7][0:C, 0:C], maskM[0:C, 0:C], ALU.add, reads=(("ps", 7),), writes=("tmpM",))
            A("dve", V_.tensor_reduce, gsm[0:C, 3:4], tmpM[0:C, 0:C], AX.X, ALU.max, reads=("tmpM",) + G_, writes=G_)
            A("dve", V_.tensor_tensor, gsm[0:C, 4:5], gsm[0:C, 3:4], gsm[0:C, 1:2], ALU.add, reads=G_, writes=G_)
            A("dve", V_.tensor_tensor, gsm[0:C, 5:6], gsm[0:C, 4:5], PMB[:, 63:64], ALU.max, reads=G_ + ("PMB",), writes=G_)
            A("dve", V_.tensor_scalar, gsm[0:C, 6:7], gsm[0:C, 5:6], -1.0, None, ALU.mult, reads=G_, writes=G_)
            A("dve", V_.tensor_tensor, gsm[0:C, 8:9], gsm[0:C, 4:5], gsm[0:C, 5:6], ALU.subtract, reads=G_, writes=G_)
            A("dve", V_.tensor_scalar, AT, PMB, gsm[0:C, 4:5], -1.0, ALU.max, ALU.mult, reads=("PMB",) + G_, writes=("AT",))
            A("act", nc.scalar.activation, out=IST, in_=AT, func=AF.Exp, bias=gsm[0:C, 4:5], scale=1.0, reads=("AT",) + G_, writes=("IST",))
            A("dve", V_.tensor_tensor, TMP, AT, NB_, ALU.add, reads=("AT", "NB"), writes=("TMP",))
            A("act", nc.scalar.activation, out=EMT, in_=TMP, func=AF.Exp, reads=("TMP",), writes=("EMT",))
            A("act", nc.scalar.activation, out=WT_, in_=BT_, func=AF.Exp, bias=gsm[0:C, 6:7], scale=1.0, reads=("B",) + G_, writes=("WT",))
            A("act", nc.scalar.activation, out=gsm[0:C, 7:8], in_=gsm[0:C, 8:9], func=AF.Exp, reads=G_, writes=G_)
            A("sp", nc.sync.dma_start, out=GSCR[0:1, 0:S].rearrange("o (c t) -> (o c) t", t=64), in_=AT,
              reads=("AT",), writes=(("GS", 0),), dma=True)
            A("sp", nc.sync.dma_start, out=GSCR[1:2, 0:S].rearrange("o (c t) -> (o c) t", t=64), in_=IST,
              reads=("IST",), writes=(("GS", 1),), dma=True)
            A("sp", nc.sync.dma_start, out=GSCR[2:3, 0:C].rearrange("o c -> c o"), in_=gsm[0:C, 7:8],
              reads=G_, writes=(("GS", 2),), dma=True)
            A("sp", nc.sync.dma_start, out=A_bcm[:, :], in_=GSCR[0:1, 0:S].partition_broadcast(64),
              reads=(("GS", 0),), writes=("Abc",), dma=True)
            A("sp", nc.sync.dma_start, out=IS_bc[:, :], in_=GSCR[1:2, 0:S].partition_broadcast(128),
              reads=(("GS", 1),), writes=("ISbc",), dma=True)
            A("sp", nc.sync.dma_start, out=dec_bc[:, 0:C], in_=GSCR[2:3, 0:C].partition_broadcast(128),
              reads=(("GS", 2),), writes=("decbc",), dma=True)
            A3 = A_bcm[:, :].rearrange("p (c t) -> p c t", t=64)
            A("dve", V_.tensor_tensor, A3, A3, cm64[:, :].unsqueeze(1).broadcast_to([64, C, 64]), ALU.add,
              reads=("Abc",), writes=("Abc",))
            for (src, dst, nm) in ((BT_, BT, "BT"), (WT_, wT, "wT"), (EMT, emtT, "emtT")):
                A("pe", nc.tensor.transpose, PS[6][0:64, 0:C], src, ident_f[0:C, 0:C],
                  reads=("B", "WT", "EMT"), writes=(("ps", 6),))
                A("act", nc.scalar.copy, dst[:, 0:C], PS[6][0:64, 0:C], reads=(("ps", 6),), writes=(nm,))
            A("pool", nc.gpsimd.memset, CTf[:, :, :], 0.0, writes=(("CTf", 0), ("CTf", 1)))
            A("pool", nc.gpsimd.memset, CTb[:, :, :], 0.0, writes=("CTb",))

            for sbk in range(NSUB):
                t0 = sbk * SUB
                irank, tl = t0 // NTOK, t0 % NTOK
                so = t0 < cfg.P
                for qk, dstT, base, scl, nm in ((0, QTb, cfg.QM0, 1.0, "QTb"), (1, KTb, cfg.KM0, K_SCALE, "KTb")):
                    if so and qk == 0:
                        continue
                    for dc in range(2):
                        rr = rawr[0] % 2
                        rawr[0] += 1
                        row0 = irank * cfg.RF + base + (hl * 2 + dc) * 128
                        A("sp", nc.sync.dma_start, out=rawp[:, rr, 3:3 + SUB], in_=EXF_o[row0:row0 + 128, tl:tl + SUB],
                          writes=(("rawp", rr),), dma=True)
                        if t0 == 0:
                            A("pool", nc.gpsimd.memset, rawp[:, rr, 0:3], 0.0, writes=(("rawh", rr),))
                        elif tl >= 3:
                            A("sp", nc.sync.dma_start, out=rawp[:, rr, 0:3], in_=EXF_o[row0:row0 + 128, tl - 3:tl],
                              writes=(("rawh", rr),), dma=True)
                        else:
                            rowp = (irank - 1) * cfg.RF + base + (hl * 2 + dc) * 128
                            A("sp", nc.sync.dma_start, out=rawp[:, rr, 0:3], in_=EXF_o[rowp:rowp + 128, NTOK - 3:NTOK],
                              writes=(("rawh", rr),), dma=True)
                        wi = ((qk * HM + hl) * 2 + dc) * 4
                        RD = (("rawp", rr), ("rawh", rr), "convw")
                        A("dve", V_.tensor_scalar, acc[:, :], rawp[:, rr, 3:3 + SUB], convw[:, wi + 3:wi + 4], None, ALU.mult,
                          reads=RD, writes=("acc",))
                        for j in range(3):
                            A("dve", V_.scalar_tensor_tensor, out=acc[:, :], in0=rawp[:, rr, j:j + SUB],
                              scalar=convw[:, wi + j:wi + j + 1], in1=acc[:, :], op0=ALU.mult, op1=ALU.add,
                              reads=RD + ("acc",), writes=("acc",))
                        A("act", nc.scalar.activation, out=sig[:, :], in_=acc[:, :], func=AF.Sigmoid, reads=("acc",), writes=("sig",))
                        A("dve", V_.scalar_tensor_tensor, out=dstT[:, dc, :], in0=acc[:, :], scalar=scl, in1=sig[:, :],
                          op0=ALU.mult, op1=ALU.mult, reads=("acc", "sig"), writes=((nm, dc),))
                for dc in range(2):
                    if so:
                        continue
                    A("dve", V_.tensor_tensor, QsT[:, dc, :], QTb[:, dc, :], IS_bc[:, t0:t0 + SUB], ALU.mult,
                      reads=(("QTb", dc), "ISbc"), writes=(("QsT", dc),))
                r0 = irank * NTOK + tl
                A("sp", nc.sync.dma_start, out=Vp[:, :, 0:256],
                  in_=EXT_o[r0:r0 + SUB, cfg.VM0 + hl * 256:cfg.VM0 + (hl + 1) * 256].rearrange("(c s) v -> s c v", s=64),
                  writes=("Vp",), dma=True)
                if not so:
                    A("sp", nc.sync.dma_start, out=OMb[:, :, :],
                      in_=EXT_o[r0:r0 + SUB, cfg.OM0 + hl * 256:cfg.OM0 + (hl + 1) * 256].rearrange("(c s) v -> s c v", s=64),
                      writes=("OMb",), dma=True)
                def ci(cl):
                    c = sbk * CS + cl
                    return c, c % 2, c % 4, slice(cl * 64, (cl + 1) * 64)

                def pe_front(cl):
                    c, r, r4, csl = ci(cl)
                    ps_s = PS[0][0:64, 0:64]
                    for dc in range(2):
                        if so:
                            continue
                        A("pe", nc.tensor.matmul, ps_s, lhsT=KTb[:, dc, csl], rhs=QTb[:, dc, csl], start=(dc == 0), stop=(dc == 1),
                          reads=(("KTb", dc), ("QTb", dc)), writes=(("ps", 0),))
                    for dc in range(2):
                        A("pe", nc.tensor.transpose, PSB[1][0:64, dc * 128:(dc + 1) * 128], KTb[:, dc, csl], ident_b,
                          reads=(("KTb", dc),), writes=(("ps", 1),))

                def act_d(cl):
                    c, r, r4, csl = ci(cl)
                    A("act", nc.scalar.activation, out=Dm[:, r, :], in_=A_bcm[:, c * 64:(c + 1) * 64], func=AF.Exp,
                      bias=BT[:, c:c + 1], scale=1.0, reads=("Abc", "BT"), writes=(("Dm", r),))

                def dve_front(cl):
                    c, r, r4, csl = ci(cl)
                    if not so:
                        A("dve", V_.tensor_tensor, scT[:, r, :], PS[0][0:64, 0:64], Dm[:, r, :], ALU.mult,
                          reads=(("ps", 0), ("Dm", r)), writes=(("scT", r),))
                    A("dve", V_.tensor_scalar, Kw[:, r, :], PSB[1][0:64, 0:256], wT[:, c:c + 1], None, ALU.mult,
                      reads=(("ps", 1), "wT"), writes=(("Kw", r),))

                def pe_core(cl):
                    c, r, r4, csl = ci(cl)
                    ub = 4 + 2 * r
                    for dc in range(2):
                        A("pe", nc.tensor.matmul, PS[ub + dc][:, 0:257], lhsT=Kw[:, r, dc * 128:(dc + 1) * 128], rhs=Vp[:, cl, 0:257],
                          start=True, stop=True, reads=(("Kw", r), "Vp", "vp1"), writes=(("ps", ub + dc),))
                    ps_n = PS[2][0:64, 0:257]
                    if so:
                        return
                    A("pe", nc.tensor.matmul, ps_n, lhsT=scT[:, r, :], rhs=Vp[:, cl, 0:257], start=True, stop=False,
                      reads=(("scT", r), "Vp", "vp1"), writes=(("ps", 2),))
                    for dc in range(2):
                        A("pe", nc.tensor.matmul, ps_n, lhsT=QsT[:, dc, csl], rhs=CTb[:, dc, 0:257], start=False, stop=(dc == 1),
                          reads=(("QsT", dc), "CTb"), writes=(("ps", 2),))

                def dve_core(cl):
                    c, r, r4, csl = ci(cl)
                    ub = 4 + 2 * r
                    ps_n = PS[2][0:64, 0:257]
                    SMT = (("sm", r4),)
                    if so:
                        for dc in range(2):
                            A("dve", V_.scalar_tensor_tensor, out=CTf[:, dc, :], in0=CTf[:, dc, :], scalar=dec_bc[:, c:c + 1],
                              in1=PS[ub + dc][:, 0:257], op0=ALU.mult, op1=ALU.add,
                              reads=(("CTf", dc), "decbc", ("ps", ub + dc)), writes=(("CTf", dc),))
                        return
                    A("dve", V_.tensor_scalar, sm[:, r4, 6:7], ps_n[:, 256:257], -1.0, emtT[:, c:c + 1], ALU.mult, ALU.max,
                      reads=(("ps", 2), "emtT"), writes=SMT)
                    A("dve", V_.scalar_tensor_tensor, out=CTf[:, 0, :], in0=CTf[:, 0, :], scalar=dec_bc[:, c:c + 1],
                      in1=PS[ub][:, 0:257], op0=ALU.mult, op1=ALU.add,
                      reads=(("CTf", 0), "decbc", ("ps", ub)), writes=(("CTf", 0),))
                    A("dve", V_.tensor_tensor, sm[:, r4, 0:1], sm[:, r4, 6:7], ps_n[:, 256:257], ALU.max,
                      reads=(("ps", 2),) + SMT, writes=SMT)
                    A("dve", V_.scalar_tensor_tensor, out=CTf[:, 1, :], in0=CTf[:, 1, :], scalar=dec_bc[:, c:c + 1],
                      in1=PS[ub + 1][:, 0:257], op0=ALU.mult, op1=ALU.add,
                      reads=(("CTf", 1), "decbc", ("ps", ub + 1)), writes=(("CTf", 1),))
                    A("dve", V_.reciprocal, sm[:, r4, 1:2], sm[:, r4, 0:1], reads=SMT, writes=SMT)

                def act_core(cl):
                    A("act", nc.scalar.copy, CTb[:, :, 0:257], CTf[:, :, :], reads=(("CTf", 0), ("CTf", 1)), writes=("CTb",))

                def dve_h(cl):
                    c, r, r4, csl = ci(cl)
                    SMT = (("sm", r4),)
                    A("dve", V_.tensor_scalar, hraw[:, r4, :], PS[2][0:64, 0:256], sm[:, r4, 1:2], None, ALU.mult,
                      reads=(("ps", 2),) + SMT, writes=(("hraw", r4),))

                def dve_bn(cl):
                    c, r, r4, csl = ci(cl)
                    SMT = (("sm", r4),)
                    A("dve", V_.bn_stats, sm[:, r4, 8:14], hraw[:, r4, :], reads=(("hraw", r4),), writes=SMT)
                    A("dve", V_.bn_aggr, sm[:, r4, 2:4], sm[:, r4, 8:14], reads=SMT, writes=SMT)

                def act_sqrt(cl):
                    c, r, r4, csl = ci(cl)
                    SMT = (("sm", r4),)
                    A("pool", nc.gpsimd.tensor_scalar, sm[:, r4, 4:5], sm[:, r4, 3:4], EPS_H, None, ALU.add,
                      reads=SMT, writes=SMT)
                    A("pool", nc.gpsimd.tensor_tensor, sm[:, r4, 5:6], sm[:, r4, 4:5], eps_t[0:64, 3:4], ALU.pow,
                      reads=SMT, writes=SMT)

                def dve_norm(cl):
                    c, r, r4, csl = ci(cl)
                    SMT = (("sm", r4),)
                    A("dve", V_.tensor_scalar, hraw[:, r4, :], hraw[:, r4, :], sm[:, r4, 2:3], sm[:, r4, 5:6], ALU.subtract, ALU.mult,
                      reads=(("hraw", r4),) + SMT, writes=(("hraw", r4),))
                    A("pool", nc.gpsimd.tensor_tensor, hraw[:, r4, :], hraw[:, r4, :], normg[:, hl * 256:(hl + 1) * 256], ALU.mult,
                      reads=(("hraw", r4), "normg"), writes=(("hraw", r4),))
                    A("pool", nc.gpsimd.tensor_tensor, yt[:, r4, :], hraw[:, r4, :], OMb[:, cl, :], ALU.mult,
                      reads=(("hraw", r4), "OMb"), writes=(("yt", r4),))

                def tail(cl):
                    c, r, r4, csl = ci(cl)
                    yb = 3
                    for dc in range(2):
                        A("pe", nc.tensor.transpose, PSB[yb][:, dc * 64:(dc + 1) * 64], yt[:, r4, dc * 128:(dc + 1) * 128],
                          ident_b[0:64, 0:64], reads=(("yt", r4),), writes=(("ps", yb),))
                    yr = (c // 8) % 2
                    A("act", nc.scalar.copy, ystg[:, yr, :, (c % 8) * 64:(c % 8 + 1) * 64],
                      PSB[yb][:, 0:128].rearrange("p (d t) -> p d t", t=64), reads=(("ps", yb),), writes=(("ystg", yr, c % 8),))
                    if c % 8 == 7:
                        tt = (c - 7) * 64 - cfg.P
                        j, tlo = 0, tt
                        row0 = j * cfg.YR + hl * 256
                        A("sp", nc.sync.dma_start, out=EY[row0:row0 + 256, tlo:tlo + 512].rearrange("(d p) t -> p d t", p=128),
                          in_=ystg[:, yr, :, :], reads=tuple(("ystg", yr, q) for q in range(8)), dma=True)

                ok = lambda k: 0 <= k < CS
                for it in range(CS + 5):
                    if ok(it):
                        pe_front(it)
                        if not so:
                            act_d(it)
                    if ok(it - 1):
                        pe_core(it - 1)
                    if ok(it):
                        dve_front(it)
                    if ok(it - 1):
                        dve_core(it - 1)
                        act_core(it - 1)
                        if not so:
                            dve_h(it - 1)
                    if so:
                        continue
                    if ok(it - 2):
                        dve_bn(it - 2)
                        act_sqrt(it - 2)
                    if ok(it - 3):
                        dve_norm(it - 3)
                    if ok(it - 4):
                        tail(it - 4)


def make_in_maps(inputs, cfg, core_assign):
    maps = []
    R, NTOK, HM = cfg.R, cfg.NTOK, cfg.HM
    wnames = ("ffn1_w1", "ffn1_w3", "ffn1_w2", "w_in", "w_up_m", "w_up_sb", "w_out", "ffn2_w1", "ffn2_w3",
              "ffn2_w2", "w_ple_gate", "w_ple_proj")
    shared = {n: np.ascontiguousarray(np.asarray(inputs[n])[0], dtype=np.float32) for n in wnames}
    for n in ("ln1_g", "ln1_b", "ln2_g", "ln2_b", "ln3_g", "ln3_b"):
        shared[n] = np.ascontiguousarray(np.asarray(inputs[n])[0].reshape(1, D), dtype=np.float32)
    bgm = np.asarray(inputs["b_gates_m"])[0]
    convm = np.asarray(inputs["conv_m"])[0]
    normm = np.asarray(inputs["norm_m"])[0]
    x = np.asarray(inputs["x"])
    p = np.asarray(inputs["p"])[0]
    for (b, rk) in core_assign:
        m = dict(shared)
        if cfg.split:
            P, NOWN = cfg.P, cfg.NOWN
            m["x"] = np.ascontiguousarray(x[b, rk * NOWN:(rk + 1) * NOWN], dtype=np.float32)
            m["p"] = np.ascontiguousarray(p[b, rk * NOWN:(rk + 1) * NOWN], dtype=np.float32)
            m["xpre"] = np.ascontiguousarray(x[b, 0:P], dtype=np.float32)
            fl = np.zeros((128, 2), np.float32)
            fl[:, 0] = 1.0 if rk == 1 else 0.0
            fl[:, 1] = 0.0 if rk == 1 else NEG
            m["flg"] = fl
            rk = 0
        else:
            m["x"] = np.ascontiguousarray(x[b, rk * NTOK:(rk + 1) * NTOK], dtype=np.float32)
            m["p"] = np.ascontiguousarray(p[b, rk * NTOK:(rk + 1) * NTOK], dtype=np.float32)
        h0 = rk * HM
        m["bgates"] = np.ascontiguousarray(
            np.concatenate([bgm[h0:h0 + HM], bgm[NHM + h0:NHM + h0 + HM]]).reshape(1, 2 * HM), dtype=np.float32)
        cv = convm.reshape(4, 2, NHM, 2, 128)[:, :, h0:h0 + HM]
        m["convT"] = np.ascontiguousarray(cv.transpose(4, 1, 2, 3, 0).reshape(128, 2 * HM * 2 * 4), dtype=np.float32)
        m["normm"] = np.ascontiguousarray(normm[h0 * 256:(h0 + HM) * 256].reshape(1, HM * 256), dtype=np.float32)
        maps.append(m)
    return maps


S_FULL = 8192
_NC_CACHE = {}


def kernel(**inputs):
    cfg = Cfg(S_FULL, 1, split=True)
    if "nc" not in _NC_CACHE:
        _NC_CACHE["nc"] = build(cfg)
    nc = _NC_CACHE["nc"]
    B = np.asarray(inputs["x"]).shape[0]
    assign = [(c // 2, c % 2) for c in range(8)]
    maps = make_in_maps(inputs, cfg, assign)
    res = run_bass_kernel_spmd(nc, maps, core_ids=list(range(8)))
    out = np.empty((B, S_FULL, D), np.float32)
    for c, (b, h) in enumerate(assign):
        out[b, h * cfg.NOWN:(h + 1) * cfg.NOWN] = res.results[c]["out"]
    return out
```
